# Optimizing a Trainium2 kernel written in Bass

```python
import jax, jax.numpy as jnp
from jax import lax
import numpy as np

D_MODEL = 1024
BATCH = 2
SEQ = 16384
DEPTH = 1
DEC_BATCH = 16
DEC_SEQ = 32
PAST_LEN = 2048

CHUNK = 64
GLA_HEADS = 4
GLA_DK = 64
GLA_DV = 128
GLA_RANK = 16
GLA_TAU = 16.0
GLA_QK = GLA_HEADS * GLA_DK
GLA_V = GLA_HEADS * GLA_DV
FOX_HEADS = 8
FOX_DH = 64
FOX_W = FOX_HEADS * FOX_DH
Q_BLOCK = 128
D_FF = ((8 * D_MODEL) // 3 + 255) // 256 * 256
DN_ALPHA = (2.0 * DEPTH) ** 0.25
DN_BETA = (8.0 * DEPTH) ** -0.25
LN_EPS = 1e-5
RMS_EPS = 1e-5
SPLIT_SIZES = (GLA_QK, GLA_QK, GLA_V, GLA_RANK, GLA_V, FOX_W, FOX_W, FOX_W, FOX_HEADS, D_MODEL, D_MODEL)
D_IN = sum(SPLIT_SIZES)
VALUE_SPLITS = (2, 7)

kernel_name = "gla_fox_deepnorm_streaming_encoder_step"


def _split_points():
    return [int(i) for i in np.cumsum(SPLIT_SIZES)[:-1]]


def layer_norm(x, g, b):
    xf = x.astype(jnp.float32)
    mu = jnp.mean(xf, axis=-1, keepdims=True)
    var = jnp.mean(jnp.square(xf - mu), axis=-1, keepdims=True)
    return ((xf - mu) * lax.rsqrt(var + LN_EPS)).astype(x.dtype) * g + b


def gla_block(S, xs):
    q, k, v, la = xs
    L = q.shape[1]
    b = jnp.cumsum(la, axis=1)
    causal = jnp.tril(jnp.ones((L, L), dtype=bool))
    rel = b[:, :, None] - b[:, None, :]
    decay = jnp.exp(jnp.where(causal[None, :, :, None, None], rel, -jnp.inf))
    A = jnp.einsum('bthd,bshd,btshd->bhts', q.astype(jnp.float32), k.astype(jnp.float32), decay)
    intra = jnp.einsum('bhts,bshv->bthv', A, v.astype(jnp.float32))
    inter = jnp.einsum('bthd,bhdv->bthv', q * jnp.exp(b), S)
    b_last = b[:, -1]
    k_dec = k * jnp.exp(b_last[:, None] - b)
    S_new = jnp.exp(b_last)[..., None] * S + jnp.einsum('bshd,bshv->bhdv', k_dec, v.astype(jnp.float32))
    return S_new, inter + intra


def gla_scan(S0, q, k, v, la):
    B, L = q.shape[0], q.shape[1]
    c = min(CHUNK, L)
    nc = L // c

    def to_blocks(t):
        return jnp.moveaxis(t.reshape((B, nc, c) + t.shape[2:]), 1, 0)

    S_final, o = lax.scan(gla_block, S0.astype(jnp.float32),
                          (to_blocks(q), to_blocks(k), to_blocks(v), to_blocks(la)))
    o = jnp.moveaxis(o, 0, 1).reshape(B, L, GLA_HEADS, GLA_DV)
    return S_final, o


def fox_attention(q, k, v, c_q, c_k, q_pos, k_pos):
    B, Lq, H, Dh = q.shape
    bs = min(Q_BLOCK, Lq)
    nb = Lq // bs
    scale = Dh ** -0.5
    qb = jnp.moveaxis(q.reshape(B, nb, bs, H, Dh), 1, 0)
    cb = jnp.moveaxis(c_q.reshape(B, nb, bs, H), 1, 0)
    pb = q_pos.reshape(nb, bs)
    ck_h = jnp.swapaxes(c_k, 1, 2)

    def one_block(args):
        qi, ci, pi = args
        s = jnp.einsum('bqhd,bkhd->bhqk', qi, k).astype(jnp.float32) * scale
        s = s + jnp.swapaxes(ci, 1, 2)[..., None] - ck_h[:, :, None, :]
        s = jnp.where((pi[:, None] >= k_pos[None, :])[None, None], s, -jnp.inf)
        p = jax.nn.softmax(s, axis=-1)
        return jnp.einsum('bhqk,bkhd->bqhd', p.astype(v.dtype), v)

    o = lax.map(one_block, (qb, cb, pb))
    return jnp.moveaxis(o, 0, 1).reshape(B, Lq, H, Dh)


def hybrid_layer(x, gla_s0, fox_k_past, fox_v_past, fox_lf_past,
                 w_in, b_in, w_alpha2, b_alpha2, gla_norm_g, w_proj_gla, w_proj_fox, w_out,
                 ln1_g, ln1_b, w_ffn_gate, w_ffn_up, w_ffn_down, ln2_g, ln2_b):
    B, L, _ = x.shape
    h = x @ w_in + b_in
    gq, gk, gv, ga, gr, fq, fk, fv, ff, zg, zf = jnp.split(h, _split_points(), axis=-1)

    gq = gq.reshape(B, L, GLA_HEADS, GLA_DK) * (GLA_DK ** -0.5)
    gk = gk.reshape(B, L, GLA_HEADS, GLA_DK)
    gv = gv.reshape(B, L, GLA_HEADS, GLA_DV)
    la = jax.nn.log_sigmoid((ga @ w_alpha2 + b_alpha2).astype(jnp.float32)) / GLA_TAU
    la = la.reshape(B, L, GLA_HEADS, GLA_DK)
    gla_state, o = gla_scan(gla_s0, gq, gk, gv, la)
    o = o * lax.rsqrt(jnp.mean(jnp.square(o), axis=-1, keepdims=True) + RMS_EPS)
    o = o.reshape(B, L, GLA_V) * gla_norm_g * jax.nn.silu(gr.astype(jnp.float32))
    y_gla = o.astype(x.dtype) @ w_proj_gla

    fq = fq.reshape(B, L, FOX_HEADS, FOX_DH)
    fk = fk.reshape(B, L, FOX_HEADS, FOX_DH)
    fv = fv.reshape(B, L, FOX_HEADS, FOX_DH)
    lf = jax.nn.log_sigmoid(ff.astype(jnp.float32))
    if fox_k_past is None:
        k_all, v_all, lf_all = fk, fv, lf
    else:
        k_all = jnp.concatenate([fox_k_past.astype(fk.dtype), fk], axis=1)
        v_all = jnp.concatenate([fox_v_past.astype(fv.dtype), fv], axis=1)
        lf_all = jnp.concatenate([fox_lf_past.astype(jnp.float32), lf], axis=1)
    Lk = k_all.shape[1]
    past = Lk - L
    c_all = jnp.cumsum(lf_all, axis=1)
    o_f = fox_attention(fq, k_all, v_all, c_all[:, past:], c_all,
                        past + jnp.arange(L), jnp.arange(Lk))
    y_fox = o_f.reshape(B, L, FOX_W) @ w_proj_fox

    m = jax.nn.sigmoid(zg) * y_gla + jax.nn.sigmoid(zf) * y_fox
    x1 = layer_norm(DN_ALPHA * x + m @ w_out, ln1_g, ln1_b)

    f = (jax.nn.silu(x1 @ w_ffn_gate) * (x1 @ w_ffn_up)) @ w_ffn_down
    x2 = layer_norm(DN_ALPHA * x1 + f, ln2_g, ln2_b)
    return x2, gla_state, fk, fv, lf


def setup_inputs(seed: int = 0) -> dict:
    key = jax.random.key(seed)
    ks = jax.random.split(key, 24)
    nrm = jax.random.normal
    f32 = jnp.float32
    col_scale = np.concatenate([np.full((n,), DN_BETA if i in VALUE_SPLITS else 1.0, np.float32)
                                for i, n in enumerate(SPLIT_SIZES)])
    return {
        "x_prompt": nrm(ks[0], (BATCH, SEQ, D_MODEL), f32),
        "x_sample": nrm(ks[1], (DEC_BATCH, DEC_SEQ, D_MODEL), f32),
        "state_gla": 2.0 * nrm(ks[2], (DEPTH, DEC_BATCH, GLA_HEADS, GLA_DK, GLA_DV), f32),
        "cache_fox_k": nrm(ks[3], (DEPTH, DEC_BATCH, PAST_LEN, FOX_HEADS, FOX_DH), f32),
        "cache_fox_v": DN_BETA * nrm(ks[4], (DEPTH, DEC_BATCH, PAST_LEN, FOX_HEADS, FOX_DH), f32),
        "cache_fox_logf": jax.nn.log_sigmoid(nrm(ks[5], (DEPTH, DEC_BATCH, PAST_LEN, FOX_HEADS), f32)),
        "w_in": nrm(ks[6], (DEPTH, D_MODEL, D_IN), f32) * (D_MODEL ** -0.5) * jnp.asarray(col_scale),
        "b_in": 0.02 * nrm(ks[7], (DEPTH, D_IN), f32),
        "w_alpha2": nrm(ks[8], (DEPTH, GLA_RANK, GLA_QK), f32) * (GLA_RANK ** -0.5),
        "b_alpha2": 0.02 * nrm(ks[9], (DEPTH, GLA_QK), f32),
        "gla_norm_g": 1.0 + 0.02 * nrm(ks[10], (DEPTH, GLA_V), f32),
        "w_proj_gla": nrm(ks[11], (DEPTH, GLA_V, D_MODEL), f32) * (GLA_V ** -0.5),
        "w_proj_fox": nrm(ks[12], (DEPTH, FOX_W, D_MODEL), f32) * (FOX_W ** -0.5),
        "w_out": nrm(ks[13], (DEPTH, D_MODEL, D_MODEL), f32) * (D_MODEL ** -0.5) * DN_BETA,
        "ln1_g": 1.0 + 0.02 * nrm(ks[14], (DEPTH, D_MODEL), f32),
        "ln1_b": 0.02 * nrm(ks[15], (DEPTH, D_MODEL), f32),
        "w_ffn_gate": nrm(ks[16], (DEPTH, D_MODEL, D_FF), f32) * (D_MODEL ** -0.5) * DN_BETA,
        "w_ffn_up": nrm(ks[17], (DEPTH, D_MODEL, D_FF), f32) * (D_MODEL ** -0.5) * DN_BETA,
        "w_ffn_down": nrm(ks[18], (DEPTH, D_FF, D_MODEL), f32) * (D_FF ** -0.5) * DN_BETA,
        "ln2_g": 1.0 + 0.02 * nrm(ks[19], (DEPTH, D_MODEL), f32),
        "ln2_b": 0.02 * nrm(ks[20], (DEPTH, D_MODEL), f32),
    }


def reference(x_prompt, x_sample, state_gla, cache_fox_k, cache_fox_v, cache_fox_logf,
              w_in, b_in, w_alpha2, b_alpha2, gla_norm_g, w_proj_gla, w_proj_fox, w_out,
              ln1_g, ln1_b, w_ffn_gate, w_ffn_up, w_ffn_down, ln2_g, ln2_b):
    yp, ys = x_prompt, x_sample
    p_gla, p_k, p_v, p_lf = [], [], [], []
    s_gla, s_k, s_v, s_lf = [], [], [], []
    for l in range(DEPTH):
        w = (w_in[l], b_in[l], w_alpha2[l], b_alpha2[l], gla_norm_g[l], w_proj_gla[l], w_proj_fox[l],
             w_out[l], ln1_g[l], ln1_b[l], w_ffn_gate[l], w_ffn_up[l], w_ffn_down[l], ln2_g[l], ln2_b[l])
        s0 = jnp.zeros((yp.shape[0], GLA_HEADS, GLA_DK, GLA_DV), jnp.float32)
        yp, g1, k1, v1, lf1 = hybrid_layer(yp, s0, None, None, None, *w)
        p_gla.append(g1); p_k.append(k1); p_v.append(v1); p_lf.append(lf1)
        ys, g2, k2, v2, lf2 = hybrid_layer(ys, state_gla[l], cache_fox_k[l], cache_fox_v[l],
                                           cache_fox_logf[l], *w)
        s_gla.append(g2); s_k.append(k2); s_v.append(v2); s_lf.append(lf2)
    return (yp, ys,
            jnp.stack(p_gla), jnp.stack(p_k), jnp.stack(p_v), jnp.stack(p_lf),
            jnp.stack(s_gla), jnp.stack(s_k), jnp.stack(s_v), jnp.stack(s_lf))
```

```python
import numpy as np
import concourse.bass as bass
import concourse.mybir as mybir
from concourse.bass_utils import run_bass_kernel_spmd

F32 = mybir.dt.float32
BF16 = mybir.dt.bfloat16
ALU = mybir.AluOpType
AF = mybir.ActivationFunctionType

D = 1024
NSEQ = 16384
DFF = 2816
NFC = 22
C_GQ, C_GK, C_GV, C_GA, C_GR, C_FQ, C_FK, C_FV, C_FF, C_ZG, C_ZF = 0, 256, 512, 1024, 1040, 1552, 2064, 2576, 3088, 3096, 4120
DIN = 5144
ALPHA = float(2.0 ** 0.25)
NOWN = 4096 + 256
NTO = NOWN // 128
SLOTS = [(s * 512, 512) for s in range(8)] + [(4096, 128), (4224, 128)]
NPRE = [12, 28, 44, 60, 76, 92, 108, 124]
NEG = -60000.0
NCST = 900


def own_chunks(c):
    r = []
    for g in range(4):
        r += [8 * g + c, 8 * g + 7 - c]
    return r


class T:
    __slots__ = ("ap", "w", "r")

    def __init__(self, ap):
        self.ap = ap
        self.w = None
        self.r = {}

    def __getitem__(self, k):
        return self.ap[k]


class Eng:
    def __init__(self, K, e, name, is_pe=False):
        self.e = e
        self.name = name
        self.sem = K.nc.alloc_semaphore("sem_" + name)
        self.cnt = 0
        self.waited = {}
        self.is_pe = is_pe
        self.dsems = []
        self.dcnt = []
        self.dnext = 0

    def wait(self, tok):
        sem, val = tok
        if self.waited.get(sem, 0) < val:
            self.e.wait_ge(sem, val)
            self.waited[sem] = val


class Kern:
    def __init__(self, nc, n_dsem=12):
        self.nc = nc
        self.pe = Eng(self, nc.tensor, "pe", True)
        self.act = Eng(self, nc.scalar, "act")
        self.dve = Eng(self, nc.vector, "dve")
        self.pool = Eng(self, nc.gpsimd, "pool")
        self.sp = Eng(self, nc.sync, "sp")
        self.engs = [self.pe, self.act, self.dve, self.pool, self.sp]
        for q in (self.sp, self.pool, self.act):
            for i in range(n_dsem):
                q.dsems.append(nc.alloc_semaphore("dsem_%s_%d" % (q.name, i)))
                q.dcnt.append(0)

    def _deps(self, E, reads, writes):
        for t in reads:
            if t.w is not None and not (t.w[0] is E.sem and E.is_pe):
                E.wait(t.w)
        for t in writes:
            if t.w is not None and not (t.w[0] is E.sem and E.is_pe):
                E.wait(t.w)
            for sem, val in t.r.items():
                if sem is not E.sem:
                    E.wait((sem, val))

    def _done(self, tok, reads, writes):
        for t in reads:
            if t.r.get(tok[0], 0) < tok[1]:
                t.r[tok[0]] = tok[1]
        for t in writes:
            t.w = tok
            t.r = {}

    def op(self, E, fn, reads=(), writes=()):
        self._deps(E, reads, writes)
        ins = fn()
        E.cnt += 1
        ins.then_inc(E.sem, 1)
        self._done((E.sem, E.cnt), reads, writes)

    def dma(self, Q, out_ap, in_ap, reads=(), writes=()):
        self._deps(Q, reads, writes)
        i = Q.dnext
        Q.dnext = (Q.dnext + 1) % len(Q.dsems)
        sem = Q.dsems[i]
        if Q.dcnt[i] > 0:
            Q.wait((sem, Q.dcnt[i]))
        ins = Q.e.dma_start(out=out_ap, in_=in_ap)
        Q.dcnt[i] += 16
        ins.then_inc(sem, 16)
        self._done((sem, Q.dcnt[i]), reads, writes)

    def mm(self, out_t, out_ap, terms, start=True, stop=True, reads=()):
        E = self.pe
        self._deps(E, reads, [out_t])
        n = len(terms)
        ins = None
        for i, (l, r) in enumerate(terms):
            ins = self.nc.tensor.matmul(out_ap, l, r, start=(start and i == 0), stop=(stop and i == n - 1))
        E.cnt += 1
        ins.then_inc(E.sem, 1)
        self._done((E.sem, E.cnt), reads, [out_t])

    def _alltoks(self):
        toks = []
        for E in self.engs:
            if E.cnt > 0:
                toks.append((E.sem, E.cnt))
            for s, c in zip(E.dsems, E.dcnt):
                if c > 0:
                    toks.append((s, c))
        return toks

    def barrier(self):
        toks = self._alltoks()
        for E in self.engs:
            for tok in toks:
                if tok[0] is not E.sem:
                    E.wait(tok)

    def finish(self):
        for tok in self._alltoks():
            if tok[0] is not self.sp.sem:
                self.sp.wait(tok)


class Ring:
    def __init__(self, tiles):
        self.t = tiles
        self.i = 0

    def next(self):
        t = self.t[self.i]
        self.i = (self.i + 1) % len(self.t)
        return t


def build():
    nc = bass.Bass("TRN2", target_bir_lowering=False)
    K = Kern(nc)
    V_ = nc.vector
    A_ = nc.scalar
    G_ = nc.gpsimd

    def din(name, shape):
        return nc.dram_tensor(name, shape, F32, kind="ExternalInput").ap()

    def dout(name, shape):
        return nc.dram_tensor(name, shape, F32, kind="ExternalOutput").ap()

    def dscr(name, shape, dt=BF16):
        return nc.dram_tensor(name, shape, dt, kind="Internal").ap()

    xT_seq = din("xT_seq", [D, NSEQ])
    xT_own = din("xT_own", [D, NOWN])
    x_own = din("x_own", [NOWN, D])
    valid_d = din("valid", [128, NTO])
    w_in = din("w_in", [D, DIN])
    bias_d = din("bias_bc", [128, DIN])
    bcols_d = din("bcols", [128, 48])
    wa2_d = din("wa2", [16, 256])
    ba2_d = din("ba2_bc", [128, 256])
    wpg_d = din("wpg", [512, D])
    wpf_d = din("wpf", [512, D])
    wout_d = din("wout", [D, D])
    wg_d = din("wg", [D, DFF])
    wu_d = din("wu", [D, DFF])
    wd_d = din("wd", [DFF, D])
    ln_d = din("ln_bc", [128, 4 * D])
    kflag_d = din("kflag", [128, 8 * 128])
    oneh_d = din("onehot", [128, 8 * 32])
    cst_d = din("cst", [128, NCST])
    kTc_d = din("kTc", [2, 512, 2048])
    vc_d = din("vc", [2, 2048, 512])
    lfc_d = din("lfc", [2, 2048, 8])
    s0_d = din("s0", [2, 64, 512])

    y_o = dout("y_own", [NOWN, D])
    fk_o = dout("fk_own", [NOWN, 512])
    fv_o = dout("fv_own", [NOWN, 512])
    lf_o = dout("lf_own", [NOWN, 8])
    gp_o = dout("gstate_p", [64, 512])
    gs_o = dout("gstate_s", [2, 64, 512])

    KT_s = dscr("KT_s", [512, NSEQ])
    V_s = dscr("V_s", [8, 128, 128, 128])
    Q_s = dscr("Q_s", [8, 68, NOWN])
    KX_s = dscr("KX_s", [8, 2, NSEQ])
    oK_s = dscr("oK_s", [8, 64, NOWN])
    oV_s = dscr("oV_s", [8, 128, NTO, 128])
    fo_s = dscr("fo_s", [512, NOWN])
    go_s = dscr("go_s", [4, 128, NOWN])

    w_in_v = w_in.rearrange("(kc p) c -> p kc c", p=128)
    wz_s = dscr("wz_s", [D, 2048])
    wpg_s = dscr("wpg_s", [512, D])
    wpf_s = dscr("wpf_s", [512, D])
    wout_s = dscr("wout_s", [D, D])
    wg_s = dscr("wg_s", [D, DFF])
    wu_s = dscr("wu_s", [D, DFF])
    wd_s = dscr("wd_s", [DFF, D])
    PREP = [(wz_s[:, :], w_in[:, C_ZG:C_ZG + 2048]), (wpg_s[:, :], wpg_d[:, :]), (wpf_s[:, :], wpf_d[:, :]), (wout_s[:, :], wout_d[:, :]),
            (wg_s[:, :], wg_d[:, :]), (wu_s[:, :], wu_d[:, :]), (wd_s[:, :], wd_d[:, :])]

    def sb(name, shape, dt):
        return nc.sbuf_tensor("sb_" + name, shape, dt)

    from contextlib import ExitStack
    with ExitStack() as top:
        def alloc(name, shape, dt=F32):
            return T(top.enter_context(sb(name, shape, dt)))

        psd = [top.enter_context(nc.psum_tensor("psd%d" % i, [128, 1024], F32)) for i in range(4)]
        ps = []
        for i in range(4):
            ps.append(T(psd[i][:, 0:512]))
            ps.append(T(psd[i][:, 512:1024]))
        psD = [T(psd[i][:, :]) for i in range(3)]
        bank = Ring(ps[0:6])
        obank = Ring(ps[6:8])
        nb = bank.next

        mid = ExitStack()

        def allocm(name, shape, dt=F32):
            return T(mid.enter_context(sb(name, shape, dt)))

        cst32 = alloc("cst32", [128, NCST])
        cstbf = alloc("cstbf", [128, NCST], BF16)
        bcols = alloc("bcols", [128, 48])
        valid = alloc("valid", [128, NTO])
        negc = allocm("negc", [128, 128 * 8])
        negco = allocm("negco", [128, NTO * 8])
        Cb = allocm("Cb", [128, 32 * 8])
        Csel = allocm("Csel", [128, 64])
        kflag = allocm("kflag", [128, 1024])
        oneh = allocm("oneh", [128, 256])
        negcs = allocm("negcs", [128, 2 * 16 * 8])
        ntots = allocm("ntots", [128, 16])
        S = allocm("S", [64, 512])
        gcol = allocm("gcol", [128, 4])

        ident32 = cst32[:, 0:128]
        triU32 = cst32[:, 128:256]
        ones32 = cst32[:, 256:384]
        tri2_32 = cst32[:, 384:512]
        trirev32 = cst32[:, 512:640]
        csel32 = cst32[:, 768:770]
        trirev128_32 = cst32[:, 772:900]
        identbf = cstbf[:, 0:128]
        onesbf = cstbf[:, 256:384]
        masknegbf = cstbf[:, 640:768]

        K.dma(K.sp, cst32[:, :], cst_d[:, :], writes=[cst32])
        K.dma(K.pool, cstbf[:, :], cst_d[:, :], writes=[cstbf])
        K.dma(K.sp, bcols[:, :], bcols_d[:, :], writes=[bcols])
        K.dma(K.sp, valid[:, :], valid_d[:, :], writes=[valid])
        K.dma(K.sp, kflag[:, :], kflag_d[:, :], writes=[kflag])
        K.dma(K.sp, oneh[:, :], oneh_d[:, :], writes=[oneh])
        K.op(K.dve, lambda: V_.tensor_scalar(out=gcol[:, :], in0=bcols[:, 37:41], scalar1=1.0, scalar2=None, op0=ALU.mult), [bcols], [gcol])
        K.op(K.dve, lambda: V_.memset(S[:, :], 0.0), [], [S])

        def nlf_from(ffv, out_ap, out_t, tmp_t):
            K.op(K.act, lambda: A_.activation(tmp_t[:, :], ffv[:, :], AF.Exp, scale=-1.0), [ffv], [tmp_t])
            K.op(K.act, lambda: A_.activation(out_ap, tmp_t[:, :], AF.Ln, bias=1.0, scale=1.0), [tmp_t], [out_t])

        with ExitStack() as p12:
            def al(name, shape, dt=F32):
                return T(p12.enter_context(sb(name, shape, dt)))

            Acc = [al("Acc%d" % s, [64, 512]) for s in range(8)]
            for s in range(8):
                K.op(K.pool, lambda s=s: G_.memset(Acc[s][:, :], 0.0), [], [Acc[s]])
            Wfk = al("Wfk", [128, 8, 512], BF16)
            Wfv = al("Wfv", [128, 8, 512], BF16)
            Wgv = al("Wgv", [128, 8, 512], BF16)
            Wgkff = al("Wgkff", [128, 8, 264], BF16)
            Wga = al("Wga", [128, 8, 16], BF16)
            Wa2 = al("Wa2", [16, 256], BF16)
            ba2 = al("ba2", [128, 256])
            bias = al("bias", [128, 1800])
            K.dma(K.pool, Wfk[:, :, :], w_in_v[:, :, C_FK:C_FK + 512], writes=[Wfk])
            K.dma(K.pool, Wfv[:, :, :], w_in_v[:, :, C_FV:C_FV + 512], writes=[Wfv])
            K.dma(K.pool, Wgv[:, :, :], w_in_v[:, :, C_GV:C_GV + 512], writes=[Wgv])
            K.dma(K.pool, Wgkff[:, :, 0:256], w_in_v[:, :, C_GK:C_GK + 256], writes=[Wgkff])
            K.dma(K.pool, Wgkff[:, :, 256:264], w_in_v[:, :, C_FF:C_FF + 8], writes=[Wgkff])
            K.dma(K.pool, Wga[:, :, :], w_in_v[:, :, C_GA:C_GA + 16], writes=[Wga])
            K.dma(K.pool, Wa2[:, :], wa2_d[:, :], writes=[Wa2])
            K.dma(K.sp, ba2[:, :], ba2_d[:, :], writes=[ba2])
            for (o_, c_, n_) in ((0, C_FK, 512), (512, C_FV, 512), (1024, C_GV, 512), (1536, C_GK, 256), (1792, C_FF, 8)):
                K.dma(K.sp, bias[:, o_:o_ + n_], bias_d[:, c_:c_ + n_], writes=[bias])

            XTr = Ring([al("XT%d" % i, [128, 8, 512], BF16) for i in range(2)])
            gvr = Ring([al("gv%d" % i, [128, 512], BF16) for i in range(3)])
            gkr = Ring([al("gk%d" % i, [128, 256], BF16) for i in range(3)])
            tmp256 = Ring([al("t256_%d" % i, [128, 256]) for i in range(3)])
            nlar = Ring([al("nla%d" % i, [128, 256]) for i in range(3)])
            err = Ring([al("er%d" % i, [128, 256], BF16) for i in range(2)])
            kdr = Ring([al("kdec%d" % i, [128, 256], BF16) for i in range(3)])
            eblr = Ring([al("ebl%d" % i, [64, 8]) for i in range(3)])
            ffr = Ring([al("ff%d" % i, [128, 8]) for i in range(3)])
            t8r = Ring([al("t8_%d" % i, [128, 8]) for i in range(3)])
            nlfr = Ring([al("nlf%d" % i, [128, 8]) for i in range(3)])
            gaTr = Ring([al("gaT%d" % i, [16, 512], BF16) for i in range(2)])
            noff = al("noff", [128, 8])
            K.op(K.dve, lambda: V_.memset(noff[:, :], 0.0), [], [noff])
            onesq = al("onesq", [8, 512], BF16)
            K.op(K.dve, lambda: V_.memset(onesq[:, :], 1.0), [], [onesq])
            p1s = ExitStack()

            def al1(name, shape, dt=F32):
                return T(p1s.enter_context(sb(name, shape, dt)))

            KTst = Ring([al1("KTst%d" % i, [128, 4, 512], BF16) for i in range(2)])
            clr = Ring([al1("cl%d" % i, [8, 129]) for i in range(2)])
            corr = al1("corr", [8, 128])
            hif = al1("hif", [8, 128])
            kxr = Ring([al1("kx%d" % i, [8, 2, 512], BF16) for i in range(2)])
            for t in kxr.t:
                K.op(K.dve, lambda t=t: V_.memset(t[:, :, :], 0.0), [], [t])
            p1share = {}
            Vst = Ring([al1("Vst%d" % i, [128, 8, 4, 128], BF16) for i in range(2)])
            for t in Vst.t:
                K.op(K.pool, lambda t=t: G_.memset(t[:, :, :, :], 1.0), [], [t])

            def gla_gate(gaT, c0, gk_bf, vcol):
                p = nb()
                K.mm(p, p[:, 0:256], [(gaT[0:16, c0:c0 + 128], Wa2[0:16, :])], reads=[gaT, Wa2])
                z = tmp256.next()
                K.op(K.dve, lambda: V_.tensor_tensor(z[:, :], p[:, 0:256], ba2[:, :], ALU.add), [p, ba2], [z])
                K.op(K.act, lambda: A_.activation(z[:, :], z[:, :], AF.Exp, scale=-1.0), [z], [z])
                nla = nlar.next()
                K.op(K.act, lambda: A_.activation(nla[:, :], z[:, :], AF.Ln, bias=1.0, scale=1.0), [z], [nla])
                if vcol is not None:
                    K.op(K.dve, lambda: V_.tensor_scalar(out=nla[:, :], in0=nla[:, :], scalar1=vcol, scalar2=None, op0=ALU.mult), [nla, valid], [nla])
                p2 = nb()
                K.mm(p2, p2[:, 0:256], [(trirev32, nla[:, :])], reads=[cst32, nla])
                er = err.next()
                K.op(K.act, lambda: A_.activation(er[:, :], p2[:, 0:256], AF.Exp, scale=-1.0 / 16.0), [p2], [er])
                kdec = kdr.next()
                K.op(K.pool, lambda: G_.tensor_tensor(kdec[:, :], gk_bf[:, :], er[:, :], ALU.mult), [gk_bf, er], [kdec])
                p3 = nb()
                for h in range(4):
                    K.mm(p3, p3[0:64, h * 2:(h + 1) * 2], [(nla[:, h * 64:(h + 1) * 64], csel32)], reads=[nla, cst32])
                ebl = eblr.next()
                K.op(K.act, lambda: A_.activation(ebl[:, :], p3[0:64, 0:8], AF.Exp, scale=-1.0 / 16.0), [p3], [ebl])
                return nla, kdec, ebl

            def state_update(kdec, gv_bf, ebl, ci):
                p = nb()
                for h in range(4):
                    K.mm(p, p[0:64, h * 128:(h + 1) * 128],
                         [(kdec[ci * 64:(ci + 1) * 64, h * 64:(h + 1) * 64], gv_bf[ci * 64:(ci + 1) * 64, h * 128:(h + 1) * 128])],
                         reads=[kdec, gv_bf])
                S3 = S[:, :].rearrange("p (h v) -> p h v", h=4)
                eb = ebl[:, :].rearrange("p (h c) -> p h c", c=2)[:, :, ci:ci + 1].to_broadcast([64, 4, 128])
                K.op(K.dve, lambda: V_.tensor_tensor(S3, S3, eb, ALU.mult), [S, ebl], [S])
                K.op(K.dve, lambda: V_.tensor_tensor(S[:, :], S[:, :], p[0:64, :], ALU.add), [S, p], [S])

            def tok_major(XT, c0, W_t, ncols):
                p = nb()
                K.mm(p, p[:, 0:ncols], [(XT[:, kc, c0:c0 + 128], W_t[:, kc, 0:ncols]) for kc in range(8)], reads=[XT, W_t])
                return p

            def ga_T(XT, W):
                p = nb()
                K.mm(p, p[0:16, 0:W], [(Wga[:, kc, :], XT[:, kc, 0:W]) for kc in range(8)], reads=[Wga, XT])
                gaT = gaTr.next()
                K.op(K.act, lambda: A_.activation(gaT[0:16, 0:W], p[0:16, 0:W], AF.Identity, bias=bcols[0:16, 16:17], scale=1.0), [p, bcols], [gaT])
                return gaT

            def chain_stages(g, gaT, c0, gk_bf, gv_bf, first_of_group):
                ctx = {}

                def st1():
                    p = nb()
                    K.mm(p, p[:, 0:256], [(gaT[0:16, c0:c0 + 128], Wa2[0:16, :])], reads=[gaT, Wa2])
                    z = tmp256.next()
                    K.op(K.dve, lambda: V_.tensor_tensor(z[:, :], p[:, 0:256], ba2[:, :], ALU.add), [p, ba2], [z])
                    K.op(K.act, lambda: A_.activation(z[:, :], z[:, :], AF.Exp, scale=-1.0), [z], [z])
                    nla = nlar.next()
                    K.op(K.act, lambda: A_.activation(nla[:, :], z[:, :], AF.Ln, bias=1.0, scale=1.0), [z], [nla])
                    ctx["nla"] = nla

                def st2():
                    nla = ctx["nla"]
                    p2 = nb()
                    K.mm(p2, p2[:, 0:256], [(trirev128_32, nla[:, :])], reads=[cst32, nla])
                    p3 = nb()
                    for h in range(4):
                        K.mm(p3, p3[0:64, h:h + 1], [(nla[:, h * 64:(h + 1) * 64], ones32[:, 0:1])], reads=[nla, cst32])
                    er = err.next()
                    K.op(K.act, lambda: A_.activation(er[:, :], p2[:, 0:256], AF.Exp, scale=-1.0 / 16.0), [p2], [er])
                    ebl = eblr.next()
                    K.op(K.act, lambda: A_.activation(ebl[:, 0:4], p3[0:64, 0:4], AF.Exp, scale=-1.0 / 16.0), [p3], [ebl])
                    kdec = kdr.next()
                    K.op(K.pool, lambda: G_.tensor_tensor(kdec[:, :], gk_bf[:, :], er[:, :], ALU.mult), [gk_bf, er], [kdec])
                    ctx["kdec"] = kdec
                    ctx["ebl"] = ebl

                def st3():
                    kdec, ebl = ctx["kdec"], ctx["ebl"]
                    p = nb()
                    for h in range(4):
                        K.mm(p, p[0:64, h * 128:(h + 1) * 128], [(kdec[:, h * 64:(h + 1) * 64], gv_bf[:, h * 128:(h + 1) * 128])], reads=[kdec, gv_bf])
                    S3 = S[:, :].rearrange("p (h v) -> p h v", h=4)
                    K.op(K.dve, lambda: V_.tensor_tensor(S3, S3, ebl[:, 0:4].unsqueeze(2).to_broadcast([64, 4, 128]), ALU.mult), [S, ebl], [S])
                    K.op(K.dve, lambda: V_.tensor_tensor(S[:, :], S[:, :], p[0:64, :], ALU.add), [S, p], [S])

                def st4():
                    if first_of_group and g + 1 < NSEQ // 512:
                        g1 = g + 1
                        for s_ in range(8):
                            K.op(K.dve, lambda s_=s_: V_.scalar_tensor_tensor(out=Acc[s_][:, :], in0=S[:, :], scalar=oneh[0:64, s_ * 32 + g1:s_ * 32 + g1 + 1],
                                                                               in1=Acc[s_][:, :], op0=ALU.mult, op1=ALU.add), [S, oneh, Acc[s_]], [Acc[s_]])

                return [st1, st2, st3, st4]

            def proj_stages(g, m, XT, kst, vst, out):
                j = g * 4 + m

                def a():
                    if m == 0:
                        out["gaT"] = ga_T(XT, 512)
                    pA = tok_major(XT, m * 128, Wfv, 512)
                    K.op(K.dve, lambda: V_.tensor_tensor(vst[:, :, m, 0:64], pA[:, :].rearrange("p (h c) -> p h c", h=8),
                                                         bias[:, 512:1024].rearrange("p (h c) -> p h c", h=8), ALU.add), [pA, bias], [vst])
                    if m == 3:
                        K.dma(K.sp, V_s.rearrange("h p j c -> p h j c")[:, :, g * 4:(g + 1) * 4, :], vst[:, :, :, :], reads=[vst])

                def b():
                    pB = tok_major(XT, m * 128, Wgv, 512)
                    gv_bf = gvr.next()
                    K.op(K.dve, lambda: V_.tensor_tensor(gv_bf[:, :], pB[:, :], bias[:, 1024:1536], ALU.add), [pB, bias], [gv_bf])
                    out["gv"] = gv_bf

                def c():
                    pC = tok_major(XT, m * 128, Wgkff, 264)
                    gk_bf = gkr.next()
                    K.op(K.dve, lambda: V_.tensor_tensor(gk_bf[:, :], pC[:, 0:256], bias[:, 1536:1792], ALU.add), [pC, bias], [gk_bf])
                    out["gk"] = gk_bf
                    ffv = ffr.next()
                    K.op(K.dve, lambda: V_.tensor_tensor(ffv[:, :], pC[:, 256:264], bias[:, 1792:1800], ALU.add), [pC, bias], [ffv])
                    nlf = nlfr.next()
                    nlf_from(ffv, nlf[:, :], nlf, t8r.next())
                    out["nlf"] = nlf

                def d():
                    p = nb()
                    K.mm(p, p[:, :], [(Wfk[:, kc, m * 128:(m + 1) * 128], XT[:, kc, :]) for kc in range(8)], reads=[Wfk, XT])
                    K.op(K.act, lambda: A_.activation(kst[:, m, :], p[:, :], AF.Identity, bias=bcols[:, 4 + m:5 + m], scale=1.0), [p, bcols], [kst])
                    nlf = out["nlf"]
                    pD = nb()
                    K.mm(pD, pD[:, 0:8], [(triU32, nlf[:, :])], reads=[cst32, nlf])
                    K.mm(pD, pD[:, 8:16], [(ones32, nlf[:, :])], reads=[cst32, nlf])
                    if m == 0:
                        K.op(K.dve, lambda: V_.tensor_copy(Cb[:, g * 8:(g + 1) * 8], noff[:, :]), [noff], [Cb])
                    K.op(K.dve, lambda: V_.tensor_tensor(negc[:, j * 8:(j + 1) * 8], pD[:, 0:8], noff[:, :], ALU.add), [pD, noff], [negc])
                    K.op(K.dve, lambda: V_.tensor_tensor(noff[:, :], noff[:, :], pD[:, 8:16], ALU.add), [pD, noff], [noff])
                    if m == 3:
                        K.dma(K.sp, KT_s.rearrange("(pr p) t -> p pr t", p=128)[:, :, g * 512:(g + 1) * 512], kst[:, :, :], reads=[kst])
                    pE = nb()
                    K.mm(pE, pE[0:8, 0:128], [(nlf[:, :], triU32)], reads=[nlf, cst32])
                    K.mm(pE, pE[0:8, 128:129], [(nlf[:, :], ones32[:, 0:1])], reads=[nlf, cst32])
                    cl = clr.next()
                    K.op(K.dve, lambda: V_.tensor_copy(cl[:, :], pE[0:8, 0:129]), [pE], [cl])
                    if m == 0:
                        p1share["kx"] = kxr.next()
                    kx = p1share["kx"]
                    if m % 2 == 1:
                        prev = p1share["cl"]
                        K.op(K.dve, lambda: V_.scalar_tensor_tensor(out=corr[:, :], in0=cl[:, 0:128], scalar=prev[:, 128:129], in1=prev[:, 0:128],
                                                                    op0=ALU.add, op1=ALU.subtract), [cl, prev], [corr])
                        K.op(K.dve, lambda: V_.tensor_copy(kx[:, 0, m * 128:(m + 1) * 128], corr[:, :]), [corr], [kx])
                        K.op(K.dve, lambda: V_.tensor_copy(hif[:, :], kx[:, 0, m * 128:(m + 1) * 128]), [kx], [hif])
                        K.op(K.dve, lambda: V_.tensor_tensor(kx[:, 1, m * 128:(m + 1) * 128], corr[:, :], hif[:, :], ALU.subtract), [corr, hif], [kx])
                    p1share["cl"] = cl
                    if m == 3:
                        K.dma(K.sp, KX_s[:, :, g * 512:(g + 1) * 512], kx[:, :, :], reads=[kx])

                return [a, b, c, d]

            pending = None
            gstate = {}
            NG = NSEQ // 512
            xt_srcs = [(xT_seq.rearrange("(kc p) t -> p kc t", p=128)[:, :, g_ * 512:(g_ + 1) * 512], 512) for g_ in range(NG)]
            xt_srcs += [(xT_own.rearrange("(kc p) t -> p kc t", p=128)[:, :, t0_:t0_ + W_], W_) for (t0_, W_) in SLOTS]
            xt_issued = {}

            def xt_issue(i_):
                if i_ < len(xt_srcs) and i_ not in xt_issued:
                    t_ = XTr.next()
                    src_, W_ = xt_srcs[i_]
                    K.dma(K.pool, t_[:, :, 0:W_], src_, writes=[t_])
                    xt_issued[i_] = t_

            def xt_get(i_):
                xt_issue(i_)
                xt_issue(i_ + 1)
                return xt_issued[i_]
            for g in range(NG):
                XT = xt_get(g)
                if g % 4 == 2 and PREP:
                    po_, pi_ = PREP.pop(0)
                    K.dma(K.pool, po_, pi_)
                kst = KTst.next()
                vst = Vst.next()
                for m in range(4):
                    out = gstate if m == 0 else {"gaT": gstate["gaT"]}
                    if m == 0:
                        gstate = out = {}
                    ps_ = proj_stages(g, m, XT, kst, vst, out)
                    cs_ = pending if pending is not None else [None] * 4
                    for k_ in range(4):
                        ps_[k_]()
                        if cs_[k_] is not None:
                            cs_[k_]()
                    if m == 0:
                        gstate = out
                    gaT_cur = gstate["gaT"]
                    pending = chain_stages(g, gaT_cur, m * 128, out["gk"], out["gv"], m == 3)
            for f_ in pending:
                f_()
            K.dma(K.sp, gp_o[:, :], S[:, :], reads=[S])
            ctmp = al1("ctmp", [128, 256])
            for s in range(8):
                K.op(K.dve, lambda s=s: V_.tensor_tensor(ctmp[:, :].rearrange("p (h i) -> p h i", h=8), Cb[:, :].rearrange("p (i h) -> p h i", h=8),
                                                         oneh[:, s * 32:(s + 1) * 32].unsqueeze(1).to_broadcast([128, 8, 32]), ALU.mult), [Cb, oneh], [ctmp])
                K.op(K.dve, lambda s=s: V_.reduce_sum(Csel[:, s * 8:(s + 1) * 8], ctmp[:, :].rearrange("p (h i) -> p h i", h=8), axis=mybir.AxisListType.X), [ctmp], [Csel])

            lfc_sb = al1("lfc_sb", [128, 2 * 16 * 8])
            K.dma(K.sp, lfc_sb[:, :].rearrange("p (s j h) -> p s j h", s=2, j=16), lfc_d.rearrange("s (j p) h -> p s j h", p=128), writes=[lfc_sb])
            K.op(K.dve, lambda: V_.tensor_scalar(out=lfc_sb[:, :], in0=lfc_sb[:, :], scalar1=-1.0, scalar2=None, op0=ALU.mult), [lfc_sb], [lfc_sb])
            for st in range(2):
                for j in range(16):
                    o = (st * 16 + j) * 8
                    p = nb()
                    terms = [((triU32 if j2 == j else ones32), lfc_sb[:, (st * 16 + j2) * 8:(st * 16 + j2 + 1) * 8]) for j2 in range(j + 1)]
                    K.mm(p, p[:, 0:8], terms, reads=[cst32, lfc_sb])
                    K.op(K.dve, lambda p=p, o=o: V_.tensor_copy(negcs[:, o:o + 8], p[:, 0:8]), [p], [negcs])
                p = nb()
                K.mm(p, p[:, 0:8], [(ones32, lfc_sb[:, (st * 16 + j2) * 8:(st * 16 + j2 + 1) * 8]) for j2 in range(16)], reads=[cst32, lfc_sb])
                K.op(K.dve, lambda p=p, st=st: V_.tensor_copy(ntots[:, st * 8:(st + 1) * 8], p[:, 0:8]), [p], [ntots])

            K.barrier()
            p1s.close()
            Wfq = al("Wfq", [128, 8, 512], BF16)
            Wgq = al("Wgq", [128, 8, 256], BF16)
            Wgr = al("Wgr", [128, 8, 512], BF16)
            K.dma(K.pool, Wfq[:, :, :], w_in_v[:, :, C_FQ:C_FQ + 512], writes=[Wfq])
            K.dma(K.pool, Wgq[:, :, :], w_in_v[:, :, C_GQ:C_GQ + 256], writes=[Wgq])
            K.dma(K.pool, Wgr[:, :, :], w_in_v[:, :, C_GR:C_GR + 512], writes=[Wgr])
            Qst = Ring([al("Qst%d" % i, [128, 4, 512], BF16) for i in range(1)])
            oVst = Ring([al("oVst%d" % i, [128, 8, 4, 128], BF16) for i in range(1)])
            for t in oVst.t:
                K.op(K.pool, lambda t=t: G_.memset(t[:, :, :, :], 1.0), [], [t])
            gqT = al("gqT", [64, 4, 512], BF16)
            gkT = al("gkT", [64, 4, 512], BF16)
            grT = al("grT", [128, 4, 512], BF16)
            oT = al("oT", [128, 4, 512])
            sq = al("sq", [128, 4, 512], BF16)
            goT = al("goT", [128, 4, 512], BF16)
            fko = Ring([al("fko%d" % i, [128, 512]) for i in range(2)])
            fvo = Ring([al("fvo%d" % i, [128, 512]) for i in range(2)])
            lfo = Ring([al("lfo%d" % i, [128, 8]) for i in range(2)])
            nlfo = al("nlfo", [128, 4, 8])
            ncT = al("ncT", [8, 512])
            chi = al("chi", [8, 512], BF16)
            chif = al("chif", [8, 512])
            clo = al("clo", [8, 512], BF16)
            ebr = Ring([al("eb%d" % i, [64, 512]) for i in range(2)])
            enbr = Ring([al("enb%d" % i, [64, 512]) for i in range(2)])
            qdr = Ring([al("qd%d" % i, [64, 4, 128], BF16) for i in range(2)])
            kddr = Ring([al("kd%d" % i, [64, 4, 128], BF16) for i in range(2)])
            Amr = Ring([al("Am%d" % i, [128, 4, 128], BF16) for i in range(2)])
            Sbr = Ring([al("Sb%d" % i, [64, 512], BF16) for i in range(4)])
            rtr = Ring([al("rt%d" % i, [128, 512]) for i in range(1)])
            onr = Ring([al("on%d" % i, [128, 512]) for i in range(1)])

            for si, (tok0, W) in enumerate(SLOTS):
                nt = W // 128
                tm0 = tok0 // 128
                XT = xt_get(NG + si)
                if si < 8:
                    K.op(K.dve, lambda si=si: V_.tensor_copy(S[:, :], Acc[si][:, :]), [Acc[si]], [S])
                else:
                    K.dma(K.sp, S[:, :], s0_d[si - 8], writes=[S])
                for (Wt, bc0, dst, scale) in ((Wfq, 0, Q_s, 0.125), (Wfk, 4, oK_s, 1.0)):
                    qst = Qst.next()
                    for pr in range(4):
                        p = nb()
                        K.mm(p, p[:, 0:W], [(Wt[:, kc, pr * 128:(pr + 1) * 128], XT[:, kc, 0:W]) for kc in range(8)], reads=[Wt, XT])
                        K.op(K.dve, lambda p=p, pr=pr, qst=qst, bc0=bc0, scale=scale: V_.tensor_scalar(
                            out=qst[:, pr, 0:W], in0=p[:, 0:W], scalar1=bcols[:, bc0 + pr:bc0 + pr + 1], scalar2=scale, op0=ALU.add, op1=ALU.mult), [p, bcols], [qst])
                    for hh in range(2):
                        K.dma(K.sp, dst[:, 0:64, tok0:tok0 + W].rearrange("(pr hh) d t -> hh d pr t", hh=2)[hh], qst[hh * 64:(hh + 1) * 64, :, 0:W], reads=[qst])
                for h in range(4):
                    p = nb()
                    K.mm(p, p[0:64, 0:W], [(Wgq[:, kc, h * 64:(h + 1) * 64], XT[:, kc, 0:W]) for kc in range(8)], reads=[Wgq, XT])
                    K.op(K.dve, lambda p=p, h=h: V_.tensor_scalar(out=gqT[:, h, 0:W], in0=p[0:64, 0:W], scalar1=bcols[0:64, 8 + h:9 + h], scalar2=0.125,
                                                                  op0=ALU.add, op1=ALU.mult), [p, bcols], [gqT])
                    p = nb()
                    K.mm(p, p[0:64, 0:W], [(Wgkff[:, kc, h * 64:(h + 1) * 64], XT[:, kc, 0:W]) for kc in range(8)], reads=[Wgkff, XT])
                    K.op(K.act, lambda p=p, h=h: A_.activation(gkT[:, h, 0:W], p[0:64, 0:W], AF.Identity, bias=bcols[0:64, 12 + h:13 + h], scale=1.0), [p, bcols], [gkT])
                    p = nb()
                    K.mm(p, p[:, 0:W], [(Wgr[:, kc, h * 128:(h + 1) * 128], XT[:, kc, 0:W]) for kc in range(8)], reads=[Wgr, XT])
                    K.op(K.act, lambda p=p, h=h: A_.activation(grT[:, h, 0:W], p[:, 0:W], AF.Silu, bias=bcols[:, 17 + h:18 + h], scale=1.0), [p, bcols], [grT])
                gaT = ga_T(XT, W)
                ovst = oVst.next()
                def proj_own(m, out):
                    tm = tm0 + m
                    r0 = tm * 128

                    def a():
                        pA = tok_major(XT, m * 128, Wfk, 512)
                        fk_t = fko.next()
                        K.op(K.dve, lambda: V_.tensor_tensor(fk_t[:, :], pA[:, :], bias[:, 0:512], ALU.add), [pA, bias], [fk_t])
                        K.dma(K.sp, fk_o[r0:r0 + 128, :], fk_t[:, :], reads=[fk_t])

                    def b():
                        pB = tok_major(XT, m * 128, Wfv, 512)
                        fv_t = fvo.next()
                        K.op(K.dve, lambda: V_.tensor_tensor(fv_t[:, :], pB[:, :], bias[:, 512:1024], ALU.add), [pB, bias], [fv_t])
                        K.dma(K.sp, fv_o[r0:r0 + 128, :], fv_t[:, :], reads=[fv_t])
                        K.op(K.pool, lambda: G_.tensor_copy(ovst[:, :, m, 0:64], fv_t[:, :].rearrange("p (h c) -> p h c", h=8)), [fv_t], [ovst])

                    def c():
                        pC = tok_major(XT, m * 128, Wgv, 512)
                        gv_bf = gvr.next()
                        K.op(K.dve, lambda: V_.tensor_tensor(gv_bf[:, :], pC[:, :], bias[:, 1024:1536], ALU.add), [pC, bias], [gv_bf])
                        out["gv"] = gv_bf

                    def d():
                        pD = tok_major(XT, m * 128, Wgkff, 264)
                        gtmp = tmp256.next()
                        K.op(K.dve, lambda: V_.tensor_tensor(gtmp[:, :], pD[:, 0:256], bias[:, 1536:1792], ALU.add), [pD, bias], [gtmp])
                        gk_bf = gkr.next()
                        K.op(K.dve, lambda: V_.tensor_scalar(out=gk_bf[:, :], in0=gtmp[:, :], scalar1=valid[:, tm:tm + 1], scalar2=None, op0=ALU.mult),
                             [gtmp, valid], [gk_bf])
                        out["gk"] = gk_bf
                        ffv = ffr.next()
                        K.op(K.dve, lambda: V_.tensor_tensor(ffv[:, :], pD[:, 256:264], bias[:, 1792:1800], ALU.add), [pD, bias], [ffv])
                        nlf_from(ffv, nlfo[:, m, :], nlfo, t8r.next())
                        lf_t = lfo.next()
                        K.op(K.dve, lambda: V_.tensor_scalar(out=lf_t[:, :], in0=nlfo[:, m, :], scalar1=-1.0, scalar2=None, op0=ALU.mult), [nlfo], [lf_t])
                        K.dma(K.sp, lf_o[r0:r0 + 128, :], lf_t[:, :], reads=[lf_t])

                    return [a, b, c, d]

                def chain_own(m, out):
                    tm = tm0 + m
                    ctx = {}

                    def c1():
                        p = nb()
                        K.mm(p, p[:, 0:256], [(gaT[0:16, m * 128:(m + 1) * 128], Wa2[0:16, :])], reads=[gaT, Wa2])
                        z = tmp256.next()
                        K.op(K.dve, lambda: V_.tensor_tensor(z[:, :], p[:, 0:256], ba2[:, :], ALU.add), [p, ba2], [z])
                        K.op(K.act, lambda: A_.activation(z[:, :], z[:, :], AF.Exp, scale=-1.0), [z], [z])
                        nla = nlar.next()
                        K.op(K.act, lambda: A_.activation(nla[:, :], z[:, :], AF.Ln, bias=1.0, scale=1.0), [z], [nla])
                        K.op(K.dve, lambda: V_.tensor_scalar(out=nla[:, :], in0=nla[:, :], scalar1=valid[:, tm:tm + 1], scalar2=None, op0=ALU.mult), [nla, valid], [nla])
                        ctx["nla"] = nla

                    def c2():
                        nla = ctx["nla"]
                        gk_bf = out["gk"]
                        p2 = nb()
                        K.mm(p2, p2[:, 0:256], [(trirev128_32, nla[:, :])], reads=[cst32, nla])
                        p3 = nb()
                        for h in range(4):
                            K.mm(p3, p3[0:64, h:h + 1], [(nla[:, h * 64:(h + 1) * 64], ones32[:, 0:1])], reads=[nla, cst32])
                        pT = nb()
                        for h in range(4):
                            K.mm(pT, pT[0:64, h * 128:(h + 1) * 128], [(nla[:, h * 64:(h + 1) * 64], triU32)], reads=[nla, cst32])
                        er = err.next()
                        K.op(K.act, lambda: A_.activation(er[:, :], p2[:, 0:256], AF.Exp, scale=-1.0 / 16.0), [p2], [er])
                        ebl = eblr.next()
                        K.op(K.act, lambda: A_.activation(ebl[:, 0:4], p3[0:64, 0:4], AF.Exp, scale=-1.0 / 16.0), [p3], [ebl])
                        eb = ebr.next()
                        enb = enbr.next()
                        K.op(K.act, lambda: A_.activation(eb[:, :], pT[0:64, :], AF.Exp, scale=-1.0 / 16.0), [pT], [eb])
                        K.op(K.act, lambda: A_.activation(enb[:, :], pT[0:64, :], AF.Exp, scale=1.0 / 16.0), [pT], [enb])
                        kdec = kdr.next()
                        K.op(K.pool, lambda: G_.tensor_tensor(kdec[:, :], gk_bf[:, :], er[:, :], ALU.mult), [gk_bf, er], [kdec])
                        qd = qdr.next()
                        kd = kddr.next()
                        K.op(K.pool, lambda: G_.tensor_tensor(qd[:, :, :], gqT[:, :, m * 128:(m + 1) * 128], eb[:, :].rearrange("p (h t) -> p h t", h=4), ALU.mult),
                             [gqT, eb], [qd])
                        K.op(K.pool, lambda: G_.tensor_tensor(kd[:, :, :], gkT[:, :, m * 128:(m + 1) * 128], enb[:, :].rearrange("p (h t) -> p h t", h=4), ALU.mult),
                             [gkT, enb], [kd])
                        ctx.update(kdec=kdec, ebl=ebl, qd=qd, kd=kd)

                    def c3():
                        qd, kd = ctx["qd"], ctx["kd"]
                        pA2 = nb()
                        for h in range(4):
                            K.mm(pA2, pA2[:, h * 128:(h + 1) * 128], [(kd[:, h, :], qd[:, h, :])], reads=[kd, qd])
                        Am = Amr.next()
                        K.op(K.dve, lambda: V_.tensor_tensor(Am[:, :, :], pA2[:, :].rearrange("p (h t) -> p h t", h=4),
                                                             triU32.unsqueeze(1).to_broadcast([128, 4, 128]), ALU.mult), [pA2, cst32], [Am])
                        Sb0 = Sbr.next()
                        K.op(K.dve, lambda: V_.tensor_copy(Sb0[:, :], S[:, :]), [S], [Sb0])
                        kdec, ebl, gv_bf = ctx["kdec"], ctx["ebl"], out["gv"]
                        p = nb()
                        for h in range(4):
                            K.mm(p, p[0:64, h * 128:(h + 1) * 128], [(kdec[:, h * 64:(h + 1) * 64], gv_bf[:, h * 128:(h + 1) * 128])], reads=[kdec, gv_bf])
                        S3 = S[:, :].rearrange("p (h v) -> p h v", h=4)
                        K.op(K.dve, lambda: V_.tensor_tensor(S3, S3, ebl[:, 0:4].unsqueeze(2).to_broadcast([64, 4, 128]), ALU.mult), [S, ebl], [S])
                        K.op(K.dve, lambda: V_.tensor_tensor(S[:, :], S[:, :], p[0:64, :], ALU.add), [S, p], [S])
                        ctx.update(Am=Am, Sb0=Sb0)

                    def c4():
                        qd, Am, Sb0, gv_bf = ctx["qd"], ctx["Am"], ctx["Sb0"], out["gv"]
                        pO = nb()
                        for h in range(4):
                            K.mm(pO, pO[:, h * 128:(h + 1) * 128], [(gv_bf[:, h * 128:(h + 1) * 128], Am[:, h, :])], start=True, stop=False, reads=[gv_bf, Am])
                            K.mm(pO, pO[:, h * 128:(h + 1) * 128], [(Sb0[:, h * 128:(h + 1) * 128], qd[:, h, :])], start=False, stop=True, reads=[Sb0, qd])
                        K.op(K.act, lambda: A_.copy(oT[:, :, m * 128:(m + 1) * 128], pO[:, :].rearrange("p (h t) -> p h t", h=4)), [pO], [oT])

                    return [c1, c2, c3, c4]

                outs = [dict() for _ in range(nt)]
                for f_ in proj_own(0, outs[0]):
                    f_()
                for m in range(1, nt):
                    ps_ = proj_own(m, outs[m])
                    cs_ = chain_own(m - 1, outs[m - 1])
                    for k_ in range(4):
                        ps_[k_]()
                        cs_[k_]()
                for f_ in chain_own(nt - 1, outs[nt - 1]):
                    f_()
                if si >= 8:
                    K.dma(K.sp, gs_o[si - 8], S[:, :], reads=[S])
                K.dma(K.sp, oV_s.rearrange("h p j c -> p h j c")[:, :, tm0:tm0 + nt, :], ovst[:, :, 0:nt, :], reads=[ovst])
                for m in range(nt):
                    p = nb()
                    K.mm(p, p[:, 0:8], [((triU32 if m2 == m else ones32), nlfo[:, m2, :]) for m2 in range(m + 1)], reads=[cst32, nlfo])
                    K.op(K.dve, lambda p=p, m=m: V_.tensor_copy(negco[:, (tm0 + m) * 8:(tm0 + m + 1) * 8], p[:, 0:8]), [p], [negco])
                    p = nb()
                    K.mm(p, p[0:8, 0:128], [(nlfo[:, m2, :], (triU32 if m2 == m else ones32)) for m2 in range(m + 1)], reads=[cst32, nlfo])
                    K.op(K.dve, lambda p=p, m=m: V_.tensor_copy(ncT[:, m * 128:(m + 1) * 128], p[0:8, 0:128]), [p], [ncT])
                K.op(K.dve, lambda: V_.tensor_scalar(out=chi[:, 0:W], in0=ncT[:, 0:W], scalar1=-1.0, scalar2=None, op0=ALU.mult), [ncT], [chi])
                K.op(K.dve, lambda: V_.tensor_copy(chif[:, 0:W], chi[:, 0:W]), [chi], [chif])
                K.op(K.dve, lambda: V_.scalar_tensor_tensor(out=clo[:, 0:W], in0=ncT[:, 0:W], scalar=-1.0, in1=chif[:, 0:W], op0=ALU.mult, op1=ALU.subtract), [ncT, chif], [clo])
                K.dma(K.sp, Q_s[:, 64, tok0:tok0 + W], chi[:, 0:W], reads=[chi])
                K.dma(K.sp, Q_s[:, 65, tok0:tok0 + W], clo[:, 0:W], reads=[clo])
                K.dma(K.sp, Q_s[:, 66, tok0:tok0 + W], onesq[:, 0:W], reads=[onesq])
                K.dma(K.sp, Q_s[:, 67, tok0:tok0 + W], onesq[:, 0:W], reads=[onesq])
                K.op(K.act, lambda: A_.activation(sq[:, :, 0:W], oT[:, :, 0:W], AF.Square), [oT], [sq])
                for h in range(4):
                    p = nb()
                    K.mm(p, p[:, 0:W], [(onesbf, sq[:, h, 0:W])], reads=[cstbf, sq])
                    rt = rtr.next()
                    K.op(K.act, lambda p=p, rt=rt: A_.activation(rt[:, 0:W], p[:, 0:W], AF.Ln, bias=cst32[:, 770:771], scale=1.0 / 128.0), [p, cst32], [rt])
                    K.op(K.act, lambda rt=rt: A_.activation(rt[:, 0:W], rt[:, 0:W], AF.Exp, scale=-0.5), [rt], [rt])
                    on = onr.next()
                    K.op(K.dve, lambda on=on, rt=rt, h=h: V_.tensor_tensor(on[:, 0:W], oT[:, h, 0:W], rt[:, 0:W], ALU.mult), [oT, rt], [on])
                    K.op(K.dve, lambda on=on, h=h: V_.scalar_tensor_tensor(out=goT[:, h, 0:W], in0=on[:, 0:W], scalar=gcol[:, h:h + 1], in1=grT[:, h, 0:W],
                                                                           op0=ALU.mult, op1=ALU.mult), [on, gcol, grT], [goT])
                K.dma(K.sp, go_s.rearrange("h v t -> v h t")[:, :, tok0:tok0 + W], goT[:, :, 0:W], reads=[goT])
            K.barrier()

        with ExitStack() as p2b:
            def al(name, shape, dt=F32):
                return T(p2b.enter_context(sb(name, shape, dt)))

            KA_raw = p2b.enter_context(sb("KA", [128, NSEQ], BF16))
            VA_raw = p2b.enter_context(sb("VA", [128, 128, 128], BF16))
            KAc = [T(KA_raw[:, c * 4096:(c + 1) * 4096]) for c in range(4)]
            VAc = [T(VA_raw[:, c * 32:(c + 1) * 32, :]) for c in range(4)]
            KAs = Ring([al("KAs%d" % i, [128, 2048], BF16) for i in range(2)])
            VAs = Ring([al("VAs%d" % i, [128, 16, 128], BF16) for i in range(2)])
            QAr = [al("QA%d" % i, [128, NOWN], BF16) for i in range(2)]
            oKAr = [al("oKA%d" % i, [128, NOWN], BF16) for i in range(2)]
            oVAr = [al("oVA%d" % i, [128, NTO, 128], BF16) for i in range(2)]
            for QA in QAr[0:1]:
                K.op(K.pool, lambda QA=QA: G_.memset(QA[64:128, :], 0.0), [], [QA])
            for t_ in reversed(KAc):
                K.op(K.dve, lambda t_=t_: V_.memset(t_[64:128, :], 0.0), [], [t_])
                K.op(K.dve, lambda t_=t_: V_.memset(t_[64:66, :], 1.0), [], [t_])
            for oKA in oKAr[0:1]:
                K.op(K.pool, lambda oKA=oKA: G_.memset(oKA[64:128, :], 0.0), [], [oKA])
                K.op(K.pool, lambda oKA=oKA: G_.memset(oKA[64:66, :], 1.0), [], [oKA])
            for QA in QAr[1:2]:
                K.op(K.pool, lambda QA=QA: G_.memset(QA[64:128, :], 0.0), [], [QA])
            for oKA in oKAr[1:2]:
                K.op(K.pool, lambda oKA=oKA: G_.memset(oKA[64:128, :], 0.0), [], [oKA])
                K.op(K.pool, lambda oKA=oKA: G_.memset(oKA[64:66, :], 1.0), [], [oKA])
            for t_ in KAs.t:
                K.op(K.pool, lambda t_=t_: G_.memset(t_[64:128, :], 0.0), [], [t_])
                K.op(K.pool, lambda t_=t_: G_.memset(t_[64:66, :], 1.0), [], [t_])
            for t_ in VAs.t:
                K.op(K.pool, lambda t_=t_: G_.memset(t_[:, :, :], 1.0), [], [t_])
            biar = Ring([al("bia%d" % i, [128, 128]) for i in range(10)])
            Pr = Ring([al("P%d" % i, [128, 1024], BF16) for i in range(3)])
            otr = Ring([al("ot%d" % i, [128, 512]) for i in range(2)])
            dshr = Ring([al("dsh%d" % i, [64, 512]) for i in range(2)])
            for_ = Ring([al("fo%d" % i, [64, 512], BF16) for i in range(2)])
            sbank = Ring(psD)

            def make_bias(si, h, n, kind):
                tok0, W = SLOTS[si]
                nt = W // 128
                tm0 = tok0 // 128
                bia = biar.next()
                if kind == "p":
                    K.op(K.dve, lambda: V_.scalar_tensor_tensor(out=bia[:, 0:n], in0=negc[:, :].rearrange("p (j h) -> p h j", h=8)[:, h, 0:n],
                                                                scalar=Csel[:, si * 8 + h:si * 8 + h + 1], in1=kflag[:, si * 128:si * 128 + n],
                                                                op0=ALU.subtract, op1=ALU.add), [negc, Csel, kflag], [bia])
                else:
                    st = si - 8
                    K.op(K.dve, lambda: V_.tensor_scalar(out=bia[:, 0:n], in0=negcs[:, :].rearrange("p (s j h) -> p s h j", s=2, h=8)[:, st, h, 0:n],
                                                         scalar1=ntots[:, st * 8 + h:st * 8 + h + 1], scalar2=None, op0=ALU.subtract), [negcs, ntots], [bia])
                return bia

            def attend(si, h, n, kind, kblk, vblk, bia, QA, oKA, oVA):
                tok0, W = SLOTS[si]
                nt = W // 128
                tm0 = tok0 // 128
                pO = obank.next()
                if kind == "p":
                    items = [("pp", i_) for i_ in reversed(range(n // 2))] + [("own", m) for m in range(nt)]
                else:
                    items = [("pre", j) for j in reversed(range(n))] + [("own", m) for m in range(nt)]

                def emit_S(it):
                    kind_, j = it
                    p = sbank.next()
                    if kind_ == "pp":
                        for u in range(2):
                            kt_, kap_ = kblk(2 * j + u)
                            K.mm(p, p[:, u * 512:u * 512 + 512], [(kap_, QA[:, tok0:tok0 + 512])], reads=[kt_, QA])
                    elif kind_ == "pre":
                        kt_, kap_ = kblk(j)
                        K.mm(p, p[:, 0:W], [(kap_, QA[:, tok0:tok0 + W])], reads=[kt_, QA])
                    else:
                        q0 = j * 128
                        K.mm(p, p[:, q0:W], [(oKA[:, tok0 + q0:tok0 + q0 + 128], QA[:, tok0 + q0:tok0 + W])], start=True, stop=False, reads=[oKA, QA])
                        K.mm(p, p[:, q0:q0 + 128], [(identbf, masknegbf)], start=False, stop=True, reads=[cstbf])
                    return p

                def emit_E(it, p):
                    kind_, j = it
                    P = Pr.next()
                    if kind_ == "pp":
                        K.op(K.act, lambda: A_.activation(P[:, 0:1024], p[:, 0:1024], AF.Exp, bias=bia[:, 2 * j:2 * j + 1], scale=1.0), [p, bia], [P])
                    elif kind_ == "pre":
                        K.op(K.act, lambda: A_.activation(P[:, 0:W], p[:, 0:W], AF.Exp, bias=bia[:, j:j + 1], scale=1.0), [p, bia], [P])
                    else:
                        q0 = j * 128
                        K.op(K.act, lambda: A_.activation(P[:, q0:W], p[:, q0:W], AF.Exp, bias=negco[:, (tm0 + j) * 8 + h:(tm0 + j) * 8 + h + 1], scale=1.0), [p, negco], [P])
                    return P

                def emit_PV(it, P, first, last):
                    kind_, j = it
                    if kind_ == "pp":
                        for u in range(2):
                            vt_, vap_ = vblk(2 * j + u)
                            K.mm(pO, pO[:, 0:512], [(vap_, P[:, u * 512:u * 512 + 512])], start=(first and u == 0), stop=(last and u == 1), reads=[vt_, P])
                    elif kind_ == "pre":
                        vt_, vap_ = vblk(j)
                        K.mm(pO, pO[:, 0:W], [(vap_, P[:, 0:W])], start=first, stop=last, reads=[vt_, P])
                    else:
                        q0 = j * 128
                        K.mm(pO, pO[:, q0:W], [(oVA[:, tm0 + j, :], P[:, q0:W])], start=first, stop=last, reads=[oVA, P])

                LA = 2
                pend = [emit_S(items[i_]) for i_ in range(min(LA, len(items)))]
                for i, it in enumerate(items):
                    p = pend.pop(0)
                    if i + LA < len(items):
                        pend.append(emit_S(items[i + LA]))
                    P = emit_E(it, p)
                    emit_PV(it, P, i == 0, i == len(items) - 1)
                ot = otr.next()
                K.op(K.dve, lambda: V_.tensor_copy(ot[:, 0:W], pO[:, 0:W]), [pO], [ot])
                dsh = dshr.next()
                K.dma(K.sp, dsh[:, 0:W], ot[64:128, 0:W], reads=[ot], writes=[dsh])
                K.op(K.dve, lambda: V_.reciprocal(dsh[:, 0:W], dsh[:, 0:W]), [dsh], [dsh])
                fo = for_.next()
                K.op(K.dve, lambda: V_.tensor_tensor(fo[:, 0:W], ot[0:64, 0:W], dsh[:, 0:W], ALU.mult), [ot, dsh], [fo])
                K.dma(K.sp, fo_s[h * 64:(h + 1) * 64, tok0:tok0 + W], fo[:, 0:W], reads=[fo])

            def load_q(h_):
                QA, oKA, oVA = QAr[h_ % 2], oKAr[h_ % 2], oVAr[h_ % 2]
                K.dma(K.sp, QA[0:68, :], Q_s[h_], writes=[QA])
                K.dma(K.sp, oKA[0:64, :], oK_s[h_], writes=[oKA])
                K.dma(K.sp, oVA[:, :, :], oV_s[h_], writes=[oVA])

            def load_kv_chunk(h_, c):
                K.dma(K.sp, KAc[c][0:64, :], KT_s[h_ * 64:(h_ + 1) * 64, c * 4096:(c + 1) * 4096], writes=[KAc[c]])
                K.dma(K.sp, KAc[c][66:68, :], KX_s[h_, :, c * 4096:(c + 1) * 4096], writes=[KAc[c]])
                K.dma(K.sp, VAc[c][:, :, :], V_s[h_, :, c * 32:(c + 1) * 32, :], writes=[VAc[c]])

            def load_kv(h_):
                for c in (3, 2, 1, 0):
                    load_kv_chunk(h_, c)

            load_q(0)
            load_kv(0)
            for h in range(8):
                QA, oKA, oVA = QAr[h % 2], oKAr[h % 2], oVAr[h % 2]
                if h + 1 < 8:
                    load_q(h + 1)
                sk = []
                for st in range(2):
                    ks_, vs_ = KAs.next(), VAs.next()
                    K.dma(K.pool, ks_[0:64, :], kTc_d[st, h * 64:(h + 1) * 64, :], writes=[ks_])
                    K.dma(K.pool, vs_[:, :, 0:64], vc_d[st].rearrange("(j p) (h c) -> p j h c", p=128, h=8)[:, :, h, :], writes=[vs_])
                    sk.append((ks_, vs_))
                if h == 0:
                    bias_p = [make_bias(si, 0, NPRE[si], "p") for si in range(8)]
                bias_t = bias_p + [make_bias(8 + st, h, 16, "s") for st in range(2)]
                free_after = {6: 3, 4: 2, 2: 1, 0: 0}
                for si in range(7, -1, -1):
                    attend(si, h, NPRE[si], "p",
                           lambda j: (KAc[j // 32], KAc[j // 32][:, (j % 32) * 128:(j % 32 + 1) * 128]),
                           lambda j: (VAc[j // 32], VAc[j // 32][:, j % 32, :]), bias_t[si], QA, oKA, oVA)
                    if h + 1 < 8 and si in free_after:
                        load_kv_chunk(h + 1, free_after[si])
                if h + 1 < 8:
                    bias_p = [make_bias(si, h + 1, NPRE[si], "p") for si in range(8)]
                for st in range(2):
                    ks_, vs_ = sk[st]
                    attend(8 + st, h, 16, "s", lambda j, ks_=ks_: (ks_, ks_[:, j * 128:(j + 1) * 128]), lambda j, vs_=vs_: (vs_, vs_[:, j, :]),
                           bias_t[8 + st], QA, oKA, oVA)
            K.barrier()
        mid.close()

        with ExitStack() as p3:
            def al(name, shape, dt=F32):
                return T(p3.enter_context(sb(name, shape, dt)))

            Wpg = al("Wpg", [128, 4, D], BF16)
            Wpf = al("Wpf", [128, 4, D], BF16)
            Wout = al("Wout", [128, 8, D], BF16)
            lnp = al("lnp", [128, 4 * D])
            K.dma(K.sp, Wpg[:, :, :], wpg_s.rearrange("(h p) e -> p h e", p=128), writes=[Wpg])
            K.dma(K.sp, Wpf[:, :, :], wpf_s.rearrange("(h p) e -> p h e", p=128), writes=[Wpf])
            K.dma(K.sp, Wout[:, :, :], wout_s.rearrange("(kc p) e -> p kc e", p=128), writes=[Wout])
            K.dma(K.sp, lnp[:, :], ln_d[:, :], writes=[lnp])
            wpool = Ring([al("wp%d" % i, [128, 8, 512], BF16) for i in range(4)])
            Wdr = Ring([al("Wd%d" % i, [128, NFC, 256], BF16) for i in range(2)])
            XT = al("XT3", [128, 8, 512], BF16)
            xin = Ring([al("xin%d" % i, [128, D]) for i in range(1)])
            foT = al("foT", [128, 4, 512], BF16)
            goT3 = al("goT3", [128, 4, 512], BF16)
            sgr = Ring([al("sg%d" % i, [128, 512], BF16) for i in range(4)])
            t1r = Ring([al("t1_%d" % i, [128, 512]) for i in range(1)])
            t2r = Ring([al("t2_%d" % i, [128, 512]) for i in range(1)])
            mT = al("mT", [128, 8, 512], BF16)
            r1r = Ring([al("r1_%d" % i, [128, D]) for i in range(2)])
            junk = al("junk", [128, D], BF16)
            x1 = al("x1", [128, 4, D])
            x1bf = Ring([al("x1bf%d" % i, [128, D], BF16) for i in range(2)])
            x1T = al("x1T", [128, 8, 512], BF16)
            hT = al("hT", [128, NFC, 512], BF16)
            st_r = Ring([al("st%d" % i, [128, 8]) for i in range(8)])
            stb_r = Ring([al("stb%d" % i, [128, 2]) for i in range(8)])

            def ln_stages(src_ap, src_t, dst_ap, dst_t, g_ap, b_ap):
                stt = st_r.next()
                stb = stb_r.next()

                def A():
                    K.op(K.dve, lambda: V_.memset(stb[:, :], 0.0), [], [stb])
                    K.op(K.dve, lambda: V_.memset(stt[:, :], 0.0), [], [stt])
                    K.op(K.dve, lambda: V_.reduce_sum(stt[:, 0:1], src_ap, axis=mybir.AxisListType.X), [src_t], [stt])
                    K.op(K.dve, lambda: V_.tensor_scalar(out=stt[:, 1:2], in0=stt[:, 0:1], scalar1=-1.0 / D, scalar2=None, op0=ALU.mult), [stt], [stt])

                def B():
                    K.op(K.act, lambda: A_.activation(junk[:, :], src_ap, AF.Square, accum_out=stb[:, 0:1]), [src_t], [junk, stb])

                def C():
                    K.op(K.dve, lambda: V_.tensor_tensor(stt[:, 4:5], stt[:, 1:2], stt[:, 1:2], ALU.mult), [stt], [stt])
                    K.op(K.dve, lambda: V_.tensor_scalar(out=stt[:, 5:6], in0=stt[:, 4:5], scalar1=-1.0, scalar2=1e-5, op0=ALU.mult, op1=ALU.add), [stt], [stt])
                    K.op(K.act, lambda: A_.activation(stt[:, 3:4], stb[:, 0:1], AF.Sqrt, bias=stt[:, 5:6], scale=1.0 / D), [stb, stt], [stt])
                    K.op(K.dve, lambda: V_.reciprocal(stt[:, 3:4], stt[:, 3:4]), [stt], [stt])
                    K.op(K.dve, lambda: V_.scalar_tensor_tensor(out=dst_ap, in0=src_ap, scalar=stt[:, 1:2], in1=g_ap, op0=ALU.add, op1=ALU.mult), [src_t, stt, lnp], [dst_t])
                    K.op(K.dve, lambda: V_.scalar_tensor_tensor(out=dst_ap, in0=dst_ap, scalar=stt[:, 3:4], in1=b_ap, op0=ALU.mult, op1=ALU.add), [dst_t, stt, lnp], [dst_t])

                return A, B, C

            ln2_pending = []

            def emit_ln2_one():
                if ln2_pending:
                    m_, r0_ = ln2_pending.pop(0)
                    A, B, C = ln_stages(x1[:, m_, :], x1, x1[:, m_, :], x1, lnp[:, 2 * D:3 * D], lnp[:, 3 * D:4 * D])
                    A()
                    B()
                    C()
                    K.dma(K.sp, y_o[r0_:r0_ + 128, :], x1[:, m_, :], reads=[x1])

            P3SLOTS = [(s_ * 512, 512) for s_ in range(8)] + [(4096, 256)]
            wz_v = wz_s.rearrange("(kc p) c -> p kc c", p=128)
            wg_v = wg_s.rearrange("(kc p) f -> p kc f", p=128)
            wu_v = wu_s.rearrange("(kc p) f -> p kc f", p=128)
            wd_v = wd_s.rearrange("(fc p) e -> p fc e", p=128)
            wsrc = []
            for _ in P3SLOTS:
                for e4_ in range(2):
                    wsrc.append((wz_v[:, :, e4_ * 512:(e4_ + 1) * 512], 512))
                    wsrc.append((wz_v[:, :, 1024 + e4_ * 512:1024 + (e4_ + 1) * 512], 512))
                for f4_ in range(6):
                    nf_ = 4 if f4_ < 5 else 2
                    wsrc.append((wg_v[:, :, f4_ * 512:f4_ * 512 + nf_ * 128], nf_ * 128))
                    wsrc.append((wu_v[:, :, f4_ * 512:f4_ * 512 + nf_ * 128], nf_ * 128))
            wiss = {}
            wcnt = [0]

            def w_issue(upto):
                for i_ in range(len(wiss), min(upto + 1, len(wsrc))):
                    t_ = wpool.next()
                    src_, n_ = wsrc[i_]
                    K.dma(K.act, t_[:, :, 0:n_], src_, writes=[t_])
                    wiss[i_] = t_

            def w_get():
                i_ = wcnt[0]
                wcnt[0] += 1
                w_issue(i_ + 2)
                return wiss[i_]

            dsrc = [wd_v[:, :, cg_ * 256:(cg_ + 1) * 256] for _ in P3SLOTS for cg_ in range(4)]
            diss = {}
            dcnt = [0]

            def d_issue(upto):
                for i_ in range(len(diss), min(upto + 1, len(dsrc))):
                    t_ = Wdr.next()
                    K.dma(K.act, t_[:, :, :], dsrc[i_], writes=[t_])
                    diss[i_] = t_

            def d_get():
                i_ = dcnt[0]
                dcnt[0] += 1
                d_issue(i_ + 1)
                return diss[i_]
            for si, (tok0, W) in enumerate(P3SLOTS):
                nt = W // 128
                def load_inputs(tok0_, W_):
                    K.dma(K.pool, XT[:, :, 0:W_], xT_own.rearrange("(kc p) t -> p kc t", p=128)[:, :, tok0_:tok0_ + W_], writes=[XT])
                    K.dma(K.sp, foT[:, :, 0:W_], fo_s.rearrange("(pr p) t -> p pr t", p=128)[:, :, tok0_:tok0_ + W_], writes=[foT])
                    K.dma(K.sp, goT3[:, :, 0:W_], go_s.rearrange("h v t -> v h t")[:, :, tok0_:tok0_ + W_], writes=[goT3])

                if si == 0:
                    load_inputs(tok0, W)
                for e4 in range(2):
                    wz = w_get()
                    wf = w_get()
                    for e in range(4):
                        ec = e4 * 4 + e
                        pz = nb()
                        K.mm(pz, pz[:, 0:W], [(wz[:, kc, e * 128:(e + 1) * 128], XT[:, kc, 0:W]) for kc in range(8)], reads=[wz, XT])
                        sg = sgr.next()
                        K.op(K.act, lambda pz=pz, sg=sg, ec=ec: A_.activation(sg[:, 0:W], pz[:, 0:W], AF.Sigmoid, bias=bcols[:, 21 + ec:22 + ec], scale=1.0), [pz, bcols], [sg])
                        pf = nb()
                        K.mm(pf, pf[:, 0:W], [(wf[:, kc, e * 128:(e + 1) * 128], XT[:, kc, 0:W]) for kc in range(8)], reads=[wf, XT])
                        sz = sgr.next()
                        K.op(K.act, lambda pf=pf, sz=sz, ec=ec: A_.activation(sz[:, 0:W], pf[:, 0:W], AF.Sigmoid, bias=bcols[:, 29 + ec:30 + ec], scale=1.0), [pf, bcols], [sz])
                        pg = nb()
                        K.mm(pg, pg[:, 0:W], [(Wpg[:, h, ec * 128:(ec + 1) * 128], goT3[:, h, 0:W]) for h in range(4)], reads=[Wpg, goT3])
                        t1 = t1r.next()
                        K.op(K.dve, lambda pg=pg, t1=t1, sg=sg: V_.tensor_tensor(t1[:, 0:W], pg[:, 0:W], sg[:, 0:W], ALU.mult), [pg, sg], [t1])
                        pff = nb()
                        K.mm(pff, pff[:, 0:W], [(Wpf[:, h, ec * 128:(ec + 1) * 128], foT[:, h, 0:W]) for h in range(4)], reads=[Wpf, foT])
                        t2 = t2r.next()
                        K.op(K.dve, lambda pff=pff, t2=t2, sz=sz: V_.tensor_tensor(t2[:, 0:W], pff[:, 0:W], sz[:, 0:W], ALU.mult), [pff, sz], [t2])
                        K.op(K.dve, lambda t1=t1, t2=t2, ec=ec: V_.tensor_tensor(mT[:, ec, 0:W], t1[:, 0:W], t2[:, 0:W], ALU.add), [t1, t2], [mT])
                        if ec % 2 == 1:
                            emit_ln2_one()
                if si + 1 < len(P3SLOTS):
                    load_inputs(*P3SLOTS[si + 1])
                def transposes(xb, m):
                    for half in range(2):
                        p = nb()
                        for kq in range(4):
                            kc = half * 4 + kq
                            K.mm(p, p[:, kq * 128:(kq + 1) * 128], [(xb[:, kc * 128:(kc + 1) * 128], identbf)], reads=[xb, cstbf])
                        K.op(K.act, lambda: A_.copy(x1T[:, half * 4:(half + 1) * 4, m * 128:(m + 1) * 128],
                                                    p[:, :].rearrange("p (k t) -> p k t", k=4)), [p], [x1T])

                while ln2_pending:
                    emit_ln2_one()
                prevC = None
                prevT = None
                for m in range(nt):
                    r0 = tok0 + m * 128
                    xi = xin.next()
                    K.dma(K.sp, xi[:, :], x_own[r0:r0 + 128, :], writes=[xi])
                    r1 = r1r.next()
                    for cg in range(2):
                        p = nb()
                        K.mm(p, p[:, :], [(mT[:, e, m * 128:(m + 1) * 128], Wout[:, e, cg * 512:(cg + 1) * 512]) for e in range(8)], reads=[mT, Wout])
                        K.op(K.dve, lambda: V_.scalar_tensor_tensor(out=r1[:, cg * 512:(cg + 1) * 512], in0=xi[:, cg * 512:(cg + 1) * 512], scalar=ALPHA,
                                                                    in1=p[:, :], op0=ALU.mult, op1=ALU.add), [p, xi], [r1])
                    A, B, C = ln_stages(r1[:, :], r1, x1[:, m, :], x1, lnp[:, 0:D], lnp[:, D:2 * D])
                    A()
                    if prevC is not None:
                        prevC()
                    B()
                    if prevT is not None:
                        transposes(*prevT)
                        prevT = None
                    if prevC is not None:
                        pm = m - 1
                        xb = x1bf.next()
                        K.op(K.act, lambda: A_.copy(xb[:, :], x1[:, pm, :]), [x1], [xb])
                        prevT = (xb, pm)
                    prevC = C
                prevC()
                if prevT is not None:
                    transposes(*prevT)
                xb = x1bf.next()
                K.op(K.act, lambda: A_.copy(xb[:, :], x1[:, nt - 1, :]), [x1], [xb])
                transposes(xb, nt - 1)
                for f4 in range(6):
                    nf = 4 if f4 < 5 else 2
                    if f4 == 0:
                        d_issue(dcnt[0] + 1)
                    wg = w_get()
                    wu = w_get()
                    for f in range(nf):
                        fc = f4 * 4 + f
                        pg = nb()
                        K.mm(pg, pg[:, 0:W], [(wg[:, kc, f * 128:(f + 1) * 128], x1T[:, kc, 0:W]) for kc in range(8)], reads=[wg, x1T])
                        sg = sgr.next()
                        K.op(K.act, lambda pg=pg, sg=sg: A_.activation(sg[:, 0:W], pg[:, 0:W], AF.Silu), [pg], [sg])
                        pu = nb()
                        K.mm(pu, pu[:, 0:W], [(wu[:, kc, f * 128:(f + 1) * 128], x1T[:, kc, 0:W]) for kc in range(8)], reads=[wu, x1T])
                        K.op(K.dve, lambda pu=pu, sg=sg, fc=fc: V_.tensor_tensor(hT[:, fc, 0:W], pu[:, 0:W], sg[:, 0:W], ALU.mult), [pu, sg], [hT])
                for cg in range(4):
                    Wd = d_get()
                    for m in range(nt):
                        p = nb()
                        K.mm(p, p[:, 0:256], [(hT[:, fc, m * 128:(m + 1) * 128], Wd[:, fc, :]) for fc in range(NFC)], reads=[hT, Wd])
                        K.op(K.dve, lambda p=p, cg=cg, m=m: V_.scalar_tensor_tensor(out=x1[:, m, cg * 256:(cg + 1) * 256], in0=x1[:, m, cg * 256:(cg + 1) * 256], scalar=ALPHA,
                                                                                    in1=p[:, 0:256], op0=ALU.mult, op1=ALU.add), [p, x1], [x1])
                for m in range(nt):
                    ln2_pending.append((m, tok0 + m * 128))
            while ln2_pending:
                emit_ln2_one()
            K.finish()
    return nc


def _host_inputs(inp):
    f = np.float32
    w_in = np.ascontiguousarray(inp["w_in"][0], f)
    b_in = np.asarray(inp["b_in"][0], f)
    bias_bc = np.ascontiguousarray(np.broadcast_to(b_in[None, :], (128, DIN)))
    bcols = np.zeros((128, 48), f)
    for pr in range(4):
        bcols[:, pr] = b_in[C_FQ + pr * 128:C_FQ + (pr + 1) * 128]
        bcols[:, 4 + pr] = b_in[C_FK + pr * 128:C_FK + (pr + 1) * 128]
        bcols[0:64, 8 + pr] = b_in[C_GQ + pr * 64:C_GQ + (pr + 1) * 64]
        bcols[0:64, 12 + pr] = b_in[C_GK + pr * 64:C_GK + (pr + 1) * 64]
        bcols[:, 17 + pr] = b_in[C_GR + pr * 128:C_GR + (pr + 1) * 128]
        bcols[:, 37 + pr] = inp["gla_norm_g"][0][pr * 128:(pr + 1) * 128]
    bcols[0:16, 16] = b_in[C_GA:C_GA + 16]
    for e in range(8):
        bcols[:, 21 + e] = b_in[C_ZG + e * 128:C_ZG + (e + 1) * 128]
        bcols[:, 29 + e] = b_in[C_ZF + e * 128:C_ZF + (e + 1) * 128]
    cst = np.zeros((128, NCST), f)
    i = np.arange(128)
    cst[:, 0:128] = np.eye(128)
    cst[:, 128:256] = (i[:, None] <= i[None, :])
    cst[:, 256:384] = 1.0
    same = (i[:, None] // 64) == (i[None, :] // 64)
    cst[:, 384:512] = same & (i[:, None] <= i[None, :])
    cst[:, 512:640] = same & (i[:, None] > i[None, :])
    cst[:, 640:768] = np.where(i[:, None] > i[None, :], -30000.0, 0.0)
    cst[:, 768] = (i < 64)
    cst[:, 769] = (i >= 64)
    cst[:, 770] = 1e-5
    cst[:, 772:900] = (i[:, None] > i[None, :])
    bc = lambda v: np.ascontiguousarray(np.broadcast_to(np.asarray(v, f)[None, :], (128, len(v))))
    ln_bc = np.concatenate([bc(inp["ln1_g"][0]), bc(inp["ln1_b"][0]), bc(inp["ln2_g"][0]), bc(inp["ln2_b"][0])], axis=1)
    shared = dict(w_in=w_in, bias_bc=bias_bc, bcols=bcols, wa2=np.ascontiguousarray(inp["w_alpha2"][0], f),
                  ba2_bc=bc(inp["b_alpha2"][0]), wpg=np.ascontiguousarray(inp["w_proj_gla"][0], f),
                  wpf=np.ascontiguousarray(inp["w_proj_fox"][0], f), wout=np.ascontiguousarray(inp["w_out"][0], f),
                  wg=np.ascontiguousarray(inp["w_ffn_gate"][0], f), wu=np.ascontiguousarray(inp["w_ffn_up"][0], f),
                  wd=np.ascontiguousarray(inp["w_ffn_down"][0], f), ln_bc=np.ascontiguousarray(ln_bc), cst=cst)
    maps = []
    xp = np.asarray(inp["x_prompt"], f)
    xs = np.asarray(inp["x_sample"], f)
    for core in range(8):
        b, c = core // 4, core % 4
        chunks = own_chunks(c)
        x_own = np.zeros((NOWN, D), f)
        for s, ci in enumerate(chunks):
            x_own[s * 512:(s + 1) * 512] = xp[b, ci * 512:(ci + 1) * 512]
        valid = np.ones((NOWN,), f)
        for st in range(2):
            sidx = core * 2 + st
            x_own[4096 + st * 128:4096 + st * 128 + 32] = xs[sidx]
            valid[4096 + st * 128 + 32:4096 + (st + 1) * 128] = 0.0
        kflag = np.zeros((8, 128), f)
        oneh = np.zeros((8, 32), f)
        for s, ci in enumerate(chunks):
            kflag[s, 4 * ci:] = NEG
            oneh[s, ci] = 1.0
        m = dict(shared)
        m["xT_seq"] = np.ascontiguousarray(xp[b].T)
        m["xT_own"] = np.ascontiguousarray(x_own.T)
        m["x_own"] = x_own
        m["valid"] = np.ascontiguousarray(valid.reshape(NTO, 128).T)
        m["kflag"] = np.ascontiguousarray(np.broadcast_to(kflag.reshape(1, 1024), (128, 1024)))
        m["onehot"] = np.ascontiguousarray(np.broadcast_to(oneh.reshape(1, 256), (128, 256)))
        sl = slice(core * 2, core * 2 + 2)
        m["kTc"] = np.ascontiguousarray(np.asarray(inp["cache_fox_k"][0][sl], f).transpose(0, 2, 3, 1).reshape(2, 512, 2048))
        m["vc"] = np.ascontiguousarray(np.asarray(inp["cache_fox_v"][0][sl], f).reshape(2, 2048, 512))
        m["lfc"] = np.ascontiguousarray(np.asarray(inp["cache_fox_logf"][0][sl], f))
        m["s0"] = np.ascontiguousarray(np.asarray(inp["state_gla"][0][sl], f).transpose(0, 2, 1, 3).reshape(2, 64, 512))
        maps.append(m)
    return maps


_NC = None


def kernel(**inp):
    global _NC
    if _NC is None:
        _NC = build()
    maps = _host_inputs(inp)
    res = run_bass_kernel_spmd(_NC, maps, core_ids=list(range(8)))
    R = res.results
    f = np.float32
    yp = np.zeros((2, NSEQ, D), f)
    fkp = np.zeros((2, NSEQ, 512), f)
    fvp = np.zeros((2, NSEQ, 512), f)
    lfp = np.zeros((2, NSEQ, 8), f)
    ys = np.zeros((16, 32, D), f)
    fks = np.zeros((16, 32, 512), f)
    fvs = np.zeros((16, 32, 512), f)
    lfs = np.zeros((16, 32, 8), f)
    gss = np.zeros((16, 4, 64, 128), f)
    gsp = np.zeros((2, 4, 64, 128), f)
    for core in range(8):
        b, c = core // 4, core % 4
        r = R[core]
        for s, ci in enumerate(own_chunks(c)):
            yp[b, ci * 512:(ci + 1) * 512] = r["y_own"][s * 512:(s + 1) * 512]
            fkp[b, ci * 512:(ci + 1) * 512] = r["fk_own"][s * 512:(s + 1) * 512]
            fvp[b, ci * 512:(ci + 1) * 512] = r["fv_own"][s * 512:(s + 1) * 512]
            lfp[b, ci * 512:(ci + 1) * 512] = r["lf_own"][s * 512:(s + 1) * 512]
        if c == 0:
            gsp[b] = r["gstate_p"].reshape(64, 4, 128).transpose(1, 0, 2)
        for st in range(2):
            sidx = core * 2 + st
            o = 4096 + st * 128
            ys[sidx] = r["y_own"][o:o + 32]
            fks[sidx] = r["fk_own"][o:o + 32]
            fvs[sidx] = r["fv_own"][o:o + 32]
            lfs[sidx] = r["lf_own"][o:o + 32]
            gss[sidx] = r["gstate_s"][st].reshape(64, 4, 128).transpose(1, 0, 2)
    return (yp, ys, gsp[None], fkp.reshape(1, 2, NSEQ, 8, 64), fvp.reshape(1, 2, NSEQ, 8, 64), lfp[None],
            gss[None], fks.reshape(1, 16, 32, 8, 64), fvs.reshape(1, 16, 32, 8, 64), lfs[None])
```

```python
import numpy as np
import concourse.bass as bass
import concourse.mybir as mybir
from concourse.bass_utils import run_bass_kernel_spmd

F32 = mybir.dt.float32
BF16 = mybir.dt.bfloat16
ALU = mybir.AluOpType
AF = mybir.ActivationFunctionType

D = 1024
NSEQ = 16384
DFF = 2816
NFC = 22
C_GQ, C_GK, C_GV, C_GA, C_GR, C_FQ, C_FK, C_FV, C_FF, C_ZG, C_ZF = 0, 256, 512, 1024, 1040, 1552, 2064, 2576, 3088, 3096, 4120
DIN = 5144
ALPHA = float(2.0 ** 0.25)
NOWN = 4096 + 256
NTO = NOWN // 128
SLOTS = [(s * 512, 512) for s in range(8)] + [(4096, 128), (4224, 128)]
NPRE = [12, 28, 44, 60, 76, 92, 108, 124]
NEG = -60000.0
NCST = 900


def own_chunks(c):
    r = []
    for g in range(4):
        r += [8 * g + c, 8 * g + 7 - c]
    return r


class T:
    __slots__ = ("ap", "w", "r")

    def __init__(self, ap):
        self.ap = ap
        self.w = None
        self.r = {}

    def __getitem__(self, k):
        return self.ap[k]


class Eng:
    def __init__(self, K, e, name, is_pe=False):
        self.e = e
        self.name = name
        self.sem = K.nc.alloc_semaphore("sem_" + name)
        self.cnt = 0
        self.waited = {}
        self.is_pe = is_pe
        self.dsems = []
        self.dcnt = []
        self.dnext = 0

    def wait(self, tok):
        sem, val = tok
        if self.waited.get(sem, 0) < val:
            self.e.wait_ge(sem, val)
            self.waited[sem] = val


class Kern:
    def __init__(self, nc, n_dsem=12):
        self.nc = nc
        self.pe = Eng(self, nc.tensor, "pe", True)
        self.act = Eng(self, nc.scalar, "act")
        self.dve = Eng(self, nc.vector, "dve")
        self.pool = Eng(self, nc.gpsimd, "pool")
        self.sp = Eng(self, nc.sync, "sp")
        self.engs = [self.pe, self.act, self.dve, self.pool, self.sp]
        for q in (self.sp, self.pool, self.act):
            for i in range(n_dsem):
                q.dsems.append(nc.alloc_semaphore("dsem_%s_%d" % (q.name, i)))
                q.dcnt.append(0)

    def _deps(self, E, reads, writes):
        for t in reads:
            if t.w is not None and not (t.w[0] is E.sem and E.is_pe):
                E.wait(t.w)
        for t in writes:
            if t.w is not None and not (t.w[0] is E.sem and E.is_pe):
                E.wait(t.w)
            for sem, val in t.r.items():
                if sem is not E.sem:
                    E.wait((sem, val))

    def _done(self, tok, reads, writes):
        for t in reads:
            if t.r.get(tok[0], 0) < tok[1]:
                t.r[tok[0]] = tok[1]
        for t in writes:
            t.w = tok
            t.r = {}

    def op(self, E, fn, reads=(), writes=()):
        self._deps(E, reads, writes)
        ins = fn()
        E.cnt += 1
        ins.then_inc(E.sem, 1)
        self._done((E.sem, E.cnt), reads, writes)

    def dma(self, Q, out_ap, in_ap, reads=(), writes=()):
        self._deps(Q, reads, writes)
        i = Q.dnext
        Q.dnext = (Q.dnext + 1) % len(Q.dsems)
        sem = Q.dsems[i]
        if Q.dcnt[i] > 0:
            Q.wait((sem, Q.dcnt[i]))
        ins = Q.e.dma_start(out=out_ap, in_=in_ap)
        Q.dcnt[i] += 16
        ins.then_inc(sem, 16)
        self._done((sem, Q.dcnt[i]), reads, writes)

    def mm(self, out_t, out_ap, terms, start=True, stop=True, reads=()):
        E = self.pe
        self._deps(E, reads, [out_t])
        n = len(terms)
        ins = None
        for i, (l, r) in enumerate(terms):
            ins = self.nc.tensor.matmul(out_ap, l, r, start=(start and i == 0), stop=(stop and i == n - 1))
        E.cnt += 1
        ins.then_inc(E.sem, 1)
        self._done((E.sem, E.cnt), reads, [out_t])

    def _alltoks(self):
        toks = []
        for E in self.engs:
            if E.cnt > 0:
                toks.append((E.sem, E.cnt))
            for s, c in zip(E.dsems, E.dcnt):
                if c > 0:
                    toks.append((s, c))
        return toks

    def barrier(self):
        toks = self._alltoks()
        for E in self.engs:
            for tok in toks:
                if tok[0] is not E.sem:
                    E.wait(tok)

    def finish(self):
        for tok in self._alltoks():
            if tok[0] is not self.sp.sem:
                self.sp.wait(tok)


class Ring:
    def __init__(self, tiles):
        self.t = tiles
        self.i = 0

    def next(self):
        t = self.t[self.i]
        self.i = (self.i + 1) % len(self.t)
        return t


def build():
    nc = bass.Bass("TRN2", target_bir_lowering=False)
    K = Kern(nc)
    V_ = nc.vector
    A_ = nc.scalar
    G_ = nc.gpsimd

    def din(name, shape):
        return nc.dram_tensor(name, shape, F32, kind="ExternalInput").ap()

    def dout(name, shape):
        return nc.dram_tensor(name, shape, F32, kind="ExternalOutput").ap()

    def dscr(name, shape, dt=BF16):
        return nc.dram_tensor(name, shape, dt, kind="Internal").ap()

    xT_seq = din("xT_seq", [D, NSEQ])
    xT_own = din("xT_own", [D, NOWN])
    x_own = din("x_own", [NOWN, D])
    valid_d = din("valid", [128, NTO])
    w_in = din("w_in", [D, DIN])
    bias_d = din("bias_bc", [128, DIN])
    bcols_d = din("bcols", [128, 48])
    wa2_d = din("wa2", [16, 256])
    ba2_d = din("ba2_bc", [128, 256])
    wpg_d = din("wpg", [512, D])
    wpf_d = din("wpf", [512, D])
    wout_d = din("wout", [D, D])
    wg_d = din("wg", [D, DFF])
    wu_d = din("wu", [D, DFF])
    wd_d = din("wd", [DFF, D])
    ln_d = din("ln_bc", [128, 4 * D])
    kflag_d = din("kflag", [128, 8 * 128])
    oneh_d = din("onehot", [128, 8 * 32])
    cst_d = din("cst", [128, NCST])
    kTc_d = din("kTc", [2, 512, 2048])
    vc_d = din("vc", [2, 2048, 512])
    lfc_d = din("lfc", [2, 2048, 8])
    s0_d = din("s0", [2, 64, 512])

    y_o = dout("y_own", [NOWN, D])
    fk_o = dout("fk_own", [NOWN, 512])
    fv_o = dout("fv_own", [NOWN, 512])
    lf_o = dout("lf_own", [NOWN, 8])
    gp_o = dout("gstate_p", [64, 512])
    gs_o = dout("gstate_s", [2, 64, 512])

    KT_s = dscr("KT_s", [512, NSEQ])
    V_s = dscr("V_s", [8, 128, 128, 128])
    Q_s = dscr("Q_s", [8, 68, NOWN])
    KX_s = dscr("KX_s", [8, 2, NSEQ])
    oK_s = dscr("oK_s", [8, 64, NOWN])
    oV_s = dscr("oV_s", [8, 128, NTO, 128])
    fo_s = dscr("fo_s", [512, NOWN])
    go_s = dscr("go_s", [4, 128, NOWN])

    w_in_v = w_in.rearrange("(kc p) c -> p kc c", p=128)
    wz_s = dscr("wz_s", [D, 2048])
    wpg_s = dscr("wpg_s", [512, D])
    wpf_s = dscr("wpf_s", [512, D])
    wout_s = dscr("wout_s", [D, D])
    wg_s = dscr("wg_s", [D, DFF])
    wu_s = dscr("wu_s", [D, DFF])
    wd_s = dscr("wd_s", [DFF, D])
    PREP = [(wz_s[:, :], w_in[:, C_ZG:C_ZG + 2048]), (wpg_s[:, :], wpg_d[:, :]), (wpf_s[:, :], wpf_d[:, :]), (wout_s[:, :], wout_d[:, :]),
            (wg_s[:, :], wg_d[:, :]), (wu_s[:, :], wu_d[:, :]), (wd_s[:, :], wd_d[:, :])]

    def sb(name, shape, dt):
        return nc.sbuf_tensor("sb_" + name, shape, dt)

    from contextlib import ExitStack
    with ExitStack() as top:
        def alloc(name, shape, dt=F32):
            return T(top.enter_context(sb(name, shape, dt)))

        psd = [top.enter_context(nc.psum_tensor("psd%d" % i, [128, 1024], F32)) for i in range(4)]
        ps = []
        for i in range(4):
            ps.append(T(psd[i][:, 0:512]))
            ps.append(T(psd[i][:, 512:1024]))
        psD = [T(psd[i][:, :]) for i in range(3)]
        bank = Ring(ps[0:6])
        obank = Ring(ps[6:8])
        nb = bank.next

        mid = ExitStack()

        def allocm(name, shape, dt=F32):
            return T(mid.enter_context(sb(name, shape, dt)))

        cst32 = alloc("cst32", [128, NCST])
        cstbf = alloc("cstbf", [128, NCST], BF16)
        bcols = alloc("bcols", [128, 48])
        valid = alloc("valid", [128, NTO])
        negc = allocm("negc", [128, 128 * 8])
        negco = allocm("negco", [128, NTO * 8])
        Cb = allocm("Cb", [128, 32 * 8])
        Csel = allocm("Csel", [128, 64])
        kflag = allocm("kflag", [128, 1024])
        oneh = allocm("oneh", [128, 256])
        negcs = allocm("negcs", [128, 2 * 16 * 8])
        ntots = allocm("ntots", [128, 16])
        S = allocm("S", [64, 512])
        gcol = allocm("gcol", [128, 4])

        ident32 = cst32[:, 0:128]
        triU32 = cst32[:, 128:256]
        ones32 = cst32[:, 256:384]
        tri2_32 = cst32[:, 384:512]
        trirev32 = cst32[:, 512:640]
        csel32 = cst32[:, 768:770]
        trirev128_32 = cst32[:, 772:900]
        identbf = cstbf[:, 0:128]
        onesbf = cstbf[:, 256:384]
        masknegbf = cstbf[:, 640:768]

        K.dma(K.sp, cst32[:, :], cst_d[:, :], writes=[cst32])
        K.dma(K.pool, cstbf[:, :], cst_d[:, :], writes=[cstbf])
        K.dma(K.sp, bcols[:, :], bcols_d[:, :], writes=[bcols])
        K.dma(K.sp, valid[:, :], valid_d[:, :], writes=[valid])
        K.dma(K.sp, kflag[:, :], kflag_d[:, :], writes=[kflag])
        K.dma(K.sp, oneh[:, :], oneh_d[:, :], writes=[oneh])
        K.op(K.dve, lambda: V_.tensor_scalar(out=gcol[:, :], in0=bcols[:, 37:41], scalar1=1.0, scalar2=None, op0=ALU.mult), [bcols], [gcol])
        K.op(K.dve, lambda: V_.memset(S[:, :], 0.0), [], [S])

        def nlf_from(ffv, out_ap, out_t, tmp_t):
            K.op(K.act, lambda: A_.activation(tmp_t[:, :], ffv[:, :], AF.Exp, scale=-1.0), [ffv], [tmp_t])
            K.op(K.act, lambda: A_.activation(out_ap, tmp_t[:, :], AF.Ln, bias=1.0, scale=1.0), [tmp_t], [out_t])

        with ExitStack() as p12:
            def al(name, shape, dt=F32):
                return T(p12.enter_context(sb(name, shape, dt)))

            Acc = [al("Acc%d" % s, [64, 512]) for s in range(8)]
            for s in range(8):
                K.op(K.pool, lambda s=s: G_.memset(Acc[s][:, :], 0.0), [], [Acc[s]])
            Wfk = al("Wfk", [128, 8, 512], BF16)
            Wfv = al("Wfv", [128, 8, 512], BF16)
            Wgv = al("Wgv", [128, 8, 512], BF16)
            Wgkff = al("Wgkff", [128, 8, 264], BF16)
            Wga = al("Wga", [128, 8, 16], BF16)
            Wa2 = al("Wa2", [16, 256], BF16)
            ba2 = al("ba2", [128, 256])
            bias = al("bias", [128, 1800])
            K.dma(K.pool, Wfk[:, :, :], w_in_v[:, :, C_FK:C_FK + 512], writes=[Wfk])
            K.dma(K.pool, Wfv[:, :, :], w_in_v[:, :, C_FV:C_FV + 512], writes=[Wfv])
            K.dma(K.pool, Wgv[:, :, :], w_in_v[:, :, C_GV:C_GV + 512], writes=[Wgv])
            K.dma(K.pool, Wgkff[:, :, 0:256], w_in_v[:, :, C_GK:C_GK + 256], writes=[Wgkff])
            K.dma(K.pool, Wgkff[:, :, 256:264], w_in_v[:, :, C_FF:C_FF + 8], writes=[Wgkff])
            K.dma(K.pool, Wga[:, :, :], w_in_v[:, :, C_GA:C_GA + 16], writes=[Wga])
            K.dma(K.pool, Wa2[:, :], wa2_d[:, :], writes=[Wa2])
            K.dma(K.sp, ba2[:, :], ba2_d[:, :], writes=[ba2])
            for (o_, c_, n_) in ((0, C_FK, 512), (512, C_FV, 512), (1024, C_GV, 512), (1536, C_GK, 256), (1792, C_FF, 8)):
                K.dma(K.sp, bias[:, o_:o_ + n_], bias_d[:, c_:c_ + n_], writes=[bias])

            XTr = Ring([al("XT%d" % i, [128, 8, 512], BF16) for i in range(2)])
            gvr = Ring([al("gv%d" % i, [128, 512], BF16) for i in range(3)])
            gkr = Ring([al("gk%d" % i, [128, 256], BF16) for i in range(3)])
            tmp256 = Ring([al("t256_%d" % i, [128, 256]) for i in range(3)])
            nlar = Ring([al("nla%d" % i, [128, 256]) for i in range(3)])
            err = Ring([al("er%d" % i, [128, 256], BF16) for i in range(2)])
            kdr = Ring([al("kdec%d" % i, [128, 256], BF16) for i in range(3)])
            eblr = Ring([al("ebl%d" % i, [64, 8]) for i in range(3)])
            ffr = Ring([al("ff%d" % i, [128, 8]) for i in range(3)])
            t8r = Ring([al("t8_%d" % i, [128, 8]) for i in range(3)])
            nlfr = Ring([al("nlf%d" % i, [128, 8]) for i in range(3)])
            gaTr = Ring([al("gaT%d" % i, [16, 512], BF16) for i in range(2)])
            noff = al("noff", [128, 8])
            K.op(K.dve, lambda: V_.memset(noff[:, :], 0.0), [], [noff])
            onesq = al("onesq", [8, 512], BF16)
            K.op(K.dve, lambda: V_.memset(onesq[:, :], 1.0), [], [onesq])
            p1s = ExitStack()

            def al1(name, shape, dt=F32):
                return T(p1s.enter_context(sb(name, shape, dt)))

            KTst = Ring([al1("KTst%d" % i, [128, 4, 512], BF16) for i in range(2)])
            clr = Ring([al1("cl%d" % i, [8, 129]) for i in range(2)])
            corr = al1("corr", [8, 128])
            hif = al1("hif", [8, 128])
            kxr = Ring([al1("kx%d" % i, [8, 2, 512], BF16) for i in range(2)])
            for t in kxr.t:
                K.op(K.dve, lambda t=t: V_.memset(t[:, :, :], 0.0), [], [t])
            p1share = {}
            Vst = Ring([al1("Vst%d" % i, [128, 8, 4, 128], BF16) for i in range(2)])
            for t in Vst.t:
                K.op(K.pool, lambda t=t: G_.memset(t[:, :, :, :], 1.0), [], [t])

            def gla_gate(gaT, c0, gk_bf, vcol):
                p = nb()
                K.mm(p, p[:, 0:256], [(gaT[0:16, c0:c0 + 128], Wa2[0:16, :])], reads=[gaT, Wa2])
                z = tmp256.next()
                K.op(K.dve, lambda: V_.tensor_tensor(z[:, :], p[:, 0:256], ba2[:, :], ALU.add), [p, ba2], [z])
                K.op(K.act, lambda: A_.activation(z[:, :], z[:, :], AF.Exp, scale=-1.0), [z], [z])
                nla = nlar.next()
                K.op(K.act, lambda: A_.activation(nla[:, :], z[:, :], AF.Ln, bias=1.0, scale=1.0), [z], [nla])
                if vcol is not None:
                    K.op(K.dve, lambda: V_.tensor_scalar(out=nla[:, :], in0=nla[:, :], scalar1=vcol, scalar2=None, op0=ALU.mult), [nla, valid], [nla])
                p2 = nb()
                K.mm(p2, p2[:, 0:256], [(trirev32, nla[:, :])], reads=[cst32, nla])
                er = err.next()
                K.op(K.act, lambda: A_.activation(er[:, :], p2[:, 0:256], AF.Exp, scale=-1.0 / 16.0), [p2], [er])
                kdec = kdr.next()
                K.op(K.pool, lambda: G_.tensor_tensor(kdec[:, :], gk_bf[:, :], er[:, :], ALU.mult), [gk_bf, er], [kdec])
                p3 = nb()
                for h in range(4):
                    K.mm(p3, p3[0:64, h * 2:(h + 1) * 2], [(nla[:, h * 64:(h + 1) * 64], csel32)], reads=[nla, cst32])
                ebl = eblr.next()
                K.op(K.act, lambda: A_.activation(ebl[:, :], p3[0:64, 0:8], AF.Exp, scale=-1.0 / 16.0), [p3], [ebl])
                return nla, kdec, ebl

            def state_update(kdec, gv_bf, ebl, ci):
                p = nb()
                for h in range(4):
                    K.mm(p, p[0:64, h * 128:(h + 1) * 128],
                         [(kdec[ci * 64:(ci + 1) * 64, h * 64:(h + 1) * 64], gv_bf[ci * 64:(ci + 1) * 64, h * 128:(h + 1) * 128])],
                         reads=[kdec, gv_bf])
                S3 = S[:, :].rearrange("p (h v) -> p h v", h=4)
                eb = ebl[:, :].rearrange("p (h c) -> p h c", c=2)[:, :, ci:ci + 1].to_broadcast([64, 4, 128])
                K.op(K.dve, lambda: V_.tensor_tensor(S3, S3, eb, ALU.mult), [S, ebl], [S])
                K.op(K.dve, lambda: V_.tensor_tensor(S[:, :], S[:, :], p[0:64, :], ALU.add), [S, p], [S])

            def tok_major(XT, c0, W_t, ncols):
                p = nb()
                K.mm(p, p[:, 0:ncols], [(XT[:, kc, c0:c0 + 128], W_t[:, kc, 0:ncols]) for kc in range(8)], reads=[XT, W_t])
                return p

            def ga_T(XT, W):
                p = nb()
                K.mm(p, p[0:16, 0:W], [(Wga[:, kc, :], XT[:, kc, 0:W]) for kc in range(8)], reads=[Wga, XT])
                gaT = gaTr.next()
                K.op(K.act, lambda: A_.activation(gaT[0:16, 0:W], p[0:16, 0:W], AF.Identity, bias=bcols[0:16, 16:17], scale=1.0), [p, bcols], [gaT])
                return gaT

            def chain_stages(g, gaT, c0, gk_bf, gv_bf, first_of_group):
                ctx = {}

                def st1():
                    p = nb()
                    K.mm(p, p[:, 0:256], [(gaT[0:16, c0:c0 + 128], Wa2[0:16, :])], reads=[gaT, Wa2])
                    z = tmp256.next()
                    K.op(K.dve, lambda: V_.tensor_tensor(z[:, :], p[:, 0:256], ba2[:, :], ALU.add), [p, ba2], [z])
                    K.op(K.act, lambda: A_.activation(z[:, :], z[:, :], AF.Exp, scale=-1.0), [z], [z])
                    nla = nlar.next()
                    K.op(K.act, lambda: A_.activation(nla[:, :], z[:, :], AF.Ln, bias=1.0, scale=1.0), [z], [nla])
                    ctx["nla"] = nla

                def st2():
                    nla = ctx["nla"]
                    p2 = nb()
                    K.mm(p2, p2[:, 0:256], [(trirev128_32, nla[:, :])], reads=[cst32, nla])
                    p3 = nb()
                    for h in range(4):
                        K.mm(p3, p3[0:64, h:h + 1], [(nla[:, h * 64:(h + 1) * 64], ones32[:, 0:1])], reads=[nla, cst32])
                    er = err.next()
                    K.op(K.act, lambda: A_.activation(er[:, :], p2[:, 0:256], AF.Exp, scale=-1.0 / 16.0), [p2], [er])
                    ebl = eblr.next()
                    K.op(K.act, lambda: A_.activation(ebl[:, 0:4], p3[0:64, 0:4], AF.Exp, scale=-1.0 / 16.0), [p3], [ebl])
                    kdec = kdr.next()
                    K.op(K.pool, lambda: G_.tensor_tensor(kdec[:, :], gk_bf[:, :], er[:, :], ALU.mult), [gk_bf, er], [kdec])
                    ctx["kdec"] = kdec
                    ctx["ebl"] = ebl

                def st3():
                    kdec, ebl = ctx["kdec"], ctx["ebl"]
                    p = nb()
                    for h in range(4):
                        K.mm(p, p[0:64, h * 128:(h + 1) * 128], [(kdec[:, h * 64:(h + 1) * 64], gv_bf[:, h * 128:(h + 1) * 128])], reads=[kdec, gv_bf])
                    S3 = S[:, :].rearrange("p (h v) -> p h v", h=4)
                    K.op(K.dve, lambda: V_.tensor_tensor(S3, S3, ebl[:, 0:4].unsqueeze(2).to_broadcast([64, 4, 128]), ALU.mult), [S, ebl], [S])
                    K.op(K.dve, lambda: V_.tensor_tensor(S[:, :], S[:, :], p[0:64, :], ALU.add), [S, p], [S])

                def st4():
                    if first_of_group and g + 1 < NSEQ // 512:
                        g1 = g + 1
                        for s_ in range(8):
                            K.op(K.dve, lambda s_=s_: V_.scalar_tensor_tensor(out=Acc[s_][:, :], in0=S[:, :], scalar=oneh[0:64, s_ * 32 + g1:s_ * 32 + g1 + 1],
                                                                               in1=Acc[s_][:, :], op0=ALU.mult, op1=ALU.add), [S, oneh, Acc[s_]], [Acc[s_]])

                return [st1, st2, st3, st4]

            def proj_stages(g, m, XT, kst, vst, out):
                j = g * 4 + m

                def a():
                    if m == 0:
                        out["gaT"] = ga_T(XT, 512)
                    pA = tok_major(XT, m * 128, Wfv, 512)
                    K.op(K.dve, lambda: V_.tensor_tensor(vst[:, :, m, 0:64], pA[:, :].rearrange("p (h c) -> p h c", h=8),
                                                         bias[:, 512:1024].rearrange("p (h c) -> p h c", h=8), ALU.add), [pA, bias], [vst])
                    if m == 3:
                        K.dma(K.sp, V_s.rearrange("h p j c -> p h j c")[:, :, g * 4:(g + 1) * 4, :], vst[:, :, :, :], reads=[vst])

                def b():
                    pB = tok_major(XT, m * 128, Wgv, 512)
                    gv_bf = gvr.next()
                    K.op(K.dve, lambda: V_.tensor_tensor(gv_bf[:, :], pB[:, :], bias[:, 1024:1536], ALU.add), [pB, bias], [gv_bf])
                    out["gv"] = gv_bf

                def c():
                    pC = tok_major(XT, m * 128, Wgkff, 264)
                    gk_bf = gkr.next()
                    K.op(K.dve, lambda: V_.tensor_tensor(gk_bf[:, :], pC[:, 0:256], bias[:, 1536:1792], ALU.add), [pC, bias], [gk_bf])
                    out["gk"] = gk_bf
                    ffv = ffr.next()
                    K.op(K.dve, lambda: V_.tensor_tensor(ffv[:, :], pC[:, 256:264], bias[:, 1792:1800], ALU.add), [pC, bias], [ffv])
                    nlf = nlfr.next()
                    nlf_from(ffv, nlf[:, :], nlf, t8r.next())
                    out["nlf"] = nlf

                def d():
                    p = nb()
                    K.mm(p, p[:, :], [(Wfk[:, kc, m * 128:(m + 1) * 128], XT[:, kc, :]) for kc in range(8)], reads=[Wfk, XT])
                    K.op(K.act, lambda: A_.activation(kst[:, m, :], p[:, :], AF.Identity, bias=bcols[:, 4 + m:5 + m], scale=1.0), [p, bcols], [kst])
                    nlf = out["nlf"]
                    pD = nb()
                    K.mm(pD, pD[:, 0:8], [(triU32, nlf[:, :])], reads=[cst32, nlf])
                    K.mm(pD, pD[:, 8:16], [(ones32, nlf[:, :])], reads=[cst32, nlf])
                    if m == 0:
                        K.op(K.dve, lambda: V_.tensor_copy(Cb[:, g * 8:(g + 1) * 8], noff[:, :]), [noff], [Cb])
                    K.op(K.dve, lambda: V_.tensor_tensor(negc[:, j * 8:(j + 1) * 8], pD[:, 0:8], noff[:, :], ALU.add), [pD, noff], [negc])
                    K.op(K.dve, lambda: V_.tensor_tensor(noff[:, :], noff[:, :], pD[:, 8:16], ALU.add), [pD, noff], [noff])
                    if m == 3:
                        K.dma(K.sp, KT_s.rearrange("(pr p) t -> p pr t", p=128)[:, :, g * 512:(g + 1) * 512], kst[:, :, :], reads=[kst])
                    pE = nb()
                    K.mm(pE, pE[0:8, 0:128], [(nlf[:, :], triU32)], reads=[nlf, cst32])
                    K.mm(pE, pE[0:8, 128:129], [(nlf[:, :], ones32[:, 0:1])], reads=[nlf, cst32])
                    cl = clr.next()
                    K.op(K.dve, lambda: V_.tensor_copy(cl[:, :], pE[0:8, 0:129]), [pE], [cl])
                    if m == 0:
                        p1share["kx"] = kxr.next()
                    kx = p1share["kx"]
                    if m % 2 == 1:
                        prev = p1share["cl"]
                        K.op(K.dve, lambda: V_.scalar_tensor_tensor(out=corr[:, :], in0=cl[:, 0:128], scalar=prev[:, 128:129], in1=prev[:, 0:128],
                                                                    op0=ALU.add, op1=ALU.subtract), [cl, prev], [corr])
                        K.op(K.dve, lambda: V_.tensor_copy(kx[:, 0, m * 128:(m + 1) * 128], corr[:, :]), [corr], [kx])
                        K.op(K.dve, lambda: V_.tensor_copy(hif[:, :], kx[:, 0, m * 128:(m + 1) * 128]), [kx], [hif])
                        K.op(K.dve, lambda: V_.tensor_tensor(kx[:, 1, m * 128:(m + 1) * 128], corr[:, :], hif[:, :], ALU.subtract), [corr, hif], [kx])
                    p1share["cl"] = cl
                    if m == 3:
                        K.dma(K.sp, KX_s[:, :, g * 512:(g + 1) * 512], kx[:, :, :], reads=[kx])

                return [a, b, c, d]

            pending = None
            gstate = {}
            NG = NSEQ // 512
            xt_srcs = [(xT_seq.rearrange("(kc p) t -> p kc t", p=128)[:, :, g_ * 512:(g_ + 1) * 512], 512) for g_ in range(NG)]
            xt_srcs += [(xT_own.rearrange("(kc p) t -> p kc t", p=128)[:, :, t0_:t0_ + W_], W_) for (t0_, W_) in SLOTS]
            xt_issued = {}

            def xt_issue(i_):
                if i_ < len(xt_srcs) and i_ not in xt_issued:
                    t_ = XTr.next()
                    src_, W_ = xt_srcs[i_]
                    K.dma(K.pool, t_[:, :, 0:W_], src_, writes=[t_])
                    xt_issued[i_] = t_

            def xt_get(i_):
                xt_issue(i_)
                xt_issue(i_ + 1)
                return xt_issued[i_]
            for g in range(NG):
                XT = xt_get(g)
                if g % 4 == 2 and PREP:
                    po_, pi_ = PREP.pop(0)
                    K.dma(K.pool, po_, pi_)
                kst = KTst.next()
                vst = Vst.next()
                for m in range(4):
                    out = gstate if m == 0 else {"gaT": gstate["gaT"]}
                    if m == 0:
                        gstate = out = {}
                    ps_ = proj_stages(g, m, XT, kst, vst, out)
                    cs_ = pending if pending is not None else [None] * 4
                    for k_ in range(4):
                        ps_[k_]()
                        if cs_[k_] is not None:
                            cs_[k_]()
                    if m == 0:
                        gstate = out
                    gaT_cur = gstate["gaT"]
                    pending = chain_stages(g, gaT_cur, m * 128, out["gk"], out["gv"], m == 3)
            for f_ in pending:
                f_()
            K.dma(K.sp, gp_o[:, :], S[:, :], reads=[S])
            ctmp = al1("ctmp", [128, 256])
            for s in range(8):
                K.op(K.dve, lambda s=s: V_.tensor_tensor(ctmp[:, :].rearrange("p (h i) -> p h i", h=8), Cb[:, :].rearrange("p (i h) -> p h i", h=8),
                                                         oneh[:, s * 32:(s + 1) * 32].unsqueeze(1).to_broadcast([128, 8, 32]), ALU.mult), [Cb, oneh], [ctmp])
                K.op(K.dve, lambda s=s: V_.reduce_sum(Csel[:, s * 8:(s + 1) * 8], ctmp[:, :].rearrange("p (h i) -> p h i", h=8), axis=mybir.AxisListType.X), [ctmp], [Csel])

            lfc_sb = al1("lfc_sb", [128, 2 * 16 * 8])
            K.dma(K.sp, lfc_sb[:, :].rearrange("p (s j h) -> p s j h", s=2, j=16), lfc_d.rearrange("s (j p) h -> p s j h", p=128), writes=[lfc_sb])
            K.op(K.dve, lambda: V_.tensor_scalar(out=lfc_sb[:, :], in0=lfc_sb[:, :], scalar1=-1.0, scalar2=None, op0=ALU.mult), [lfc_sb], [lfc_sb])
            for st in range(2):
                for j in range(16):
                    o = (st * 16 + j) * 8
                    p = nb()
                    terms = [((triU32 if j2 == j else ones32), lfc_sb[:, (st * 16 + j2) * 8:(st * 16 + j2 + 1) * 8]) for j2 in range(j + 1)]
                    K.mm(p, p[:, 0:8], terms, reads=[cst32, lfc_sb])
                    K.op(K.dve, lambda p=p, o=o: V_.tensor_copy(negcs[:, o:o + 8], p[:, 0:8]), [p], [negcs])
                p = nb()
                K.mm(p, p[:, 0:8], [(ones32, lfc_sb[:, (st * 16 + j2) * 8:(st * 16 + j2 + 1) * 8]) for j2 in range(16)], reads=[cst32, lfc_sb])
                K.op(K.dve, lambda p=p, st=st: V_.tensor_copy(ntots[:, st * 8:(st + 1) * 8], p[:, 0:8]), [p], [ntots])

            K.barrier()
            p1s.close()
            Wfq = al("Wfq", [128, 8, 512], BF16)
            Wgq = al("Wgq", [128, 8, 256], BF16)
            Wgr = al("Wgr", [128, 8, 512], BF16)
            K.dma(K.pool, Wfq[:, :, :], w_in_v[:, :, C_FQ:C_FQ + 512], writes=[Wfq])
            K.dma(K.pool, Wgq[:, :, :], w_in_v[:, :, C_GQ:C_GQ + 256], writes=[Wgq])
            K.dma(K.pool, Wgr[:, :, :], w_in_v[:, :, C_GR:C_GR + 512], writes=[Wgr])
            Qst = Ring([al("Qst%d" % i, [128, 4, 512], BF16) for i in range(2)])
            oVst = Ring([al("oVst%d" % i, [128, 8, 4, 128], BF16) for i in range(1)])
            for t in oVst.t:
                K.op(K.pool, lambda t=t: G_.memset(t[:, :, :, :], 1.0), [], [t])
            gqT = al("gqT", [64, 4, 512], BF16)
            gkT = al("gkT", [64, 4, 512], BF16)
            grT = al("grT", [128, 4, 512], BF16)
            oT = al("oT", [128, 4, 512])
            sq = al("sq", [128, 4, 512], BF16)
            goT = al("goT", [128, 4, 512], BF16)
            fko = Ring([al("fko%d" % i, [128, 512]) for i in range(2)])
            fvo = Ring([al("fvo%d" % i, [128, 512]) for i in range(2)])
            lfo = Ring([al("lfo%d" % i, [128, 8]) for i in range(2)])
            nlfo = al("nlfo", [128, 4, 8])
            ncT = al("ncT", [8, 512])
            chi = al("chi", [8, 512], BF16)
            chif = al("chif", [8, 512])
            clo = al("clo", [8, 512], BF16)
            ebr = Ring([al("eb%d" % i, [64, 512]) for i in range(2)])
            enbr = Ring([al("enb%d" % i, [64, 512]) for i in range(2)])
            qdr = Ring([al("qd%d" % i, [64, 4, 128], BF16) for i in range(2)])
            kddr = Ring([al("kd%d" % i, [64, 4, 128], BF16) for i in range(2)])
            Amr = Ring([al("Am%d" % i, [128, 4, 128], BF16) for i in range(2)])
            Sbr = Ring([al("Sb%d" % i, [64, 512], BF16) for i in range(4)])
            rtr = Ring([al("rt%d" % i, [128, 512]) for i in range(1)])
            onr = Ring([al("on%d" % i, [128, 512]) for i in range(1)])

            for si, (tok0, W) in enumerate(SLOTS):
                nt = W // 128
                tm0 = tok0 // 128
                XT = xt_get(NG + si)
                if si < 8:
                    K.op(K.dve, lambda si=si: V_.tensor_copy(S[:, :], Acc[si][:, :]), [Acc[si]], [S])
                else:
                    K.dma(K.sp, S[:, :], s0_d[si - 8], writes=[S])
                for (Wt, bc0, dst, scale) in ((Wfq, 0, Q_s, 0.125), (Wfk, 4, oK_s, 1.0)):
                    qst = Qst.next()
                    for pr in range(4):
                        p = nb()
                        K.mm(p, p[:, 0:W], [(Wt[:, kc, pr * 128:(pr + 1) * 128], XT[:, kc, 0:W]) for kc in range(8)], reads=[Wt, XT])
                        K.op(K.dve, lambda p=p, pr=pr, qst=qst, bc0=bc0, scale=scale: V_.tensor_scalar(
                            out=qst[:, pr, 0:W], in0=p[:, 0:W], scalar1=bcols[:, bc0 + pr:bc0 + pr + 1], scalar2=scale, op0=ALU.add, op1=ALU.mult), [p, bcols], [qst])
                    for hh in range(2):
                        K.dma(K.sp, dst[:, 0:64, tok0:tok0 + W].rearrange("(pr hh) d t -> hh d pr t", hh=2)[hh], qst[hh * 64:(hh + 1) * 64, :, 0:W], reads=[qst])
                for h in range(4):
                    p = nb()
                    K.mm(p, p[0:64, 0:W], [(Wgq[:, kc, h * 64:(h + 1) * 64], XT[:, kc, 0:W]) for kc in range(8)], reads=[Wgq, XT])
                    K.op(K.dve, lambda p=p, h=h: V_.tensor_scalar(out=gqT[:, h, 0:W], in0=p[0:64, 0:W], scalar1=bcols[0:64, 8 + h:9 + h], scalar2=0.125,
                                                                  op0=ALU.add, op1=ALU.mult), [p, bcols], [gqT])
                    p = nb()
                    K.mm(p, p[0:64, 0:W], [(Wgkff[:, kc, h * 64:(h + 1) * 64], XT[:, kc, 0:W]) for kc in range(8)], reads=[Wgkff, XT])
                    K.op(K.act, lambda p=p, h=h: A_.activation(gkT[:, h, 0:W], p[0:64, 0:W], AF.Identity, bias=bcols[0:64, 12 + h:13 + h], scale=1.0), [p, bcols], [gkT])
                    p = nb()
                    K.mm(p, p[:, 0:W], [(Wgr[:, kc, h * 128:(h + 1) * 128], XT[:, kc, 0:W]) for kc in range(8)], reads=[Wgr, XT])
                    K.op(K.act, lambda p=p, h=h: A_.activation(grT[:, h, 0:W], p[:, 0:W], AF.Silu, bias=bcols[:, 17 + h:18 + h], scale=1.0), [p, bcols], [grT])
                gaT = ga_T(XT, W)
                ovst = oVst.next()
                def proj_own(m, out):
                    tm = tm0 + m
                    r0 = tm * 128

                    def a():
                        pA = tok_major(XT, m * 128, Wfk, 512)
                        fk_t = fko.next()
                        K.op(K.dve, lambda: V_.tensor_tensor(fk_t[:, :], pA[:, :], bias[:, 0:512], ALU.add), [pA, bias], [fk_t])
                        K.dma(K.sp, fk_o[r0:r0 + 128, :], fk_t[:, :], reads=[fk_t])

                    def b():
                        pB = tok_major(XT, m * 128, Wfv, 512)
                        fv_t = fvo.next()
                        K.op(K.dve, lambda: V_.tensor_tensor(fv_t[:, :], pB[:, :], bias[:, 512:1024], ALU.add), [pB, bias], [fv_t])
                        K.dma(K.sp, fv_o[r0:r0 + 128, :], fv_t[:, :], reads=[fv_t])
                        K.op(K.pool, lambda: G_.tensor_copy(ovst[:, :, m, 0:64], fv_t[:, :].rearrange("p (h c) -> p h c", h=8)), [fv_t], [ovst])

                    def c():
                        pC = tok_major(XT, m * 128, Wgv, 512)
                        gv_bf = gvr.next()
                        K.op(K.dve, lambda: V_.tensor_tensor(gv_bf[:, :], pC[:, :], bias[:, 1024:1536], ALU.add), [pC, bias], [gv_bf])
                        out["gv"] = gv_bf

                    def d():
                        pD = tok_major(XT, m * 128, Wgkff, 264)
                        gtmp = tmp256.next()
                        K.op(K.dve, lambda: V_.tensor_tensor(gtmp[:, :], pD[:, 0:256], bias[:, 1536:1792], ALU.add), [pD, bias], [gtmp])
                        gk_bf = gkr.next()
                        K.op(K.dve, lambda: V_.tensor_scalar(out=gk_bf[:, :], in0=gtmp[:, :], scalar1=valid[:, tm:tm + 1], scalar2=None, op0=ALU.mult),
                             [gtmp, valid], [gk_bf])
                        out["gk"] = gk_bf
                        ffv = ffr.next()
                        K.op(K.dve, lambda: V_.tensor_tensor(ffv[:, :], pD[:, 256:264], bias[:, 1792:1800], ALU.add), [pD, bias], [ffv])
                        nlf_from(ffv, nlfo[:, m, :], nlfo, t8r.next())
                        lf_t = lfo.next()
                        K.op(K.dve, lambda: V_.tensor_scalar(out=lf_t[:, :], in0=nlfo[:, m, :], scalar1=-1.0, scalar2=None, op0=ALU.mult), [nlfo], [lf_t])
                        K.dma(K.sp, lf_o[r0:r0 + 128, :], lf_t[:, :], reads=[lf_t])

                    return [a, b, c, d]

                def chain_own(m, out):
                    tm = tm0 + m
                    ctx = {}

                    def c1():
                        p = nb()
                        K.mm(p, p[:, 0:256], [(gaT[0:16, m * 128:(m + 1) * 128], Wa2[0:16, :])], reads=[gaT, Wa2])
                        z = tmp256.next()
                        K.op(K.dve, lambda: V_.tensor_tensor(z[:, :], p[:, 0:256], ba2[:, :], ALU.add), [p, ba2], [z])
                        K.op(K.act, lambda: A_.activation(z[:, :], z[:, :], AF.Exp, scale=-1.0), [z], [z])
                        nla = nlar.next()
                        K.op(K.act, lambda: A_.activation(nla[:, :], z[:, :], AF.Ln, bias=1.0, scale=1.0), [z], [nla])
                        K.op(K.dve, lambda: V_.tensor_scalar(out=nla[:, :], in0=nla[:, :], scalar1=valid[:, tm:tm + 1], scalar2=None, op0=ALU.mult), [nla, valid], [nla])
                        ctx["nla"] = nla

                    def c2():
                        nla = ctx["nla"]
                        gk_bf = out["gk"]
                        p2 = nb()
                        K.mm(p2, p2[:, 0:256], [(trirev128_32, nla[:, :])], reads=[cst32, nla])
                        p3 = nb()
                        for h in range(4):
                            K.mm(p3, p3[0:64, h:h + 1], [(nla[:, h * 64:(h + 1) * 64], ones32[:, 0:1])], reads=[nla, cst32])
                        pT = nb()
                        for h in range(4):
                            K.mm(pT, pT[0:64, h * 128:(h + 1) * 128], [(nla[:, h * 64:(h + 1) * 64], triU32)], reads=[nla, cst32])
                        er = err.next()
                        K.op(K.act, lambda: A_.activation(er[:, :], p2[:, 0:256], AF.Exp, scale=-1.0 / 16.0), [p2], [er])
                        ebl = eblr.next()
                        K.op(K.act, lambda: A_.activation(ebl[:, 0:4], p3[0:64, 0:4], AF.Exp, scale=-1.0 / 16.0), [p3], [ebl])
                        eb = ebr.next()
                        enb = enbr.next()
                        K.op(K.act, lambda: A_.activation(eb[:, :], pT[0:64, :], AF.Exp, scale=-1.0 / 16.0), [pT], [eb])
                        K.op(K.act, lambda: A_.activation(enb[:, :], pT[0:64, :], AF.Exp, scale=1.0 / 16.0), [pT], [enb])
                        kdec = kdr.next()
                        K.op(K.pool, lambda: G_.tensor_tensor(kdec[:, :], gk_bf[:, :], er[:, :], ALU.mult), [gk_bf, er], [kdec])
                        qd = qdr.next()
                        kd = kddr.next()
                        K.op(K.dve, lambda: V_.tensor_tensor(qd[:, :, :], gqT[:, :, m * 128:(m + 1) * 128], eb[:, :].rearrange("p (h t) -> p h t", h=4), ALU.mult),
                             [gqT, eb], [qd])
                        K.op(K.dve, lambda: V_.tensor_tensor(kd[:, :, :], gkT[:, :, m * 128:(m + 1) * 128], enb[:, :].rearrange("p (h t) -> p h t", h=4), ALU.mult),
                             [gkT, enb], [kd])
                        ctx.update(kdec=kdec, ebl=ebl, qd=qd, kd=kd)

                    def c3():
                        qd, kd = ctx["qd"], ctx["kd"]
                        pA2 = nb()
                        for h in range(4):
                            K.mm(pA2, pA2[:, h * 128:(h + 1) * 128], [(kd[:, h, :], qd[:, h, :])], reads=[kd, qd])
                        Am = Amr.next()
                        K.op(K.dve, lambda: V_.tensor_tensor(Am[:, :, :], pA2[:, :].rearrange("p (h t) -> p h t", h=4),
                                                             triU32.unsqueeze(1).to_broadcast([128, 4, 128]), ALU.mult), [pA2, cst32], [Am])
                        Sb0 = Sbr.next()
                        K.op(K.dve, lambda: V_.tensor_copy(Sb0[:, :], S[:, :]), [S], [Sb0])
                        kdec, ebl, gv_bf = ctx["kdec"], ctx["ebl"], out["gv"]
                        p = nb()
                        for h in range(4):
                            K.mm(p, p[0:64, h * 128:(h + 1) * 128], [(kdec[:, h * 64:(h + 1) * 64], gv_bf[:, h * 128:(h + 1) * 128])], reads=[kdec, gv_bf])
                        S3 = S[:, :].rearrange("p (h v) -> p h v", h=4)
                        K.op(K.dve, lambda: V_.tensor_tensor(S3, S3, ebl[:, 0:4].unsqueeze(2).to_broadcast([64, 4, 128]), ALU.mult), [S, ebl], [S])
                        K.op(K.dve, lambda: V_.tensor_tensor(S[:, :], S[:, :], p[0:64, :], ALU.add), [S, p], [S])
                        ctx.update(Am=Am, Sb0=Sb0)

                    def c4():
                        qd, Am, Sb0, gv_bf = ctx["qd"], ctx["Am"], ctx["Sb0"], out["gv"]
                        pO = nb()
                        for h in range(4):
                            K.mm(pO, pO[:, h * 128:(h + 1) * 128], [(gv_bf[:, h * 128:(h + 1) * 128], Am[:, h, :])], start=True, stop=False, reads=[gv_bf, Am])
                            K.mm(pO, pO[:, h * 128:(h + 1) * 128], [(Sb0[:, h * 128:(h + 1) * 128], qd[:, h, :])], start=False, stop=True, reads=[Sb0, qd])
                        K.op(K.act, lambda: A_.copy(oT[:, :, m * 128:(m + 1) * 128], pO[:, :].rearrange("p (h t) -> p h t", h=4)), [pO], [oT])

                    return [c1, c2, c3, c4]

                outs = [dict() for _ in range(nt)]
                for f_ in proj_own(0, outs[0]):
                    f_()
                for m in range(1, nt):
                    ps_ = proj_own(m, outs[m])
                    cs_ = chain_own(m - 1, outs[m - 1])
                    for k_ in range(4):
                        ps_[k_]()
                        cs_[k_]()
                for f_ in chain_own(nt - 1, outs[nt - 1]):
                    f_()
                if si >= 8:
                    K.dma(K.sp, gs_o[si - 8], S[:, :], reads=[S])
                K.dma(K.sp, oV_s.rearrange("h p j c -> p h j c")[:, :, tm0:tm0 + nt, :], ovst[:, :, 0:nt, :], reads=[ovst])
                for m in range(nt):
                    p = nb()
                    K.mm(p, p[:, 0:8], [((triU32 if m2 == m else ones32), nlfo[:, m2, :]) for m2 in range(m + 1)], reads=[cst32, nlfo])
                    K.op(K.dve, lambda p=p, m=m: V_.tensor_copy(negco[:, (tm0 + m) * 8:(tm0 + m + 1) * 8], p[:, 0:8]), [p], [negco])
                    p = nb()
                    K.mm(p, p[0:8, 0:128], [(nlfo[:, m2, :], (triU32 if m2 == m else ones32)) for m2 in range(m + 1)], reads=[cst32, nlfo])
                    K.op(K.dve, lambda p=p, m=m: V_.tensor_copy(ncT[:, m * 128:(m + 1) * 128], p[0:8, 0:128]), [p], [ncT])
                K.op(K.dve, lambda: V_.tensor_scalar(out=chi[:, 0:W], in0=ncT[:, 0:W], scalar1=-1.0, scalar2=None, op0=ALU.mult), [ncT], [chi])
                K.op(K.dve, lambda: V_.tensor_copy(chif[:, 0:W], chi[:, 0:W]), [chi], [chif])
                K.op(K.dve, lambda: V_.scalar_tensor_tensor(out=clo[:, 0:W], in0=ncT[:, 0:W], scalar=-1.0, in1=chif[:, 0:W], op0=ALU.mult, op1=ALU.subtract), [ncT, chif], [clo])
                K.dma(K.sp, Q_s[:, 64, tok0:tok0 + W], chi[:, 0:W], reads=[chi])
                K.dma(K.sp, Q_s[:, 65, tok0:tok0 + W], clo[:, 0:W], reads=[clo])
                K.dma(K.sp, Q_s[:, 66, tok0:tok0 + W], onesq[:, 0:W], reads=[onesq])
                K.dma(K.sp, Q_s[:, 67, tok0:tok0 + W], onesq[:, 0:W], reads=[onesq])
                K.op(K.act, lambda: A_.activation(sq[:, :, 0:W], oT[:, :, 0:W], AF.Square), [oT], [sq])
                for h in range(4):
                    p = nb()
                    K.mm(p, p[:, 0:W], [(onesbf, sq[:, h, 0:W])], reads=[cstbf, sq])
                    rt = rtr.next()
                    K.op(K.act, lambda p=p, rt=rt: A_.activation(rt[:, 0:W], p[:, 0:W], AF.Ln, bias=cst32[:, 770:771], scale=1.0 / 128.0), [p, cst32], [rt])
                    K.op(K.act, lambda rt=rt: A_.activation(rt[:, 0:W], rt[:, 0:W], AF.Exp, scale=-0.5), [rt], [rt])
                    on = onr.next()
                    K.op(K.dve, lambda on=on, rt=rt, h=h: V_.tensor_tensor(on[:, 0:W], oT[:, h, 0:W], rt[:, 0:W], ALU.mult), [oT, rt], [on])
                    K.op(K.dve, lambda on=on, h=h: V_.scalar_tensor_tensor(out=goT[:, h, 0:W], in0=on[:, 0:W], scalar=gcol[:, h:h + 1], in1=grT[:, h, 0:W],
                                                                           op0=ALU.mult, op1=ALU.mult), [on, gcol, grT], [goT])
                K.dma(K.sp, go_s.rearrange("h v t -> v h t")[:, :, tok0:tok0 + W], goT[:, :, 0:W], reads=[goT])
            K.barrier()

        with ExitStack() as p2b:
            def al(name, shape, dt=F32):
                return T(p2b.enter_context(sb(name, shape, dt)))

            KA_raw = p2b.enter_context(sb("KA", [128, NSEQ], BF16))
            VA_raw = p2b.enter_context(sb("VA", [128, 128, 128], BF16))
            KAc = [T(KA_raw[:, c * 4096:(c + 1) * 4096]) for c in range(4)]
            VAc = [T(VA_raw[:, c * 32:(c + 1) * 32, :]) for c in range(4)]
            KAs = Ring([al("KAs%d" % i, [128, 2048], BF16) for i in range(2)])
            VAs = Ring([al("VAs%d" % i, [128, 16, 128], BF16) for i in range(2)])
            QAr = [al("QA%d" % i, [128, NOWN], BF16) for i in range(2)]
            oKAr = [al("oKA%d" % i, [128, NOWN], BF16) for i in range(2)]
            oVAr = [al("oVA%d" % i, [128, NTO, 128], BF16) for i in range(2)]
            for QA in QAr[0:1]:
                K.op(K.pool, lambda QA=QA: G_.memset(QA[64:128, :], 0.0), [], [QA])
            for t_ in reversed(KAc):
                K.op(K.dve, lambda t_=t_: V_.memset(t_[64:128, :], 0.0), [], [t_])
                K.op(K.dve, lambda t_=t_: V_.memset(t_[64:66, :], 1.0), [], [t_])
            for oKA in oKAr[0:1]:
                K.op(K.pool, lambda oKA=oKA: G_.memset(oKA[64:128, :], 0.0), [], [oKA])
                K.op(K.pool, lambda oKA=oKA: G_.memset(oKA[64:66, :], 1.0), [], [oKA])
            for QA in QAr[1:2]:
                K.op(K.pool, lambda QA=QA: G_.memset(QA[64:128, :], 0.0), [], [QA])
            for oKA in oKAr[1:2]:
                K.op(K.pool, lambda oKA=oKA: G_.memset(oKA[64:128, :], 0.0), [], [oKA])
                K.op(K.pool, lambda oKA=oKA: G_.memset(oKA[64:66, :], 1.0), [], [oKA])
            for t_ in KAs.t:
                K.op(K.pool, lambda t_=t_: G_.memset(t_[64:128, :], 0.0), [], [t_])
                K.op(K.pool, lambda t_=t_: G_.memset(t_[64:66, :], 1.0), [], [t_])
            for t_ in VAs.t:
                K.op(K.pool, lambda t_=t_: G_.memset(t_[:, :, :], 1.0), [], [t_])
            biar = Ring([al("bia%d" % i, [128, 128]) for i in range(10)])
            Pr = Ring([al("P%d" % i, [128, 1024], BF16) for i in range(3)])
            otr = Ring([al("ot%d" % i, [128, 512]) for i in range(2)])
            dshr = Ring([al("dsh%d" % i, [64, 512]) for i in range(2)])
            for_ = Ring([al("fo%d" % i, [64, 512], BF16) for i in range(2)])
            sbank = Ring(psD)

            def make_bias(si, h, n, kind):
                tok0, W = SLOTS[si]
                nt = W // 128
                tm0 = tok0 // 128
                bia = biar.next()
                if kind == "p":
                    K.op(K.dve, lambda: V_.scalar_tensor_tensor(out=bia[:, 0:n], in0=negc[:, :].rearrange("p (j h) -> p h j", h=8)[:, h, 0:n],
                                                                scalar=Csel[:, si * 8 + h:si * 8 + h + 1], in1=kflag[:, si * 128:si * 128 + n],
                                                                op0=ALU.subtract, op1=ALU.add), [negc, Csel, kflag], [bia])
                else:
                    st = si - 8
                    K.op(K.dve, lambda: V_.tensor_scalar(out=bia[:, 0:n], in0=negcs[:, :].rearrange("p (s j h) -> p s h j", s=2, h=8)[:, st, h, 0:n],
                                                         scalar1=ntots[:, st * 8 + h:st * 8 + h + 1], scalar2=None, op0=ALU.subtract), [negcs, ntots], [bia])
                return bia

            def attend(si, h, n, kind, kblk, vblk, bia, QA, oKA, oVA):
                tok0, W = SLOTS[si]
                nt = W // 128
                tm0 = tok0 // 128
                pO = obank.next()
                if kind == "p":
                    items = [("pp", i_) for i_ in reversed(range(n // 2))] + [("own", m) for m in range(nt)]
                else:
                    items = [("pre", j) for j in reversed(range(n))] + [("own", m) for m in range(nt)]

                def emit_S(it):
                    kind_, j = it
                    p = sbank.next()
                    if kind_ == "pp":
                        for u in range(2):
                            kt_, kap_ = kblk(2 * j + u)
                            K.mm(p, p[:, u * 512:u * 512 + 512], [(kap_, QA[:, tok0:tok0 + 512])], reads=[kt_, QA])
                    elif kind_ == "pre":
                        kt_, kap_ = kblk(j)
                        K.mm(p, p[:, 0:W], [(kap_, QA[:, tok0:tok0 + W])], reads=[kt_, QA])
                    else:
                        q0 = j * 128
                        K.mm(p, p[:, q0:W], [(oKA[:, tok0 + q0:tok0 + q0 + 128], QA[:, tok0 + q0:tok0 + W])], start=True, stop=False, reads=[oKA, QA])
                        K.mm(p, p[:, q0:q0 + 128], [(identbf, masknegbf)], start=False, stop=True, reads=[cstbf])
                    return p

                def emit_E(it, p):
                    kind_, j = it
                    P = Pr.next()
                    if kind_ == "pp":
                        K.op(K.act, lambda: A_.activation(P[:, 0:1024], p[:, 0:1024], AF.Exp, bias=bia[:, 2 * j:2 * j + 1], scale=1.0), [p, bia], [P])
                    elif kind_ == "pre":
                        K.op(K.act, lambda: A_.activation(P[:, 0:W], p[:, 0:W], AF.Exp, bias=bia[:, j:j + 1], scale=1.0), [p, bia], [P])
                    else:
                        q0 = j * 128
                        K.op(K.act, lambda: A_.activation(P[:, q0:W], p[:, q0:W], AF.Exp, bias=negco[:, (tm0 + j) * 8 + h:(tm0 + j) * 8 + h + 1], scale=1.0), [p, negco], [P])
                    return P

                def emit_PV(it, P, first, last):
                    kind_, j = it
                    if kind_ == "pp":
                        for u in range(2):
                            vt_, vap_ = vblk(2 * j + u)
                            K.mm(pO, pO[:, 0:512], [(vap_, P[:, u * 512:u * 512 + 512])], start=(first and u == 0), stop=(last and u == 1), reads=[vt_, P])
                    elif kind_ == "pre":
                        vt_, vap_ = vblk(j)
                        K.mm(pO, pO[:, 0:W], [(vap_, P[:, 0:W])], start=first, stop=last, reads=[vt_, P])
                    else:
                        q0 = j * 128
                        K.mm(pO, pO[:, q0:W], [(oVA[:, tm0 + j, :], P[:, q0:W])], start=first, stop=last, reads=[oVA, P])

                LA = 2
                pend = [emit_S(items[i_]) for i_ in range(min(LA, len(items)))]
                for i, it in enumerate(items):
                    p = pend.pop(0)
                    if i + LA < len(items):
                        pend.append(emit_S(items[i + LA]))
                    P = emit_E(it, p)
                    emit_PV(it, P, i == 0, i == len(items) - 1)
                ot = otr.next()
                K.op(K.dve, lambda: V_.tensor_copy(ot[:, 0:W], pO[:, 0:W]), [pO], [ot])
                dsh = dshr.next()
                K.dma(K.sp, dsh[:, 0:W], ot[64:128, 0:W], reads=[ot], writes=[dsh])
                K.op(K.dve, lambda: V_.reciprocal(dsh[:, 0:W], dsh[:, 0:W]), [dsh], [dsh])
                fo = for_.next()
                K.op(K.dve, lambda: V_.tensor_tensor(fo[:, 0:W], ot[0:64, 0:W], dsh[:, 0:W], ALU.mult), [ot, dsh], [fo])
                K.dma(K.sp, fo_s[h * 64:(h + 1) * 64, tok0:tok0 + W], fo[:, 0:W], reads=[fo])

            def load_q(h_):
                QA, oKA, oVA = QAr[h_ % 2], oKAr[h_ % 2], oVAr[h_ % 2]
                K.dma(K.sp, QA[0:68, :], Q_s[h_], writes=[QA])
                K.dma(K.sp, oKA[0:64, :], oK_s[h_], writes=[oKA])
                K.dma(K.sp, oVA[:, :, :], oV_s[h_], writes=[oVA])

            def load_kv_chunk(h_, c):
                K.dma(K.sp, KAc[c][0:64, :], KT_s[h_ * 64:(h_ + 1) * 64, c * 4096:(c + 1) * 4096], writes=[KAc[c]])
                K.dma(K.sp, KAc[c][66:68, :], KX_s[h_, :, c * 4096:(c + 1) * 4096], writes=[KAc[c]])
                K.dma(K.sp, VAc[c][:, :, :], V_s[h_, :, c * 32:(c + 1) * 32, :], writes=[VAc[c]])

            def load_kv(h_):
                for c in (3, 2, 1, 0):
                    load_kv_chunk(h_, c)

            load_q(0)
            load_kv(0)
            for h in range(8):
                QA, oKA, oVA = QAr[h % 2], oKAr[h % 2], oVAr[h % 2]
                if h + 1 < 8:
                    load_q(h + 1)
                sk = []
                for st in range(2):
                    ks_, vs_ = KAs.next(), VAs.next()
                    K.dma(K.pool, ks_[0:64, :], kTc_d[st, h * 64:(h + 1) * 64, :], writes=[ks_])
                    K.dma(K.pool, vs_[:, :, 0:64], vc_d[st].rearrange("(j p) (h c) -> p j h c", p=128, h=8)[:, :, h, :], writes=[vs_])
                    sk.append((ks_, vs_))
                if h == 0:
                    bias_p = [make_bias(si, 0, NPRE[si], "p") for si in range(8)]
                bias_t = bias_p + [make_bias(8 + st, h, 16, "s") for st in range(2)]
                free_after = {6: 3, 4: 2, 2: 1, 0: 0}
                for si in range(7, -1, -1):
                    attend(si, h, NPRE[si], "p",
                           lambda j: (KAc[j // 32], KAc[j // 32][:, (j % 32) * 128:(j % 32 + 1) * 128]),
                           lambda j: (VAc[j // 32], VAc[j // 32][:, j % 32, :]), bias_t[si], QA, oKA, oVA)
                    if h + 1 < 8 and si in free_after:
                        load_kv_chunk(h + 1, free_after[si])
                if h + 1 < 8:
                    bias_p = [make_bias(si, h + 1, NPRE[si], "p") for si in range(8)]
                for st in range(2):
                    ks_, vs_ = sk[st]
                    attend(8 + st, h, 16, "s", lambda j, ks_=ks_: (ks_, ks_[:, j * 128:(j + 1) * 128]), lambda j, vs_=vs_: (vs_, vs_[:, j, :]),
                           bias_t[8 + st], QA, oKA, oVA)
            K.barrier()
        mid.close()

        with ExitStack() as p3:
            def al(name, shape, dt=F32):
                return T(p3.enter_context(sb(name, shape, dt)))

            Wpg = al("Wpg", [128, 4, D], BF16)
            Wpf = al("Wpf", [128, 4, D], BF16)
            Wout = al("Wout", [128, 8, D], BF16)
            lnp = al("lnp", [128, 4 * D])
            K.dma(K.sp, Wpg[:, :, :], wpg_s.rearrange("(h p) e -> p h e", p=128), writes=[Wpg])
            K.dma(K.sp, Wpf[:, :, :], wpf_s.rearrange("(h p) e -> p h e", p=128), writes=[Wpf])
            K.dma(K.sp, Wout[:, :, :], wout_s.rearrange("(kc p) e -> p kc e", p=128), writes=[Wout])
            K.dma(K.sp, lnp[:, :], ln_d[:, :], writes=[lnp])
            wpool = Ring([al("wp%d" % i, [128, 8, 512], BF16) for i in range(4)])
            Wdr = Ring([al("Wd%d" % i, [128, NFC, 256], BF16) for i in range(2)])
            XT = al("XT3", [128, 8, 512], BF16)
            xin = Ring([al("xin%d" % i, [128, D]) for i in range(1)])
            foT = al("foT", [128, 4, 512], BF16)
            goT3 = al("goT3", [128, 4, 512], BF16)
            sgr = Ring([al("sg%d" % i, [128, 512], BF16) for i in range(4)])
            t1r = Ring([al("t1_%d" % i, [128, 512]) for i in range(1)])
            t2r = Ring([al("t2_%d" % i, [128, 512]) for i in range(1)])
            mT = al("mT", [128, 8, 512], BF16)
            r1r = Ring([al("r1_%d" % i, [128, D]) for i in range(2)])
            junk = al("junk", [128, D], BF16)
            x1 = al("x1", [128, 4, D])
            x1bf = Ring([al("x1bf%d" % i, [128, D], BF16) for i in range(2)])
            x1T = al("x1T", [128, 8, 512], BF16)
            hT = al("hT", [128, NFC, 512], BF16)
            st_r = Ring([al("st%d" % i, [128, 4]) for i in range(8)])

            def ln_stages(src_ap, src_t, dst_ap, dst_t, g_ap, b_ap):
                stt = st_r.next()

                def A():
                    K.op(K.dve, lambda: V_.memset(stt[:, :], 0.0), [], [stt])
                    K.op(K.dve, lambda: V_.reduce_sum(stt[:, 0:1], src_ap, axis=mybir.AxisListType.X), [src_t], [stt])
                    K.op(K.dve, lambda: V_.tensor_scalar(out=stt[:, 1:2], in0=stt[:, 0:1], scalar1=-1.0 / D, scalar2=None, op0=ALU.mult), [stt], [stt])

                def B():
                    K.op(K.act, lambda: A_.activation(junk[:, :], src_ap, AF.Square, bias=stt[:, 1:2], scale=1.0, accum_out=stt[:, 2:3]), [src_t, stt], [junk, stt])
                    K.op(K.act, lambda: A_.activation(stt[:, 3:4], stt[:, 2:3], AF.Sqrt, bias=cst32[:, 770:771], scale=1.0 / D), [stt, cst32], [stt])

                def C():
                    K.op(K.dve, lambda: V_.reciprocal(stt[:, 3:4], stt[:, 3:4]), [stt], [stt])
                    K.op(K.dve, lambda: V_.scalar_tensor_tensor(out=dst_ap, in0=src_ap, scalar=stt[:, 1:2], in1=g_ap, op0=ALU.add, op1=ALU.mult), [src_t, stt, lnp], [dst_t])
                    K.op(K.dve, lambda: V_.scalar_tensor_tensor(out=dst_ap, in0=dst_ap, scalar=stt[:, 3:4], in1=b_ap, op0=ALU.mult, op1=ALU.add), [dst_t, stt, lnp], [dst_t])

                return A, B, C

            ln2_pending = []

            def emit_ln2_one():
                if ln2_pending:
                    m_, r0_ = ln2_pending.pop(0)
                    A, B, C = ln_stages(x1[:, m_, :], x1, x1[:, m_, :], x1, lnp[:, 2 * D:3 * D], lnp[:, 3 * D:4 * D])
                    A()
                    B()
                    C()
                    K.dma(K.sp, y_o[r0_:r0_ + 128, :], x1[:, m_, :], reads=[x1])

            P3SLOTS = [(s_ * 512, 512) for s_ in range(8)] + [(4096, 256)]
            wz_v = wz_s.rearrange("(kc p) c -> p kc c", p=128)
            wg_v = wg_s.rearrange("(kc p) f -> p kc f", p=128)
            wu_v = wu_s.rearrange("(kc p) f -> p kc f", p=128)
            wd_v = wd_s.rearrange("(fc p) e -> p fc e", p=128)
            wsrc = []
            for _ in P3SLOTS:
                for e4_ in range(2):
                    wsrc.append((wz_v[:, :, e4_ * 512:(e4_ + 1) * 512], 512))
                    wsrc.append((wz_v[:, :, 1024 + e4_ * 512:1024 + (e4_ + 1) * 512], 512))
                for f4_ in range(6):
                    nf_ = 4 if f4_ < 5 else 2
                    wsrc.append((wg_v[:, :, f4_ * 512:f4_ * 512 + nf_ * 128], nf_ * 128))
                    wsrc.append((wu_v[:, :, f4_ * 512:f4_ * 512 + nf_ * 128], nf_ * 128))
            wiss = {}
            wcnt = [0]

            def w_issue(upto):
                for i_ in range(len(wiss), min(upto + 1, len(wsrc))):
                    t_ = wpool.next()
                    src_, n_ = wsrc[i_]
                    K.dma(K.act, t_[:, :, 0:n_], src_, writes=[t_])
                    wiss[i_] = t_

            def w_get():
                i_ = wcnt[0]
                wcnt[0] += 1
                w_issue(i_ + 2)
                return wiss[i_]

            dsrc = [wd_v[:, :, cg_ * 256:(cg_ + 1) * 256] for _ in P3SLOTS for cg_ in range(4)]
            diss = {}
            dcnt = [0]

            def d_issue(upto):
                for i_ in range(len(diss), min(upto + 1, len(dsrc))):
                    t_ = Wdr.next()
                    K.dma(K.act, t_[:, :, :], dsrc[i_], writes=[t_])
                    diss[i_] = t_

            def d_get():
                i_ = dcnt[0]
                dcnt[0] += 1
                d_issue(i_ + 1)
                return diss[i_]
            for si, (tok0, W) in enumerate(P3SLOTS):
                nt = W // 128
                def load_inputs(tok0_, W_):
                    K.dma(K.pool, XT[:, :, 0:W_], xT_own.rearrange("(kc p) t -> p kc t", p=128)[:, :, tok0_:tok0_ + W_], writes=[XT])
                    K.dma(K.sp, foT[:, :, 0:W_], fo_s.rearrange("(pr p) t -> p pr t", p=128)[:, :, tok0_:tok0_ + W_], writes=[foT])
                    K.dma(K.sp, goT3[:, :, 0:W_], go_s.rearrange("h v t -> v h t")[:, :, tok0_:tok0_ + W_], writes=[goT3])

                if si == 0:
                    load_inputs(tok0, W)
                for e4 in range(2):
                    wz = w_get()
                    wf = w_get()
                    for e in range(4):
                        ec = e4 * 4 + e
                        pz = nb()
                        K.mm(pz, pz[:, 0:W], [(wz[:, kc, e * 128:(e + 1) * 128], XT[:, kc, 0:W]) for kc in range(8)], reads=[wz, XT])
                        sg = sgr.next()
                        K.op(K.act, lambda pz=pz, sg=sg, ec=ec: A_.activation(sg[:, 0:W], pz[:, 0:W], AF.Sigmoid, bias=bcols[:, 21 + ec:22 + ec], scale=1.0), [pz, bcols], [sg])
                        pf = nb()
                        K.mm(pf, pf[:, 0:W], [(wf[:, kc, e * 128:(e + 1) * 128], XT[:, kc, 0:W]) for kc in range(8)], reads=[wf, XT])
                        sz = sgr.next()
                        K.op(K.act, lambda pf=pf, sz=sz, ec=ec: A_.activation(sz[:, 0:W], pf[:, 0:W], AF.Sigmoid, bias=bcols[:, 29 + ec:30 + ec], scale=1.0), [pf, bcols], [sz])
                        pg = nb()
                        K.mm(pg, pg[:, 0:W], [(Wpg[:, h, ec * 128:(ec + 1) * 128], goT3[:, h, 0:W]) for h in range(4)], reads=[Wpg, goT3])
                        t1 = t1r.next()
                        K.op(K.dve, lambda pg=pg, t1=t1, sg=sg: V_.tensor_tensor(t1[:, 0:W], pg[:, 0:W], sg[:, 0:W], ALU.mult), [pg, sg], [t1])
                        pff = nb()
                        K.mm(pff, pff[:, 0:W], [(Wpf[:, h, ec * 128:(ec + 1) * 128], foT[:, h, 0:W]) for h in range(4)], reads=[Wpf, foT])
                        t2 = t2r.next()
                        K.op(K.dve, lambda pff=pff, t2=t2, sz=sz: V_.tensor_tensor(t2[:, 0:W], pff[:, 0:W], sz[:, 0:W], ALU.mult), [pff, sz], [t2])
                        K.op(K.dve, lambda t1=t1, t2=t2, ec=ec: V_.tensor_tensor(mT[:, ec, 0:W], t1[:, 0:W], t2[:, 0:W], ALU.add), [t1, t2], [mT])
                        if ec % 2 == 1:
                            emit_ln2_one()
                if si + 1 < len(P3SLOTS):
                    load_inputs(*P3SLOTS[si + 1])
                def transposes(xb, m):
                    for half in range(2):
                        p = nb()
                        for kq in range(4):
                            kc = half * 4 + kq
                            K.mm(p, p[:, kq * 128:(kq + 1) * 128], [(xb[:, kc * 128:(kc + 1) * 128], identbf)], reads=[xb, cstbf])
                        K.op(K.act, lambda: A_.copy(x1T[:, half * 4:(half + 1) * 4, m * 128:(m + 1) * 128],
                                                    p[:, :].rearrange("p (k t) -> p k t", k=4)), [p], [x1T])

                while ln2_pending:
                    emit_ln2_one()
                prevC = None
                prevT = None
                for m in range(nt):
                    r0 = tok0 + m * 128
                    xi = xin.next()
                    K.dma(K.sp, xi[:, :], x_own[r0:r0 + 128, :], writes=[xi])
                    r1 = r1r.next()
                    for cg in range(2):
                        p = nb()
                        K.mm(p, p[:, :], [(mT[:, e, m * 128:(m + 1) * 128], Wout[:, e, cg * 512:(cg + 1) * 512]) for e in range(8)], reads=[mT, Wout])
                        K.op(K.dve, lambda: V_.scalar_tensor_tensor(out=r1[:, cg * 512:(cg + 1) * 512], in0=xi[:, cg * 512:(cg + 1) * 512], scalar=ALPHA,
                                                                    in1=p[:, :], op0=ALU.mult, op1=ALU.add), [p, xi], [r1])
                    A, B, C = ln_stages(r1[:, :], r1, x1[:, m, :], x1, lnp[:, 0:D], lnp[:, D:2 * D])
                    A()
                    if prevC is not None:
                        prevC()
                    B()
                    if prevT is not None:
                        transposes(*prevT)
                        prevT = None
                    if prevC is not None:
                        pm = m - 1
                        xb = x1bf.next()
                        K.op(K.act, lambda: A_.copy(xb[:, :], x1[:, pm, :]), [x1], [xb])
                        prevT = (xb, pm)
                    prevC = C
                prevC()
                if prevT is not None:
                    transposes(*prevT)
                xb = x1bf.next()
                K.op(K.act, lambda: A_.copy(xb[:, :], x1[:, nt - 1, :]), [x1], [xb])
                transposes(xb, nt - 1)
                for f4 in range(6):
                    nf = 4 if f4 < 5 else 2
                    if f4 == 0:
                        d_issue(dcnt[0] + 1)
                    wg = w_get()
                    wu = w_get()
                    for f in range(nf):
                        fc = f4 * 4 + f
                        pg = nb()
                        K.mm(pg, pg[:, 0:W], [(wg[:, kc, f * 128:(f + 1) * 128], x1T[:, kc, 0:W]) for kc in range(8)], reads=[wg, x1T])
                        sg = sgr.next()
                        K.op(K.act, lambda pg=pg, sg=sg: A_.activation(sg[:, 0:W], pg[:, 0:W], AF.Silu), [pg], [sg])
                        pu = nb()
                        K.mm(pu, pu[:, 0:W], [(wu[:, kc, f * 128:(f + 1) * 128], x1T[:, kc, 0:W]) for kc in range(8)], reads=[wu, x1T])
                        K.op(K.dve, lambda pu=pu, sg=sg, fc=fc: V_.tensor_tensor(hT[:, fc, 0:W], pu[:, 0:W], sg[:, 0:W], ALU.mult), [pu, sg], [hT])
                for cg in range(4):
                    Wd = d_get()
                    for m in range(nt):
                        p = nb()
                        K.mm(p, p[:, 0:256], [(hT[:, fc, m * 128:(m + 1) * 128], Wd[:, fc, :]) for fc in range(NFC)], reads=[hT, Wd])
                        K.op(K.dve, lambda p=p, cg=cg, m=m: V_.scalar_tensor_tensor(out=x1[:, m, cg * 256:(cg + 1) * 256], in0=x1[:, m, cg * 256:(cg + 1) * 256], scalar=ALPHA,
                                                                                    in1=p[:, 0:256], op0=ALU.mult, op1=ALU.add), [p, x1], [x1])
                for m in range(nt):
                    ln2_pending.append((m, tok0 + m * 128))
            while ln2_pending:
                emit_ln2_one()
            K.finish()
    return nc


def _host_inputs(inp):
    f = np.float32
    w_in = np.ascontiguousarray(inp["w_in"][0], f)
    b_in = np.asarray(inp["b_in"][0], f)
    bias_bc = np.ascontiguousarray(np.broadcast_to(b_in[None, :], (128, DIN)))
    bcols = np.zeros((128, 48), f)
    for pr in range(4):
        bcols[:, pr] = b_in[C_FQ + pr * 128:C_FQ + (pr + 1) * 128]
        bcols[:, 4 + pr] = b_in[C_FK + pr * 128:C_FK + (pr + 1) * 128]
        bcols[0:64, 8 + pr] = b_in[C_GQ + pr * 64:C_GQ + (pr + 1) * 64]
        bcols[0:64, 12 + pr] = b_in[C_GK + pr * 64:C_GK + (pr + 1) * 64]
        bcols[:, 17 + pr] = b_in[C_GR + pr * 128:C_GR + (pr + 1) * 128]
        bcols[:, 37 + pr] = inp["gla_norm_g"][0][pr * 128:(pr + 1) * 128]
    bcols[0:16, 16] = b_in[C_GA:C_GA + 16]
    for e in range(8):
        bcols[:, 21 + e] = b_in[C_ZG + e * 128:C_ZG + (e + 1) * 128]
        bcols[:, 29 + e] = b_in[C_ZF + e * 128:C_ZF + (e + 1) * 128]
    cst = np.zeros((128, NCST), f)
    i = np.arange(128)
    cst[:, 0:128] = np.eye(128)
    cst[:, 128:256] = (i[:, None] <= i[None, :])
    cst[:, 256:384] = 1.0
    same = (i[:, None] // 64) == (i[None, :] // 64)
    cst[:, 384:512] = same & (i[:, None] <= i[None, :])
    cst[:, 512:640] = same & (i[:, None] > i[None, :])
    cst[:, 640:768] = np.where(i[:, None] > i[None, :], -30000.0, 0.0)
    cst[:, 768] = (i < 64)
    cst[:, 769] = (i >= 64)
    cst[:, 770] = 1e-5
    cst[:, 772:900] = (i[:, None] > i[None, :])
    bc = lambda v: np.ascontiguousarray(np.broadcast_to(np.asarray(v, f)[None, :], (128, len(v))))
    ln_bc = np.concatenate([bc(inp["ln1_g"][0]), bc(inp["ln1_b"][0]), bc(inp["ln2_g"][0]), bc(inp["ln2_b"][0])], axis=1)
    shared = dict(w_in=w_in, bias_bc=bias_bc, bcols=bcols, wa2=np.ascontiguousarray(inp["w_alpha2"][0], f),
                  ba2_bc=bc(inp["b_alpha2"][0]), wpg=np.ascontiguousarray(inp["w_proj_gla"][0], f),
                  wpf=np.ascontiguousarray(inp["w_proj_fox"][0], f), wout=np.ascontiguousarray(inp["w_out"][0], f),
                  wg=np.ascontiguousarray(inp["w_ffn_gate"][0], f), wu=np.ascontiguousarray(inp["w_ffn_up"][0], f),
                  wd=np.ascontiguousarray(inp["w_ffn_down"][0], f), ln_bc=np.ascontiguousarray(ln_bc), cst=cst)
    maps = []
    xp = np.asarray(inp["x_prompt"], f)
    xs = np.asarray(inp["x_sample"], f)
    for core in range(8):
        b, c = core // 4, core % 4
        chunks = own_chunks(c)
        x_own = np.zeros((NOWN, D), f)
        for s, ci in enumerate(chunks):
            x_own[s * 512:(s + 1) * 512] = xp[b, ci * 512:(ci + 1) * 512]
        valid = np.ones((NOWN,), f)
        for st in range(2):
            sidx = core * 2 + st
            x_own[4096 + st * 128:4096 + st * 128 + 32] = xs[sidx]
            valid[4096 + st * 128 + 32:4096 + (st + 1) * 128] = 0.0
        kflag = np.zeros((8, 128), f)
        oneh = np.zeros((8, 32), f)
        for s, ci in enumerate(chunks):
            kflag[s, 4 * ci:] = NEG
            oneh[s, ci] = 1.0
        m = dict(shared)
        m["xT_seq"] = np.ascontiguousarray(xp[b].T)
        m["xT_own"] = np.ascontiguousarray(x_own.T)
        m["x_own"] = x_own
        m["valid"] = np.ascontiguousarray(valid.reshape(NTO, 128).T)
        m["kflag"] = np.ascontiguousarray(np.broadcast_to(kflag.reshape(1, 1024), (128, 1024)))
        m["onehot"] = np.ascontiguousarray(np.broadcast_to(oneh.reshape(1, 256), (128, 256)))
        sl = slice(core * 2, core * 2 + 2)
        m["kTc"] = np.ascontiguousarray(np.asarray(inp["cache_fox_k"][0][sl], f).transpose(0, 2, 3, 1).reshape(2, 512, 2048))
        m["vc"] = np.ascontiguousarray(np.asarray(inp["cache_fox_v"][0][sl], f).reshape(2, 2048, 512))
        m["lfc"] = np.ascontiguousarray(np.asarray(inp["cache_fox_logf"][0][sl], f))
        m["s0"] = np.ascontiguousarray(np.asarray(inp["state_gla"][0][sl], f).transpose(0, 2, 1, 3).reshape(2, 64, 512))
        maps.append(m)
    return maps


_NC = None


def kernel(**inp):
    global _NC
    if _NC is None:
        _NC = build()
    maps = _host_inputs(inp)
    res = run_bass_kernel_spmd(_NC, maps, core_ids=list(range(8)))
    R = res.results
    f = np.float32
    yp = np.zeros((2, NSEQ, D), f)
    fkp = np.zeros((2, NSEQ, 512), f)
    fvp = np.zeros((2, NSEQ, 512), f)
    lfp = np.zeros((2, NSEQ, 8), f)
    ys = np.zeros((16, 32, D), f)
    fks = np.zeros((16, 32, 512), f)
    fvs = np.zeros((16, 32, 512), f)
    lfs = np.zeros((16, 32, 8), f)
    gss = np.zeros((16, 4, 64, 128), f)
    gsp = np.zeros((2, 4, 64, 128), f)
    for core in range(8):
        b, c = core // 4, core % 4
        r = R[core]
        for s, ci in enumerate(own_chunks(c)):
            yp[b, ci * 512:(ci + 1) * 512] = r["y_own"][s * 512:(s + 1) * 512]
            fkp[b, ci * 512:(ci + 1) * 512] = r["fk_own"][s * 512:(s + 1) * 512]
            fvp[b, ci * 512:(ci + 1) * 512] = r["fv_own"][s * 512:(s + 1) * 512]
            lfp[b, ci * 512:(ci + 1) * 512] = r["lf_own"][s * 512:(s + 1) * 512]
        if c == 0:
            gsp[b] = r["gstate_p"].reshape(64, 4, 128).transpose(1, 0, 2)
        for st in range(2):
            sidx = core * 2 + st
            o = 4096 + st * 128
            ys[sidx] = r["y_own"][o:o + 32]
            fks[sidx] = r["fk_own"][o:o + 32]
            fvs[sidx] = r["fv_own"][o:o + 32]
            lfs[sidx] = r["lf_own"][o:o + 32]
            gss[sidx] = r["gstate_s"][st].reshape(64, 4, 128).transpose(1, 0, 2)
    return (yp, ys, gsp[None], fkp.reshape(1, 2, NSEQ, 8, 64), fvp.reshape(1, 2, NSEQ, 8, 64), lfp[None],
            gss[None], fks.reshape(1, 16, 32, 8, 64), fvs.reshape(1, 16, 32, 8, 64), lfs[None])
```

```python
import numpy as np
import concourse.bass as bass
import concourse.mybir as mybir
from concourse.bass_utils import run_bass_kernel_spmd

F32 = mybir.dt.float32
BF16 = mybir.dt.bfloat16
ALU = mybir.AluOpType
AF = mybir.ActivationFunctionType

D = 1024
NSEQ = 16384
DFF = 2816
NFC = 22
C_GQ, C_GK, C_GV, C_GA, C_GR, C_FQ, C_FK, C_FV, C_FF, C_ZG, C_ZF = 0, 256, 512, 1024, 1040, 1552, 2064, 2576, 3088, 3096, 4120
DIN = 5144
ALPHA = float(2.0 ** 0.25)
NOWN = 4096 + 256
NTO = NOWN // 128
SLOTS = [(s * 512, 512) for s in range(8)] + [(4096, 128), (4224, 128)]
NPRE = [12, 28, 44, 60, 76, 92, 108, 124]
NEG = -60000.0
NCST = 900


def own_chunks(c):
    r = []
    for g in range(4):
        r += [8 * g + c, 8 * g + 7 - c]
    return r


class T:
    __slots__ = ("ap", "w", "r")

    def __init__(self, ap):
        self.ap = ap
        self.w = None
        self.r = {}

    def __getitem__(self, k):
        return self.ap[k]


class Eng:
    def __init__(self, K, e, name, is_pe=False):
        self.e = e
        self.name = name
        self.sem = K.nc.alloc_semaphore("sem_" + name)
        self.cnt = 0
        self.waited = {}
        self.is_pe = is_pe
        self.dsems = []
        self.dcnt = []
        self.dnext = 0

    def wait(self, tok):
        sem, val = tok
        if self.waited.get(sem, 0) < val:
            self.e.wait_ge(sem, val)
            self.waited[sem] = val


class Kern:
    def __init__(self, nc, n_dsem=12):
        self.nc = nc
        self.pe = Eng(self, nc.tensor, "pe", True)
        self.act = Eng(self, nc.scalar, "act")
        self.dve = Eng(self, nc.vector, "dve")
        self.pool = Eng(self, nc.gpsimd, "pool")
        self.sp = Eng(self, nc.sync, "sp")
        self.engs = [self.pe, self.act, self.dve, self.pool, self.sp]
        for q in (self.sp, self.pool, self.act):
            for i in range(n_dsem):
                q.dsems.append(nc.alloc_semaphore("dsem_%s_%d" % (q.name, i)))
                q.dcnt.append(0)

    def _deps(self, E, reads, writes):
        for t in reads:
            if t.w is not None and not (t.w[0] is E.sem and E.is_pe):
                E.wait(t.w)
        for t in writes:
            if t.w is not None and not (t.w[0] is E.sem and E.is_pe):
                E.wait(t.w)
            for sem, val in t.r.items():
                if sem is not E.sem:
                    E.wait((sem, val))

    def _done(self, tok, reads, writes):
        for t in reads:
            if t.r.get(tok[0], 0) < tok[1]:
                t.r[tok[0]] = tok[1]
        for t in writes:
            t.w = tok
            t.r = {}

    def op(self, E, fn, reads=(), writes=()):
        self._deps(E, reads, writes)
        ins = fn()
        E.cnt += 1
        ins.then_inc(E.sem, 1)
        self._done((E.sem, E.cnt), reads, writes)

    def dma(self, Q, out_ap, in_ap, reads=(), writes=()):
        self._deps(Q, reads, writes)
        i = Q.dnext
        Q.dnext = (Q.dnext + 1) % len(Q.dsems)
        sem = Q.dsems[i]
        if Q.dcnt[i] > 0:
            Q.wait((sem, Q.dcnt[i]))
        ins = Q.e.dma_start(out=out_ap, in_=in_ap)
        Q.dcnt[i] += 16
        ins.then_inc(sem, 16)
        self._done((sem, Q.dcnt[i]), reads, writes)

    def mm(self, out_t, out_ap, terms, start=True, stop=True, reads=()):
        E = self.pe
        self._deps(E, reads, [out_t])
        n = len(terms)
        ins = None
        for i, (l, r) in enumerate(terms):
            ins = self.nc.tensor.matmul(out_ap, l, r, start=(start and i == 0), stop=(stop and i == n - 1))
        E.cnt += 1
        ins.then_inc(E.sem, 1)
        self._done((E.sem, E.cnt), reads, [out_t])

    def _alltoks(self):
        toks = []
        for E in self.engs:
            if E.cnt > 0:
                toks.append((E.sem, E.cnt))
            for s, c in zip(E.dsems, E.dcnt):
                if c > 0:
                    toks.append((s, c))
        return toks

    def barrier(self):
        toks = self._alltoks()
        for E in self.engs:
            for tok in toks:
                if tok[0] is not E.sem:
                    E.wait(tok)

    def finish(self):
        for tok in self._alltoks():
            if tok[0] is not self.sp.sem:
                self.sp.wait(tok)


class Ring:
    def __init__(self, tiles):
        self.t = tiles
        self.i = 0

    def next(self):
        t = self.t[self.i]
        self.i = (self.i + 1) % len(self.t)
        return t


def build():
    nc = bass.Bass("TRN2", target_bir_lowering=False)
    K = Kern(nc)
    V_ = nc.vector
    A_ = nc.scalar
    G_ = nc.gpsimd

    def din(name, shape):
        return nc.dram_tensor(name, shape, F32, kind="ExternalInput").ap()

    def dout(name, shape):
        return nc.dram_tensor(name, shape, F32, kind="ExternalOutput").ap()

    def dscr(name, shape, dt=BF16):
        return nc.dram_tensor(name, shape, dt, kind="Internal").ap()

    xT_seq = din("xT_seq", [D, NSEQ])
    xT_own = din("xT_own", [D, NOWN])
    x_own = din("x_own", [NOWN, D])
    valid_d = din("valid", [128, NTO])
    w_in = din("w_in", [D, DIN])
    bias_d = din("bias_bc", [128, DIN])
    bcols_d = din("bcols", [128, 48])
    wa2_d = din("wa2", [16, 256])
    ba2_d = din("ba2_bc", [128, 256])
    wpg_d = din("wpg", [512, D])
    wpf_d = din("wpf", [512, D])
    wout_d = din("wout", [D, D])
    wg_d = din("wg", [D, DFF])
    wu_d = din("wu", [D, DFF])
    wd_d = din("wd", [DFF, D])
    ln_d = din("ln_bc", [128, 4 * D])
    kflag_d = din("kflag", [128, 8 * 128])
    oneh_d = din("onehot", [128, 8 * 32])
    cst_d = din("cst", [128, NCST])
    kTc_d = din("kTc", [2, 512, 2048])
    vc_d = din("vc", [2, 2048, 512])
    lfc_d = din("lfc", [2, 2048, 8])
    s0_d = din("s0", [2, 64, 512])

    y_o = dout("y_own", [NOWN, D])
    fk_o = dout("fk_own", [NOWN, 512])
    fv_o = dout("fv_own", [NOWN, 512])
    lf_o = dout("lf_own", [NOWN, 8])
    gp_o = dout("gstate_p", [64, 512])
    gs_o = dout("gstate_s", [2, 64, 512])

    KT_s = dscr("KT_s", [512, NSEQ])
    V_s = dscr("V_s", [8, 128, 128, 128])
    Q_s = dscr("Q_s", [8, 68, NOWN])
    KX_s = dscr("KX_s", [8, 2, NSEQ])
    oK_s = dscr("oK_s", [8, 64, NOWN])
    oV_s = dscr("oV_s", [8, 128, NTO, 128])
    fo_s = dscr("fo_s", [512, NOWN])
    go_s = dscr("go_s", [4, 128, NOWN])

    w_in_v = w_in.rearrange("(kc p) c -> p kc c", p=128)
    wz_s = dscr("wz_s", [D, 2048])
    wpg_s = dscr("wpg_s", [512, D])
    wpf_s = dscr("wpf_s", [512, D])
    wout_s = dscr("wout_s", [D, D])
    wg_s = dscr("wg_s", [D, DFF])
    wu_s = dscr("wu_s", [D, DFF])
    wd_s = dscr("wd_s", [DFF, D])
    PREP = [(wz_s[:, :], w_in[:, C_ZG:C_ZG + 2048]), (wpg_s[:, :], wpg_d[:, :]), (wpf_s[:, :], wpf_d[:, :]), (wout_s[:, :], wout_d[:, :]),
            (wg_s[:, :], wg_d[:, :]), (wu_s[:, :], wu_d[:, :]), (wd_s[:, :], wd_d[:, :])]

    def sb(name, shape, dt):
        return nc.sbuf_tensor("sb_" + name, shape, dt)

    from contextlib import ExitStack
    with ExitStack() as top:
        def alloc(name, shape, dt=F32):
            return T(top.enter_context(sb(name, shape, dt)))

        psd = [top.enter_context(nc.psum_tensor("psd%d" % i, [128, 1024], F32)) for i in range(4)]
        ps = []
        for i in range(4):
            ps.append(T(psd[i][:, 0:512]))
            ps.append(T(psd[i][:, 512:1024]))
        psD = [T(psd[i][:, :]) for i in range(3)]
        bank = Ring(ps[0:6])
        obank = Ring(ps[6:8])
        nb = bank.next

        mid = ExitStack()

        def allocm(name, shape, dt=F32):
            return T(mid.enter_context(sb(name, shape, dt)))

        cst32 = alloc("cst32", [128, NCST])
        cstbf = alloc("cstbf", [128, NCST], BF16)
        bcols = alloc("bcols", [128, 48])
        valid = alloc("valid", [128, NTO])
        negc = allocm("negc", [128, 128 * 8])
        negco = allocm("negco", [128, NTO * 8])
        Cb = allocm("Cb", [128, 32 * 8])
        Csel = allocm("Csel", [128, 64])
        kflag = allocm("kflag", [128, 1024])
        oneh = allocm("oneh", [128, 256])
        negcs = allocm("negcs", [128, 2 * 16 * 8])
        ntots = allocm("ntots", [128, 16])
        S = allocm("S", [64, 512])
        gcol = allocm("gcol", [128, 4])

        ident32 = cst32[:, 0:128]
        triU32 = cst32[:, 128:256]
        ones32 = cst32[:, 256:384]
        tri2_32 = cst32[:, 384:512]
        trirev32 = cst32[:, 512:640]
        csel32 = cst32[:, 768:770]
        trirev128_32 = cst32[:, 772:900]
        identbf = cstbf[:, 0:128]
        onesbf = cstbf[:, 256:384]
        masknegbf = cstbf[:, 640:768]

        K.dma(K.sp, cst32[:, :], cst_d[:, :], writes=[cst32])
        K.dma(K.pool, cstbf[:, :], cst_d[:, :], writes=[cstbf])
        K.dma(K.sp, bcols[:, :], bcols_d[:, :], writes=[bcols])
        K.dma(K.sp, valid[:, :], valid_d[:, :], writes=[valid])
        K.dma(K.sp, kflag[:, :], kflag_d[:, :], writes=[kflag])
        K.dma(K.sp, oneh[:, :], oneh_d[:, :], writes=[oneh])
        K.op(K.dve, lambda: V_.tensor_scalar(out=gcol[:, :], in0=bcols[:, 37:41], scalar1=1.0, scalar2=None, op0=ALU.mult), [bcols], [gcol])
        K.op(K.dve, lambda: V_.memset(S[:, :], 0.0), [], [S])

        def nlf_from(ffv, out_ap, out_t, tmp_t):
            K.op(K.act, lambda: A_.activation(tmp_t[:, :], ffv[:, :], AF.Exp, scale=-1.0), [ffv], [tmp_t])
            K.op(K.act, lambda: A_.activation(out_ap, tmp_t[:, :], AF.Ln, bias=1.0, scale=1.0), [tmp_t], [out_t])

        with ExitStack() as p12:
            def al(name, shape, dt=F32):
                return T(p12.enter_context(sb(name, shape, dt)))

            Acc = [al("Acc%d" % s, [64, 512]) for s in range(8)]
            for s in range(8):
                K.op(K.pool, lambda s=s: G_.memset(Acc[s][:, :], 0.0), [], [Acc[s]])
            Wfk = al("Wfk", [128, 8, 512], BF16)
            Wfv = al("Wfv", [128, 8, 512], BF16)
            Wgv = al("Wgv", [128, 8, 512], BF16)
            Wgkff = al("Wgkff", [128, 8, 264], BF16)
            Wga = al("Wga", [128, 8, 16], BF16)
            Wa2 = al("Wa2", [16, 256], BF16)
            ba2 = al("ba2", [128, 256])
            bias = al("bias", [128, 1800])
            K.dma(K.pool, Wfk[:, :, :], w_in_v[:, :, C_FK:C_FK + 512], writes=[Wfk])
            K.dma(K.pool, Wfv[:, :, :], w_in_v[:, :, C_FV:C_FV + 512], writes=[Wfv])
            K.dma(K.pool, Wgv[:, :, :], w_in_v[:, :, C_GV:C_GV + 512], writes=[Wgv])
            K.dma(K.pool, Wgkff[:, :, 0:256], w_in_v[:, :, C_GK:C_GK + 256], writes=[Wgkff])
            K.dma(K.pool, Wgkff[:, :, 256:264], w_in_v[:, :, C_FF:C_FF + 8], writes=[Wgkff])
            K.dma(K.pool, Wga[:, :, :], w_in_v[:, :, C_GA:C_GA + 16], writes=[Wga])
            K.dma(K.pool, Wa2[:, :], wa2_d[:, :], writes=[Wa2])
            K.dma(K.sp, ba2[:, :], ba2_d[:, :], writes=[ba2])
            for (o_, c_, n_) in ((0, C_FK, 512), (512, C_FV, 512), (1024, C_GV, 512), (1536, C_GK, 256), (1792, C_FF, 8)):
                K.dma(K.sp, bias[:, o_:o_ + n_], bias_d[:, c_:c_ + n_], writes=[bias])

            XTr = Ring([al("XT%d" % i, [128, 8, 512], BF16) for i in range(2)])
            gvr = Ring([al("gv%d" % i, [128, 512], BF16) for i in range(3)])
            gkr = Ring([al("gk%d" % i, [128, 256], BF16) for i in range(3)])
            tmp256 = Ring([al("t256_%d" % i, [128, 256]) for i in range(3)])
            nlar = Ring([al("nla%d" % i, [128, 256]) for i in range(3)])
            err = Ring([al("er%d" % i, [128, 256], BF16) for i in range(2)])
            kdr = Ring([al("kdec%d" % i, [128, 256], BF16) for i in range(3)])
            eblr = Ring([al("ebl%d" % i, [64, 8]) for i in range(3)])
            ffr = Ring([al("ff%d" % i, [128, 8]) for i in range(3)])
            t8r = Ring([al("t8_%d" % i, [128, 8]) for i in range(3)])
            nlfr = Ring([al("nlf%d" % i, [128, 8]) for i in range(3)])
            gaTr = Ring([al("gaT%d" % i, [16, 512], BF16) for i in range(2)])
            noff = al("noff", [128, 8])
            K.op(K.dve, lambda: V_.memset(noff[:, :], 0.0), [], [noff])
            onesq = al("onesq", [8, 512], BF16)
            K.op(K.dve, lambda: V_.memset(onesq[:, :], 1.0), [], [onesq])
            p1s = ExitStack()

            def al1(name, shape, dt=F32):
                return T(p1s.enter_context(sb(name, shape, dt)))

            KTst = Ring([al1("KTst%d" % i, [128, 4, 512], BF16) for i in range(2)])
            clr = Ring([al1("cl%d" % i, [8, 129]) for i in range(2)])
            corr = al1("corr", [8, 128])
            hif = al1("hif", [8, 128])
            kxr = Ring([al1("kx%d" % i, [8, 2, 512], BF16) for i in range(2)])
            for t in kxr.t:
                K.op(K.dve, lambda t=t: V_.memset(t[:, :, :], 0.0), [], [t])
            p1share = {}
            Vst = Ring([al1("Vst%d" % i, [128, 8, 4, 128], BF16) for i in range(2)])
            for t in Vst.t:
                K.op(K.pool, lambda t=t: G_.memset(t[:, :, :, :], 1.0), [], [t])

            def gla_gate(gaT, c0, gk_bf, vcol):
                p = nb()
                K.mm(p, p[:, 0:256], [(gaT[0:16, c0:c0 + 128], Wa2[0:16, :])], reads=[gaT, Wa2])
                z = tmp256.next()
                K.op(K.dve, lambda: V_.tensor_tensor(z[:, :], p[:, 0:256], ba2[:, :], ALU.add), [p, ba2], [z])
                K.op(K.act, lambda: A_.activation(z[:, :], z[:, :], AF.Exp, scale=-1.0), [z], [z])
                nla = nlar.next()
                K.op(K.act, lambda: A_.activation(nla[:, :], z[:, :], AF.Ln, bias=1.0, scale=1.0), [z], [nla])
                if vcol is not None:
                    K.op(K.dve, lambda: V_.tensor_scalar(out=nla[:, :], in0=nla[:, :], scalar1=vcol, scalar2=None, op0=ALU.mult), [nla, valid], [nla])
                p2 = nb()
                K.mm(p2, p2[:, 0:256], [(trirev32, nla[:, :])], reads=[cst32, nla])
                er = err.next()
                K.op(K.act, lambda: A_.activation(er[:, :], p2[:, 0:256], AF.Exp, scale=-1.0 / 16.0), [p2], [er])
                kdec = kdr.next()
                K.op(K.pool, lambda: G_.tensor_tensor(kdec[:, :], gk_bf[:, :], er[:, :], ALU.mult), [gk_bf, er], [kdec])
                p3 = nb()
                for h in range(4):
                    K.mm(p3, p3[0:64, h * 2:(h + 1) * 2], [(nla[:, h * 64:(h + 1) * 64], csel32)], reads=[nla, cst32])
                ebl = eblr.next()
                K.op(K.act, lambda: A_.activation(ebl[:, :], p3[0:64, 0:8], AF.Exp, scale=-1.0 / 16.0), [p3], [ebl])
                return nla, kdec, ebl

            def state_update(kdec, gv_bf, ebl, ci):
                p = nb()
                for h in range(4):
                    K.mm(p, p[0:64, h * 128:(h + 1) * 128],
                         [(kdec[ci * 64:(ci + 1) * 64, h * 64:(h + 1) * 64], gv_bf[ci * 64:(ci + 1) * 64, h * 128:(h + 1) * 128])],
                         reads=[kdec, gv_bf])
                S3 = S[:, :].rearrange("p (h v) -> p h v", h=4)
                eb = ebl[:, :].rearrange("p (h c) -> p h c", c=2)[:, :, ci:ci + 1].to_broadcast([64, 4, 128])
                K.op(K.dve, lambda: V_.tensor_tensor(S3, S3, eb, ALU.mult), [S, ebl], [S])
                K.op(K.dve, lambda: V_.tensor_tensor(S[:, :], S[:, :], p[0:64, :], ALU.add), [S, p], [S])

            def tok_major(XT, c0, W_t, ncols):
                p = nb()
                K.mm(p, p[:, 0:ncols], [(XT[:, kc, c0:c0 + 128], W_t[:, kc, 0:ncols]) for kc in range(8)], reads=[XT, W_t])
                return p

            def ga_T(XT, W):
                p = nb()
                K.mm(p, p[0:16, 0:W], [(Wga[:, kc, :], XT[:, kc, 0:W]) for kc in range(8)], reads=[Wga, XT])
                gaT = gaTr.next()
                K.op(K.act, lambda: A_.activation(gaT[0:16, 0:W], p[0:16, 0:W], AF.Identity, bias=bcols[0:16, 16:17], scale=1.0), [p, bcols], [gaT])
                return gaT

            def chain_stages(g, gaT, c0, gk_bf, gv_bf, first_of_group):
                ctx = {}

                def st1():
                    p = nb()
                    K.mm(p, p[:, 0:256], [(gaT[0:16, c0:c0 + 128], Wa2[0:16, :])], reads=[gaT, Wa2])
                    z = tmp256.next()
                    K.op(K.dve, lambda: V_.tensor_tensor(z[:, :], p[:, 0:256], ba2[:, :], ALU.add), [p, ba2], [z])
                    K.op(K.act, lambda: A_.activation(z[:, :], z[:, :], AF.Exp, scale=-1.0), [z], [z])
                    nla = nlar.next()
                    K.op(K.act, lambda: A_.activation(nla[:, :], z[:, :], AF.Ln, bias=1.0, scale=1.0), [z], [nla])
                    ctx["nla"] = nla

                def st2():
                    nla = ctx["nla"]
                    p2 = nb()
                    K.mm(p2, p2[:, 0:256], [(trirev128_32, nla[:, :])], reads=[cst32, nla])
                    p3 = nb()
                    for h in range(4):
                        K.mm(p3, p3[0:64, h:h + 1], [(nla[:, h * 64:(h + 1) * 64], ones32[:, 0:1])], reads=[nla, cst32])
                    er = err.next()
                    K.op(K.act, lambda: A_.activation(er[:, :], p2[:, 0:256], AF.Exp, scale=-1.0 / 16.0), [p2], [er])
                    ebl = eblr.next()
                    K.op(K.act, lambda: A_.activation(ebl[:, 0:4], p3[0:64, 0:4], AF.Exp, scale=-1.0 / 16.0), [p3], [ebl])
                    kdec = kdr.next()
                    K.op(K.pool, lambda: G_.tensor_tensor(kdec[:, :], gk_bf[:, :], er[:, :], ALU.mult), [gk_bf, er], [kdec])
                    ctx["kdec"] = kdec
                    ctx["ebl"] = ebl

                def st3():
                    kdec, ebl = ctx["kdec"], ctx["ebl"]
                    p = nb()
                    for h in range(4):
                        K.mm(p, p[0:64, h * 128:(h + 1) * 128], [(kdec[:, h * 64:(h + 1) * 64], gv_bf[:, h * 128:(h + 1) * 128])], reads=[kdec, gv_bf])
                    S3 = S[:, :].rearrange("p (h v) -> p h v", h=4)
                    K.op(K.dve, lambda: V_.tensor_tensor(S3, S3, ebl[:, 0:4].unsqueeze(2).to_broadcast([64, 4, 128]), ALU.mult), [S, ebl], [S])
                    K.op(K.dve, lambda: V_.tensor_tensor(S[:, :], S[:, :], p[0:64, :], ALU.add), [S, p], [S])

                def st4():
                    if first_of_group and g + 1 < NSEQ // 512:
                        g1 = g + 1
                        for s_ in range(8):
                            K.op(K.dve, lambda s_=s_: V_.scalar_tensor_tensor(out=Acc[s_][:, :], in0=S[:, :], scalar=oneh[0:64, s_ * 32 + g1:s_ * 32 + g1 + 1],
                                                                               in1=Acc[s_][:, :], op0=ALU.mult, op1=ALU.add), [S, oneh, Acc[s_]], [Acc[s_]])

                return [st1, st2, st3, st4]

            def proj_stages(g, m, XT, kst, vst, out):
                j = g * 4 + m

                def a():
                    if m == 0:
                        out["gaT"] = ga_T(XT, 512)
                    pA = tok_major(XT, m * 128, Wfv, 512)
                    K.op(K.dve, lambda: V_.tensor_tensor(vst[:, :, m, 0:64], pA[:, :].rearrange("p (h c) -> p h c", h=8),
                                                         bias[:, 512:1024].rearrange("p (h c) -> p h c", h=8), ALU.add), [pA, bias], [vst])
                    if m == 3:
                        K.dma(K.sp, V_s.rearrange("h p j c -> p h j c")[:, :, g * 4:(g + 1) * 4, :], vst[:, :, :, :], reads=[vst])

                def b():
                    pB = tok_major(XT, m * 128, Wgv, 512)
                    gv_bf = gvr.next()
                    K.op(K.dve, lambda: V_.tensor_tensor(gv_bf[:, :], pB[:, :], bias[:, 1024:1536], ALU.add), [pB, bias], [gv_bf])
                    out["gv"] = gv_bf

                def c():
                    pC = tok_major(XT, m * 128, Wgkff, 264)
                    gk_bf = gkr.next()
                    K.op(K.dve, lambda: V_.tensor_tensor(gk_bf[:, :], pC[:, 0:256], bias[:, 1536:1792], ALU.add), [pC, bias], [gk_bf])
                    out["gk"] = gk_bf
                    ffv = ffr.next()
                    K.op(K.dve, lambda: V_.tensor_tensor(ffv[:, :], pC[:, 256:264], bias[:, 1792:1800], ALU.add), [pC, bias], [ffv])
                    nlf = nlfr.next()
                    nlf_from(ffv, nlf[:, :], nlf, t8r.next())
                    out["nlf"] = nlf

                def d():
                    p = nb()
                    K.mm(p, p[:, :], [(Wfk[:, kc, m * 128:(m + 1) * 128], XT[:, kc, :]) for kc in range(8)], reads=[Wfk, XT])
                    K.op(K.act, lambda: A_.activation(kst[:, m, :], p[:, :], AF.Identity, bias=bcols[:, 4 + m:5 + m], scale=1.0), [p, bcols], [kst])
                    nlf = out["nlf"]
                    pD = nb()
                    K.mm(pD, pD[:, 0:8], [(triU32, nlf[:, :])], reads=[cst32, nlf])
                    K.mm(pD, pD[:, 8:16], [(ones32, nlf[:, :])], reads=[cst32, nlf])
                    if m == 0:
                        K.op(K.dve, lambda: V_.tensor_copy(Cb[:, g * 8:(g + 1) * 8], noff[:, :]), [noff], [Cb])
                    K.op(K.dve, lambda: V_.tensor_tensor(negc[:, j * 8:(j + 1) * 8], pD[:, 0:8], noff[:, :], ALU.add), [pD, noff], [negc])
                    K.op(K.dve, lambda: V_.tensor_tensor(noff[:, :], noff[:, :], pD[:, 8:16], ALU.add), [pD, noff], [noff])
                    if m == 3:
                        K.dma(K.sp, KT_s.rearrange("(pr p) t -> p pr t", p=128)[:, :, g * 512:(g + 1) * 512], kst[:, :, :], reads=[kst])
                    pE = nb()
                    K.mm(pE, pE[0:8, 0:128], [(nlf[:, :], triU32)], reads=[nlf, cst32])
                    K.mm(pE, pE[0:8, 128:129], [(nlf[:, :], ones32[:, 0:1])], reads=[nlf, cst32])
                    cl = clr.next()
                    K.op(K.dve, lambda: V_.tensor_copy(cl[:, :], pE[0:8, 0:129]), [pE], [cl])
                    if m == 0:
                        p1share["kx"] = kxr.next()
                    kx = p1share["kx"]
                    if m % 2 == 1:
                        prev = p1share["cl"]
                        K.op(K.dve, lambda: V_.scalar_tensor_tensor(out=corr[:, :], in0=cl[:, 0:128], scalar=prev[:, 128:129], in1=prev[:, 0:128],
                                                                    op0=ALU.add, op1=ALU.subtract), [cl, prev], [corr])
                        K.op(K.dve, lambda: V_.tensor_copy(kx[:, 0, m * 128:(m + 1) * 128], corr[:, :]), [corr], [kx])
                        K.op(K.dve, lambda: V_.tensor_copy(hif[:, :], kx[:, 0, m * 128:(m + 1) * 128]), [kx], [hif])
                        K.op(K.dve, lambda: V_.tensor_tensor(kx[:, 1, m * 128:(m + 1) * 128], corr[:, :], hif[:, :], ALU.subtract), [corr, hif], [kx])
                    p1share["cl"] = cl
                    if m == 3:
                        K.dma(K.sp, KX_s[:, :, g * 512:(g + 1) * 512], kx[:, :, :], reads=[kx])

                return [a, b, c, d]

            pending = None
            gstate = {}
            NG = NSEQ // 512
            xt_srcs = [(xT_seq.rearrange("(kc p) t -> p kc t", p=128)[:, :, g_ * 512:(g_ + 1) * 512], 512) for g_ in range(NG)]
            xt_srcs += [(xT_own.rearrange("(kc p) t -> p kc t", p=128)[:, :, t0_:t0_ + W_], W_) for (t0_, W_) in SLOTS]
            xt_issued = {}

            def xt_issue(i_):
                if i_ < len(xt_srcs) and i_ not in xt_issued:
                    t_ = XTr.next()
                    src_, W_ = xt_srcs[i_]
                    K.dma(K.pool, t_[:, :, 0:W_], src_, writes=[t_])
                    xt_issued[i_] = t_

            def xt_get(i_):
                xt_issue(i_)
                xt_issue(i_ + 1)
                return xt_issued[i_]
            for g in range(NG):
                XT = xt_get(g)
                if g % 4 == 2 and PREP:
                    po_, pi_ = PREP.pop(0)
                    K.dma(K.pool, po_, pi_)
                kst = KTst.next()
                vst = Vst.next()
                for m in range(4):
                    out = gstate if m == 0 else {"gaT": gstate["gaT"]}
                    if m == 0:
                        gstate = out = {}
                    ps_ = proj_stages(g, m, XT, kst, vst, out)
                    cs_ = pending if pending is not None else [None] * 4
                    for k_ in range(4):
                        ps_[k_]()
                        if cs_[k_] is not None:
                            cs_[k_]()
                    if m == 0:
                        gstate = out
                    gaT_cur = gstate["gaT"]
                    pending = chain_stages(g, gaT_cur, m * 128, out["gk"], out["gv"], m == 3)
            for f_ in pending:
                f_()
            K.dma(K.sp, gp_o[:, :], S[:, :], reads=[S])
            ctmp = al1("ctmp", [128, 256])
            for s in range(8):
                K.op(K.dve, lambda s=s: V_.tensor_tensor(ctmp[:, :].rearrange("p (h i) -> p h i", h=8), Cb[:, :].rearrange("p (i h) -> p h i", h=8),
                                                         oneh[:, s * 32:(s + 1) * 32].unsqueeze(1).to_broadcast([128, 8, 32]), ALU.mult), [Cb, oneh], [ctmp])
                K.op(K.dve, lambda s=s: V_.reduce_sum(Csel[:, s * 8:(s + 1) * 8], ctmp[:, :].rearrange("p (h i) -> p h i", h=8), axis=mybir.AxisListType.X), [ctmp], [Csel])

            lfc_sb = al1("lfc_sb", [128, 2 * 16 * 8])
            K.dma(K.sp, lfc_sb[:, :].rearrange("p (s j h) -> p s j h", s=2, j=16), lfc_d.rearrange("s (j p) h -> p s j h", p=128), writes=[lfc_sb])
            K.op(K.dve, lambda: V_.tensor_scalar(out=lfc_sb[:, :], in0=lfc_sb[:, :], scalar1=-1.0, scalar2=None, op0=ALU.mult), [lfc_sb], [lfc_sb])
            for st in range(2):
                for j in range(16):
                    o = (st * 16 + j) * 8
                    p = nb()
                    terms = [((triU32 if j2 == j else ones32), lfc_sb[:, (st * 16 + j2) * 8:(st * 16 + j2 + 1) * 8]) for j2 in range(j + 1)]
                    K.mm(p, p[:, 0:8], terms, reads=[cst32, lfc_sb])
                    K.op(K.dve, lambda p=p, o=o: V_.tensor_copy(negcs[:, o:o + 8], p[:, 0:8]), [p], [negcs])
                p = nb()
                K.mm(p, p[:, 0:8], [(ones32, lfc_sb[:, (st * 16 + j2) * 8:(st * 16 + j2 + 1) * 8]) for j2 in range(16)], reads=[cst32, lfc_sb])
                K.op(K.dve, lambda p=p, st=st: V_.tensor_copy(ntots[:, st * 8:(st + 1) * 8], p[:, 0:8]), [p], [ntots])

            K.barrier()
            p1s.close()
            Wfq = al("Wfq", [128, 8, 512], BF16)
            Wgq = al("Wgq", [128, 8, 256], BF16)
            Wgr = al("Wgr", [128, 8, 512], BF16)
            K.dma(K.pool, Wfq[:, :, :], w_in_v[:, :, C_FQ:C_FQ + 512], writes=[Wfq])
            K.dma(K.pool, Wgq[:, :, :], w_in_v[:, :, C_GQ:C_GQ + 256], writes=[Wgq])
            K.dma(K.pool, Wgr[:, :, :], w_in_v[:, :, C_GR:C_GR + 512], writes=[Wgr])
            Qst = Ring([al("Qst%d" % i, [128, 4, 512], BF16) for i in range(2)])
            oVst = Ring([al("oVst%d" % i, [128, 8, 4, 128], BF16) for i in range(1)])
            for t in oVst.t:
                K.op(K.pool, lambda t=t: G_.memset(t[:, :, :, :], 1.0), [], [t])
            gqT = al("gqT", [64, 4, 512], BF16)
            gkT = al("gkT", [64, 4, 512], BF16)
            grT = al("grT", [128, 4, 512], BF16)
            oT = al("oT", [128, 4, 512])
            sq = al("sq", [128, 4, 512], BF16)
            goT = al("goT", [128, 4, 512], BF16)
            fko = Ring([al("fko%d" % i, [128, 512]) for i in range(2)])
            fvo = Ring([al("fvo%d" % i, [128, 512]) for i in range(2)])
            lfo = Ring([al("lfo%d" % i, [128, 8]) for i in range(2)])
            nlfo = al("nlfo", [128, 4, 8])
            ncT = al("ncT", [8, 512])
            chi = al("chi", [8, 512], BF16)
            chif = al("chif", [8, 512])
            clo = al("clo", [8, 512], BF16)
            ebr = Ring([al("eb%d" % i, [64, 512]) for i in range(2)])
            enbr = Ring([al("enb%d" % i, [64, 512]) for i in range(2)])
            qdr = Ring([al("qd%d" % i, [64, 4, 128], BF16) for i in range(2)])
            kddr = Ring([al("kd%d" % i, [64, 4, 128], BF16) for i in range(2)])
            Amr = Ring([al("Am%d" % i, [128, 4, 128], BF16) for i in range(2)])
            Sbr = Ring([al("Sb%d" % i, [64, 512], BF16) for i in range(4)])
            rtr = Ring([al("rt%d" % i, [128, 512]) for i in range(1)])
            onr = Ring([al("on%d" % i, [128, 512]) for i in range(1)])

            for si, (tok0, W) in enumerate(SLOTS):
                nt = W // 128
                tm0 = tok0 // 128
                XT = xt_get(NG + si)
                if si < 8:
                    K.op(K.dve, lambda si=si: V_.tensor_copy(S[:, :], Acc[si][:, :]), [Acc[si]], [S])
                else:
                    K.dma(K.sp, S[:, :], s0_d[si - 8], writes=[S])
                for (Wt, bc0, dst, scale) in ((Wfq, 0, Q_s, 0.125), (Wfk, 4, oK_s, 1.0)):
                    qst = Qst.next()
                    for pr in range(4):
                        p = nb()
                        K.mm(p, p[:, 0:W], [(Wt[:, kc, pr * 128:(pr + 1) * 128], XT[:, kc, 0:W]) for kc in range(8)], reads=[Wt, XT])
                        K.op(K.dve, lambda p=p, pr=pr, qst=qst, bc0=bc0, scale=scale: V_.tensor_scalar(
                            out=qst[:, pr, 0:W], in0=p[:, 0:W], scalar1=bcols[:, bc0 + pr:bc0 + pr + 1], scalar2=scale, op0=ALU.add, op1=ALU.mult), [p, bcols], [qst])
                    for hh in range(2):
                        K.dma(K.sp, dst[:, 0:64, tok0:tok0 + W].rearrange("(pr hh) d t -> hh d pr t", hh=2)[hh], qst[hh * 64:(hh + 1) * 64, :, 0:W], reads=[qst])
                for h in range(4):
                    p = nb()
                    K.mm(p, p[0:64, 0:W], [(Wgq[:, kc, h * 64:(h + 1) * 64], XT[:, kc, 0:W]) for kc in range(8)], reads=[Wgq, XT])
                    K.op(K.dve, lambda p=p, h=h: V_.tensor_scalar(out=gqT[:, h, 0:W], in0=p[0:64, 0:W], scalar1=bcols[0:64, 8 + h:9 + h], scalar2=0.125,
                                                                  op0=ALU.add, op1=ALU.mult), [p, bcols], [gqT])
                    p = nb()
                    K.mm(p, p[0:64, 0:W], [(Wgkff[:, kc, h * 64:(h + 1) * 64], XT[:, kc, 0:W]) for kc in range(8)], reads=[Wgkff, XT])
                    K.op(K.act, lambda p=p, h=h: A_.activation(gkT[:, h, 0:W], p[0:64, 0:W], AF.Identity, bias=bcols[0:64, 12 + h:13 + h], scale=1.0), [p, bcols], [gkT])
                    p = nb()
                    K.mm(p, p[:, 0:W], [(Wgr[:, kc, h * 128:(h + 1) * 128], XT[:, kc, 0:W]) for kc in range(8)], reads=[Wgr, XT])
                    K.op(K.act, lambda p=p, h=h: A_.activation(grT[:, h, 0:W], p[:, 0:W], AF.Silu, bias=bcols[:, 17 + h:18 + h], scale=1.0), [p, bcols], [grT])
                gaT = ga_T(XT, W)
                ovst = oVst.next()
                def proj_own(m, out):
                    tm = tm0 + m
                    r0 = tm * 128

                    def a():
                        pA = tok_major(XT, m * 128, Wfk, 512)
                        fk_t = fko.next()
                        K.op(K.dve, lambda: V_.tensor_tensor(fk_t[:, :], pA[:, :], bias[:, 0:512], ALU.add), [pA, bias], [fk_t])
                        K.dma(K.sp, fk_o[r0:r0 + 128, :], fk_t[:, :], reads=[fk_t])

                    def b():
                        pB = tok_major(XT, m * 128, Wfv, 512)
                        fv_t = fvo.next()
                        K.op(K.dve, lambda: V_.tensor_tensor(fv_t[:, :], pB[:, :], bias[:, 512:1024], ALU.add), [pB, bias], [fv_t])
                        K.dma(K.sp, fv_o[r0:r0 + 128, :], fv_t[:, :], reads=[fv_t])
                        K.op(K.pool, lambda: G_.tensor_copy(ovst[:, :, m, 0:64], fv_t[:, :].rearrange("p (h c) -> p h c", h=8)), [fv_t], [ovst])

                    def c():
                        pC = tok_major(XT, m * 128, Wgv, 512)
                        gv_bf = gvr.next()
                        K.op(K.dve, lambda: V_.tensor_tensor(gv_bf[:, :], pC[:, :], bias[:, 1024:1536], ALU.add), [pC, bias], [gv_bf])
                        out["gv"] = gv_bf

                    def d():
                        pD = tok_major(XT, m * 128, Wgkff, 264)
                        gtmp = tmp256.next()
                        K.op(K.dve, lambda: V_.tensor_tensor(gtmp[:, :], pD[:, 0:256], bias[:, 1536:1792], ALU.add), [pD, bias], [gtmp])
                        gk_bf = gkr.next()
                        K.op(K.dve, lambda: V_.tensor_scalar(out=gk_bf[:, :], in0=gtmp[:, :], scalar1=valid[:, tm:tm + 1], scalar2=None, op0=ALU.mult),
                             [gtmp, valid], [gk_bf])
                        out["gk"] = gk_bf
                        ffv = ffr.next()
                        K.op(K.dve, lambda: V_.tensor_tensor(ffv[:, :], pD[:, 256:264], bias[:, 1792:1800], ALU.add), [pD, bias], [ffv])
                        nlf_from(ffv, nlfo[:, m, :], nlfo, t8r.next())
                        lf_t = lfo.next()
                        K.op(K.dve, lambda: V_.tensor_scalar(out=lf_t[:, :], in0=nlfo[:, m, :], scalar1=-1.0, scalar2=None, op0=ALU.mult), [nlfo], [lf_t])
                        K.dma(K.sp, lf_o[r0:r0 + 128, :], lf_t[:, :], reads=[lf_t])

                    return [a, b, c, d]

                def chain_own(m, out):
                    tm = tm0 + m
                    ctx = {}

                    def c1():
                        p = nb()
                        K.mm(p, p[:, 0:256], [(gaT[0:16, m * 128:(m + 1) * 128], Wa2[0:16, :])], reads=[gaT, Wa2])
                        z = tmp256.next()
                        K.op(K.dve, lambda: V_.tensor_tensor(z[:, :], p[:, 0:256], ba2[:, :], ALU.add), [p, ba2], [z])
                        K.op(K.act, lambda: A_.activation(z[:, :], z[:, :], AF.Exp, scale=-1.0), [z], [z])
                        nla = nlar.next()
                        K.op(K.act, lambda: A_.activation(nla[:, :], z[:, :], AF.Ln, bias=1.0, scale=1.0), [z], [nla])
                        K.op(K.dve, lambda: V_.tensor_scalar(out=nla[:, :], in0=nla[:, :], scalar1=valid[:, tm:tm + 1], scalar2=None, op0=ALU.mult), [nla, valid], [nla])
                        ctx["nla"] = nla

                    def c2():
                        nla = ctx["nla"]
                        gk_bf = out["gk"]
                        p2 = nb()
                        K.mm(p2, p2[:, 0:256], [(trirev128_32, nla[:, :])], reads=[cst32, nla])
                        p3 = nb()
                        for h in range(4):
                            K.mm(p3, p3[0:64, h:h + 1], [(nla[:, h * 64:(h + 1) * 64], ones32[:, 0:1])], reads=[nla, cst32])
                        pT = nb()
                        for h in range(4):
                            K.mm(pT, pT[0:64, h * 128:(h + 1) * 128], [(nla[:, h * 64:(h + 1) * 64], triU32)], reads=[nla, cst32])
                        er = err.next()
                        K.op(K.act, lambda: A_.activation(er[:, :], p2[:, 0:256], AF.Exp, scale=-1.0 / 16.0), [p2], [er])
                        ebl = eblr.next()
                        K.op(K.act, lambda: A_.activation(ebl[:, 0:4], p3[0:64, 0:4], AF.Exp, scale=-1.0 / 16.0), [p3], [ebl])
                        eb = ebr.next()
                        enb = enbr.next()
                        K.op(K.act, lambda: A_.activation(eb[:, :], pT[0:64, :], AF.Exp, scale=-1.0 / 16.0), [pT], [eb])
                        K.op(K.act, lambda: A_.activation(enb[:, :], pT[0:64, :], AF.Exp, scale=1.0 / 16.0), [pT], [enb])
                        kdec = kdr.next()
                        K.op(K.dve, lambda: V_.tensor_tensor(kdec[:, :], gk_bf[:, :], er[:, :], ALU.mult), [gk_bf, er], [kdec])
                        qd = qdr.next()
                        kd = kddr.next()
                        K.op(K.dve, lambda: V_.tensor_tensor(qd[:, :, :], gqT[:, :, m * 128:(m + 1) * 128], eb[:, :].rearrange("p (h t) -> p h t", h=4), ALU.mult),
                             [gqT, eb], [qd])
                        K.op(K.dve, lambda: V_.tensor_tensor(kd[:, :, :], gkT[:, :, m * 128:(m + 1) * 128], enb[:, :].rearrange("p (h t) -> p h t", h=4), ALU.mult),
                             [gkT, enb], [kd])
                        ctx.update(kdec=kdec, ebl=ebl, qd=qd, kd=kd)

                    def c3():
                        qd, kd = ctx["qd"], ctx["kd"]
                        pA2 = nb()
                        for h in range(4):
                            K.mm(pA2, pA2[:, h * 128:(h + 1) * 128], [(kd[:, h, :], qd[:, h, :])], reads=[kd, qd])
                        Am = Amr.next()
                        K.op(K.dve, lambda: V_.tensor_tensor(Am[:, :, :], pA2[:, :].rearrange("p (h t) -> p h t", h=4),
                                                             triU32.unsqueeze(1).to_broadcast([128, 4, 128]), ALU.mult), [pA2, cst32], [Am])
                        Sb0 = Sbr.next()
                        K.op(K.dve, lambda: V_.tensor_copy(Sb0[:, :], S[:, :]), [S], [Sb0])
                        kdec, ebl, gv_bf = ctx["kdec"], ctx["ebl"], out["gv"]
                        p = nb()
                        for h in range(4):
                            K.mm(p, p[0:64, h * 128:(h + 1) * 128], [(kdec[:, h * 64:(h + 1) * 64], gv_bf[:, h * 128:(h + 1) * 128])], reads=[kdec, gv_bf])
                        S3 = S[:, :].rearrange("p (h v) -> p h v", h=4)
                        K.op(K.dve, lambda: V_.tensor_tensor(S3, S3, ebl[:, 0:4].unsqueeze(2).to_broadcast([64, 4, 128]), ALU.mult), [S, ebl], [S])
                        K.op(K.dve, lambda: V_.tensor_tensor(S[:, :], S[:, :], p[0:64, :], ALU.add), [S, p], [S])
                        ctx.update(Am=Am, Sb0=Sb0)

                    def c4():
                        qd, Am, Sb0, gv_bf = ctx["qd"], ctx["Am"], ctx["Sb0"], out["gv"]
                        pO = nb()
                        for h in range(4):
                            K.mm(pO, pO[:, h * 128:(h + 1) * 128], [(gv_bf[:, h * 128:(h + 1) * 128], Am[:, h, :])], start=True, stop=False, reads=[gv_bf, Am])
                            K.mm(pO, pO[:, h * 128:(h + 1) * 128], [(Sb0[:, h * 128:(h + 1) * 128], qd[:, h, :])], start=False, stop=True, reads=[Sb0, qd])
                        K.op(K.act, lambda: A_.copy(oT[:, :, m * 128:(m + 1) * 128], pO[:, :].rearrange("p (h t) -> p h t", h=4)), [pO], [oT])

                    return [c1, c2, c3, c4]

                outs = [dict() for _ in range(nt)]
                for f_ in proj_own(0, outs[0]):
                    f_()
                for m in range(1, nt):
                    ps_ = proj_own(m, outs[m])
                    cs_ = chain_own(m - 1, outs[m - 1])
                    for k_ in range(4):
                        ps_[k_]()
                        cs_[k_]()
                for f_ in chain_own(nt - 1, outs[nt - 1]):
                    f_()
                if si >= 8:
                    K.dma(K.sp, gs_o[si - 8], S[:, :], reads=[S])
                K.dma(K.sp, oV_s.rearrange("h p j c -> p h j c")[:, :, tm0:tm0 + nt, :], ovst[:, :, 0:nt, :], reads=[ovst])
                for m in range(nt):
                    p = nb()
                    K.mm(p, p[:, 0:8], [((triU32 if m2 == m else ones32), nlfo[:, m2, :]) for m2 in range(m + 1)], reads=[cst32, nlfo])
                    K.op(K.dve, lambda p=p, m=m: V_.tensor_copy(negco[:, (tm0 + m) * 8:(tm0 + m + 1) * 8], p[:, 0:8]), [p], [negco])
                    p = nb()
                    K.mm(p, p[0:8, 0:128], [(nlfo[:, m2, :], (triU32 if m2 == m else ones32)) for m2 in range(m + 1)], reads=[cst32, nlfo])
                    K.op(K.dve, lambda p=p, m=m: V_.tensor_copy(ncT[:, m * 128:(m + 1) * 128], p[0:8, 0:128]), [p], [ncT])
                K.op(K.dve, lambda: V_.tensor_scalar(out=chi[:, 0:W], in0=ncT[:, 0:W], scalar1=-1.0, scalar2=None, op0=ALU.mult), [ncT], [chi])
                K.op(K.dve, lambda: V_.tensor_copy(chif[:, 0:W], chi[:, 0:W]), [chi], [chif])
                K.op(K.dve, lambda: V_.scalar_tensor_tensor(out=clo[:, 0:W], in0=ncT[:, 0:W], scalar=-1.0, in1=chif[:, 0:W], op0=ALU.mult, op1=ALU.subtract), [ncT, chif], [clo])
                K.dma(K.sp, Q_s[:, 64, tok0:tok0 + W], chi[:, 0:W], reads=[chi])
                K.dma(K.sp, Q_s[:, 65, tok0:tok0 + W], clo[:, 0:W], reads=[clo])
                K.dma(K.sp, Q_s[:, 66, tok0:tok0 + W], onesq[:, 0:W], reads=[onesq])
                K.dma(K.sp, Q_s[:, 67, tok0:tok0 + W], onesq[:, 0:W], reads=[onesq])
                K.op(K.act, lambda: A_.activation(sq[:, :, 0:W], oT[:, :, 0:W], AF.Square), [oT], [sq])
                for h in range(4):
                    p = nb()
                    K.mm(p, p[:, 0:W], [(onesbf, sq[:, h, 0:W])], reads=[cstbf, sq])
                    rt = rtr.next()
                    K.op(K.act, lambda p=p, rt=rt: A_.activation(rt[:, 0:W], p[:, 0:W], AF.Ln, bias=cst32[:, 770:771], scale=1.0 / 128.0), [p, cst32], [rt])
                    K.op(K.act, lambda rt=rt: A_.activation(rt[:, 0:W], rt[:, 0:W], AF.Exp, scale=-0.5), [rt], [rt])
                    on = onr.next()
                    K.op(K.dve, lambda on=on, rt=rt, h=h: V_.tensor_tensor(on[:, 0:W], oT[:, h, 0:W], rt[:, 0:W], ALU.mult), [oT, rt], [on])
                    K.op(K.dve, lambda on=on, h=h: V_.scalar_tensor_tensor(out=goT[:, h, 0:W], in0=on[:, 0:W], scalar=gcol[:, h:h + 1], in1=grT[:, h, 0:W],
                                                                           op0=ALU.mult, op1=ALU.mult), [on, gcol, grT], [goT])
                K.dma(K.sp, go_s.rearrange("h v t -> v h t")[:, :, tok0:tok0 + W], goT[:, :, 0:W], reads=[goT])
            K.barrier()

        with ExitStack() as p2b:
            def al(name, shape, dt=F32):
                return T(p2b.enter_context(sb(name, shape, dt)))

            KA_raw = p2b.enter_context(sb("KA", [128, NSEQ], BF16))
            VA_raw = p2b.enter_context(sb("VA", [128, 128, 128], BF16))
            KAc = [T(KA_raw[:, c * 4096:(c + 1) * 4096]) for c in range(4)]
            VAc = [T(VA_raw[:, c * 32:(c + 1) * 32, :]) for c in range(4)]
            KAs = Ring([al("KAs%d" % i, [128, 2048], BF16) for i in range(2)])
            VAs = Ring([al("VAs%d" % i, [128, 16, 128], BF16) for i in range(2)])
            QAr = [al("QA%d" % i, [128, NOWN], BF16) for i in range(2)]
            oKAr = [al("oKA%d" % i, [128, NOWN], BF16) for i in range(2)]
            oVAr = [al("oVA%d" % i, [128, NTO, 128], BF16) for i in range(2)]
            for QA in QAr[0:1]:
                K.op(K.pool, lambda QA=QA: G_.memset(QA[64:128, :], 0.0), [], [QA])
            for t_ in reversed(KAc):
                K.op(K.dve, lambda t_=t_: V_.memset(t_[64:128, :], 0.0), [], [t_])
                K.op(K.dve, lambda t_=t_: V_.memset(t_[64:66, :], 1.0), [], [t_])
            for oKA in oKAr[0:1]:
                K.op(K.pool, lambda oKA=oKA: G_.memset(oKA[64:128, :], 0.0), [], [oKA])
                K.op(K.pool, lambda oKA=oKA: G_.memset(oKA[64:66, :], 1.0), [], [oKA])
            for QA in QAr[1:2]:
                K.op(K.pool, lambda QA=QA: G_.memset(QA[64:128, :], 0.0), [], [QA])
            for oKA in oKAr[1:2]:
                K.op(K.pool, lambda oKA=oKA: G_.memset(oKA[64:128, :], 0.0), [], [oKA])
                K.op(K.pool, lambda oKA=oKA: G_.memset(oKA[64:66, :], 1.0), [], [oKA])
            for t_ in KAs.t:
                K.op(K.pool, lambda t_=t_: G_.memset(t_[64:128, :], 0.0), [], [t_])
                K.op(K.pool, lambda t_=t_: G_.memset(t_[64:66, :], 1.0), [], [t_])
            for t_ in VAs.t:
                K.op(K.pool, lambda t_=t_: G_.memset(t_[:, :, :], 1.0), [], [t_])
            biar = Ring([al("bia%d" % i, [128, 128]) for i in range(10)])
            Pr = Ring([al("P%d" % i, [128, 1024], BF16) for i in range(3)])
            otr = Ring([al("ot%d" % i, [128, 512]) for i in range(2)])
            dshr = Ring([al("dsh%d" % i, [64, 512]) for i in range(2)])
            for_ = Ring([al("fo%d" % i, [64, 512], BF16) for i in range(2)])
            sbank = Ring(psD)

            def make_bias(si, h, n, kind):
                tok0, W = SLOTS[si]
                nt = W // 128
                tm0 = tok0 // 128
                bia = biar.next()
                if kind == "p":
                    K.op(K.dve, lambda: V_.scalar_tensor_tensor(out=bia[:, 0:n], in0=negc[:, :].rearrange("p (j h) -> p h j", h=8)[:, h, 0:n],
                                                                scalar=Csel[:, si * 8 + h:si * 8 + h + 1], in1=kflag[:, si * 128:si * 128 + n],
                                                                op0=ALU.subtract, op1=ALU.add), [negc, Csel, kflag], [bia])
                else:
                    st = si - 8
                    K.op(K.dve, lambda: V_.tensor_scalar(out=bia[:, 0:n], in0=negcs[:, :].rearrange("p (s j h) -> p s h j", s=2, h=8)[:, st, h, 0:n],
                                                         scalar1=ntots[:, st * 8 + h:st * 8 + h + 1], scalar2=None, op0=ALU.subtract), [negcs, ntots], [bia])
                return bia

            def attend(si, h, n, kind, kblk, vblk, bia, QA, oKA, oVA):
                tok0, W = SLOTS[si]
                nt = W // 128
                tm0 = tok0 // 128
                pO = obank.next()
                if kind == "p":
                    items = [("pp", i_) for i_ in reversed(range(n // 2))] + [("own", m) for m in range(nt)]
                else:
                    items = [("pre", j) for j in reversed(range(n))] + [("own", m) for m in range(nt)]

                def emit_S(it):
                    kind_, j = it
                    p = sbank.next()
                    if kind_ == "pp":
                        for u in range(2):
                            kt_, kap_ = kblk(2 * j + u)
                            K.mm(p, p[:, u * 512:u * 512 + 512], [(kap_, QA[:, tok0:tok0 + 512])], reads=[kt_, QA])
                    elif kind_ == "pre":
                        kt_, kap_ = kblk(j)
                        K.mm(p, p[:, 0:W], [(kap_, QA[:, tok0:tok0 + W])], reads=[kt_, QA])
                    else:
                        q0 = j * 128
                        K.mm(p, p[:, q0:W], [(oKA[:, tok0 + q0:tok0 + q0 + 128], QA[:, tok0 + q0:tok0 + W])], start=True, stop=False, reads=[oKA, QA])
                        K.mm(p, p[:, q0:q0 + 128], [(identbf, masknegbf)], start=False, stop=True, reads=[cstbf])
                    return p

                def emit_E(it, p):
                    kind_, j = it
                    P = Pr.next()
                    if kind_ == "pp":
                        K.op(K.act, lambda: A_.activation(P[:, 0:1024], p[:, 0:1024], AF.Exp, bias=bia[:, 2 * j:2 * j + 1], scale=1.0), [p, bia], [P])
                    elif kind_ == "pre":
                        K.op(K.act, lambda: A_.activation(P[:, 0:W], p[:, 0:W], AF.Exp, bias=bia[:, j:j + 1], scale=1.0), [p, bia], [P])
                    else:
                        q0 = j * 128
                        K.op(K.act, lambda: A_.activation(P[:, q0:W], p[:, q0:W], AF.Exp, bias=negco[:, (tm0 + j) * 8 + h:(tm0 + j) * 8 + h + 1], scale=1.0), [p, negco], [P])
                    return P

                def emit_PV(it, P, first, last):
                    kind_, j = it
                    if kind_ == "pp":
                        for u in range(2):
                            vt_, vap_ = vblk(2 * j + u)
                            K.mm(pO, pO[:, 0:512], [(vap_, P[:, u * 512:u * 512 + 512])], start=(first and u == 0), stop=(last and u == 1), reads=[vt_, P])
                    elif kind_ == "pre":
                        vt_, vap_ = vblk(j)
                        K.mm(pO, pO[:, 0:W], [(vap_, P[:, 0:W])], start=first, stop=last, reads=[vt_, P])
                    else:
                        q0 = j * 128
                        K.mm(pO, pO[:, q0:W], [(oVA[:, tm0 + j, :], P[:, q0:W])], start=first, stop=last, reads=[oVA, P])

                LA = 2
                pend = [emit_S(items[i_]) for i_ in range(min(LA, len(items)))]
                for i, it in enumerate(items):
                    p = pend.pop(0)
                    if i + LA < len(items):
                        pend.append(emit_S(items[i + LA]))
                    P = emit_E(it, p)
                    emit_PV(it, P, i == 0, i == len(items) - 1)
                ot = otr.next()
                K.op(K.dve, lambda: V_.tensor_copy(ot[:, 0:W], pO[:, 0:W]), [pO], [ot])
                dsh = dshr.next()
                K.dma(K.sp, dsh[:, 0:W], ot[64:128, 0:W], reads=[ot], writes=[dsh])
                K.op(K.dve, lambda: V_.reciprocal(dsh[:, 0:W], dsh[:, 0:W]), [dsh], [dsh])
                fo = for_.next()
                K.op(K.dve, lambda: V_.tensor_tensor(fo[:, 0:W], ot[0:64, 0:W], dsh[:, 0:W], ALU.mult), [ot, dsh], [fo])
                K.dma(K.sp, fo_s[h * 64:(h + 1) * 64, tok0:tok0 + W], fo[:, 0:W], reads=[fo])

            def load_q(h_):
                QA, oKA, oVA = QAr[h_ % 2], oKAr[h_ % 2], oVAr[h_ % 2]
                K.dma(K.sp, QA[0:68, :], Q_s[h_], writes=[QA])
                K.dma(K.sp, oKA[0:64, :], oK_s[h_], writes=[oKA])
                K.dma(K.sp, oVA[:, :, :], oV_s[h_], writes=[oVA])

            def load_kv_chunk(h_, c):
                K.dma(K.sp, KAc[c][0:64, :], KT_s[h_ * 64:(h_ + 1) * 64, c * 4096:(c + 1) * 4096], writes=[KAc[c]])
                K.dma(K.sp, KAc[c][66:68, :], KX_s[h_, :, c * 4096:(c + 1) * 4096], writes=[KAc[c]])
                K.dma(K.sp, VAc[c][:, :, :], V_s[h_, :, c * 32:(c + 1) * 32, :], writes=[VAc[c]])

            def load_kv(h_):
                for c in (3, 2, 1, 0):
                    load_kv_chunk(h_, c)

            load_q(0)
            load_kv(0)
            for h in range(8):
                QA, oKA, oVA = QAr[h % 2], oKAr[h % 2], oVAr[h % 2]
                if h + 1 < 8:
                    load_q(h + 1)
                sk = []
                for st in range(2):
                    ks_, vs_ = KAs.next(), VAs.next()
                    K.dma(K.pool, ks_[0:64, :], kTc_d[st, h * 64:(h + 1) * 64, :], writes=[ks_])
                    K.dma(K.pool, vs_[:, :, 0:64], vc_d[st].rearrange("(j p) (h c) -> p j h c", p=128, h=8)[:, :, h, :], writes=[vs_])
                    sk.append((ks_, vs_))
                if h == 0:
                    bias_p = [make_bias(si, 0, NPRE[si], "p") for si in range(8)]
                bias_t = bias_p + [make_bias(8 + st, h, 16, "s") for st in range(2)]
                free_after = {6: 3, 4: 2, 2: 1, 0: 0}
                for si in range(7, -1, -1):
                    attend(si, h, NPRE[si], "p",
                           lambda j: (KAc[j // 32], KAc[j // 32][:, (j % 32) * 128:(j % 32 + 1) * 128]),
                           lambda j: (VAc[j // 32], VAc[j // 32][:, j % 32, :]), bias_t[si], QA, oKA, oVA)
                    if h + 1 < 8 and si in free_after:
                        load_kv_chunk(h + 1, free_after[si])
                if h + 1 < 8:
                    bias_p = [make_bias(si, h + 1, NPRE[si], "p") for si in range(8)]
                for st in range(2):
                    ks_, vs_ = sk[st]
                    attend(8 + st, h, 16, "s", lambda j, ks_=ks_: (ks_, ks_[:, j * 128:(j + 1) * 128]), lambda j, vs_=vs_: (vs_, vs_[:, j, :]),
                           bias_t[8 + st], QA, oKA, oVA)
            K.barrier()
        mid.close()

        with ExitStack() as p3:
            def al(name, shape, dt=F32):
                return T(p3.enter_context(sb(name, shape, dt)))

            Wpg = al("Wpg", [128, 4, D], BF16)
            Wpf = al("Wpf", [128, 4, D], BF16)
            Wout = al("Wout", [128, 8, D], BF16)
            lnp = al("lnp", [128, 4 * D])
            K.dma(K.sp, Wpg[:, :, :], wpg_s.rearrange("(h p) e -> p h e", p=128), writes=[Wpg])
            K.dma(K.sp, Wpf[:, :, :], wpf_s.rearrange("(h p) e -> p h e", p=128), writes=[Wpf])
            K.dma(K.sp, Wout[:, :, :], wout_s.rearrange("(kc p) e -> p kc e", p=128), writes=[Wout])
            K.dma(K.sp, lnp[:, :], ln_d[:, :], writes=[lnp])
            wpool = Ring([al("wp%d" % i, [128, 8, 512], BF16) for i in range(4)])
            Wdr = Ring([al("Wd%d" % i, [128, NFC, 256], BF16) for i in range(2)])
            XT = al("XT3", [128, 8, 512], BF16)
            xin = Ring([al("xin%d" % i, [128, D]) for i in range(1)])
            foT = al("foT", [128, 4, 512], BF16)
            goT3 = al("goT3", [128, 4, 512], BF16)
            sgr = Ring([al("sg%d" % i, [128, 512], BF16) for i in range(4)])
            t1r = Ring([al("t1_%d" % i, [128, 512]) for i in range(1)])
            t2r = Ring([al("t2_%d" % i, [128, 512]) for i in range(1)])
            mT = al("mT", [128, 8, 512], BF16)
            r1r = Ring([al("r1_%d" % i, [128, D]) for i in range(2)])
            junk = al("junk", [128, D], BF16)
            x1 = al("x1", [128, 4, D])
            x1bf = Ring([al("x1bf%d" % i, [128, D], BF16) for i in range(2)])
            x1T = al("x1T", [128, 8, 512], BF16)
            hT = al("hT", [128, NFC, 512], BF16)
            st_r = Ring([al("st%d" % i, [128, 4]) for i in range(8)])

            def ln_stages(src_ap, src_t, dst_ap, dst_t, g_ap, b_ap):
                stt = st_r.next()

                def A():
                    K.op(K.dve, lambda: V_.memset(stt[:, :], 0.0), [], [stt])
                    K.op(K.dve, lambda: V_.reduce_sum(stt[:, 0:1], src_ap, axis=mybir.AxisListType.X), [src_t], [stt])
                    K.op(K.dve, lambda: V_.tensor_scalar(out=stt[:, 1:2], in0=stt[:, 0:1], scalar1=-1.0 / D, scalar2=None, op0=ALU.mult), [stt], [stt])

                def B():
                    K.op(K.act, lambda: A_.activation(junk[:, :], src_ap, AF.Square, bias=stt[:, 1:2], scale=1.0, accum_out=stt[:, 2:3]), [src_t, stt], [junk, stt])
                    K.op(K.act, lambda: A_.activation(stt[:, 3:4], stt[:, 2:3], AF.Sqrt, bias=cst32[:, 770:771], scale=1.0 / D), [stt, cst32], [stt])

                def C():
                    K.op(K.dve, lambda: V_.reciprocal(stt[:, 3:4], stt[:, 3:4]), [stt], [stt])
                    K.op(K.dve, lambda: V_.scalar_tensor_tensor(out=dst_ap, in0=src_ap, scalar=stt[:, 1:2], in1=g_ap, op0=ALU.add, op1=ALU.mult), [src_t, stt, lnp], [dst_t])
                    K.op(K.dve, lambda: V_.scalar_tensor_tensor(out=dst_ap, in0=dst_ap, scalar=stt[:, 3:4], in1=b_ap, op0=ALU.mult, op1=ALU.add), [dst_t, stt, lnp], [dst_t])

                return A, B, C

            ln2_pending = []

            def emit_ln2_one():
                if ln2_pending:
                    m_, r0_ = ln2_pending.pop(0)
                    A, B, C = ln_stages(x1[:, m_, :], x1, x1[:, m_, :], x1, lnp[:, 2 * D:3 * D], lnp[:, 3 * D:4 * D])
                    A()
                    B()
                    C()
                    K.dma(K.sp, y_o[r0_:r0_ + 128, :], x1[:, m_, :], reads=[x1])

            P3SLOTS = [(s_ * 512, 512) for s_ in range(8)] + [(4096, 256)]
            wz_v = wz_s.rearrange("(kc p) c -> p kc c", p=128)
            wg_v = wg_s.rearrange("(kc p) f -> p kc f", p=128)
            wu_v = wu_s.rearrange("(kc p) f -> p kc f", p=128)
            wd_v = wd_s.rearrange("(fc p) e -> p fc e", p=128)
            wsrc = []
            for _ in P3SLOTS:
                for e4_ in range(2):
                    wsrc.append((wz_v[:, :, e4_ * 512:(e4_ + 1) * 512], 512))
                    wsrc.append((wz_v[:, :, 1024 + e4_ * 512:1024 + (e4_ + 1) * 512], 512))
                for f4_ in range(6):
                    nf_ = 4 if f4_ < 5 else 2
                    wsrc.append((wg_v[:, :, f4_ * 512:f4_ * 512 + nf_ * 128], nf_ * 128))
                    wsrc.append((wu_v[:, :, f4_ * 512:f4_ * 512 + nf_ * 128], nf_ * 128))
            wiss = {}
            wcnt = [0]

            def w_issue(upto):
                for i_ in range(len(wiss), min(upto + 1, len(wsrc))):
                    t_ = wpool.next()
                    src_, n_ = wsrc[i_]
                    K.dma(K.act, t_[:, :, 0:n_], src_, writes=[t_])
                    wiss[i_] = t_

            def w_get():
                i_ = wcnt[0]
                wcnt[0] += 1
                w_issue(i_ + 2)
                return wiss[i_]

            dsrc = [wd_v[:, :, cg_ * 256:(cg_ + 1) * 256] for _ in P3SLOTS for cg_ in range(4)]
            diss = {}
            dcnt = [0]

            def d_issue(upto):
                for i_ in range(len(diss), min(upto + 1, len(dsrc))):
                    t_ = Wdr.next()
                    K.dma(K.act, t_[:, :, :], dsrc[i_], writes=[t_])
                    diss[i_] = t_

            def d_get():
                i_ = dcnt[0]
                dcnt[0] += 1
                d_issue(i_ + 1)
                return diss[i_]
            for si, (tok0, W) in enumerate(P3SLOTS):
                nt = W // 128
                def load_inputs(tok0_, W_):
                    K.dma(K.pool, XT[:, :, 0:W_], xT_own.rearrange("(kc p) t -> p kc t", p=128)[:, :, tok0_:tok0_ + W_], writes=[XT])
                    K.dma(K.sp, foT[:, :, 0:W_], fo_s.rearrange("(pr p) t -> p pr t", p=128)[:, :, tok0_:tok0_ + W_], writes=[foT])
                    K.dma(K.sp, goT3[:, :, 0:W_], go_s.rearrange("h v t -> v h t")[:, :, tok0_:tok0_ + W_], writes=[goT3])

                if si == 0:
                    load_inputs(tok0, W)
                for e4 in range(2):
                    wz = w_get()
                    wf = w_get()
                    for e in range(4):
                        ec = e4 * 4 + e
                        pz = nb()
                        K.mm(pz, pz[:, 0:W], [(wz[:, kc, e * 128:(e + 1) * 128], XT[:, kc, 0:W]) for kc in range(8)], reads=[wz, XT])
                        sg = sgr.next()
                        K.op(K.act, lambda pz=pz, sg=sg, ec=ec: A_.activation(sg[:, 0:W], pz[:, 0:W], AF.Sigmoid, bias=bcols[:, 21 + ec:22 + ec], scale=1.0), [pz, bcols], [sg])
                        pf = nb()
                        K.mm(pf, pf[:, 0:W], [(wf[:, kc, e * 128:(e + 1) * 128], XT[:, kc, 0:W]) for kc in range(8)], reads=[wf, XT])
                        sz = sgr.next()
                        K.op(K.act, lambda pf=pf, sz=sz, ec=ec: A_.activation(sz[:, 0:W], pf[:, 0:W], AF.Sigmoid, bias=bcols[:, 29 + ec:30 + ec], scale=1.0), [pf, bcols], [sz])
                        pg = nb()
                        K.mm(pg, pg[:, 0:W], [(Wpg[:, h, ec * 128:(ec + 1) * 128], goT3[:, h, 0:W]) for h in range(4)], reads=[Wpg, goT3])
                        t1 = t1r.next()
                        K.op(K.dve, lambda pg=pg, t1=t1, sg=sg: V_.tensor_tensor(t1[:, 0:W], pg[:, 0:W], sg[:, 0:W], ALU.mult), [pg, sg], [t1])
                        pff = nb()
                        K.mm(pff, pff[:, 0:W], [(Wpf[:, h, ec * 128:(ec + 1) * 128], foT[:, h, 0:W]) for h in range(4)], reads=[Wpf, foT])
                        t2 = t2r.next()
                        K.op(K.dve, lambda pff=pff, t2=t2, sz=sz: V_.tensor_tensor(t2[:, 0:W], pff[:, 0:W], sz[:, 0:W], ALU.mult), [pff, sz], [t2])
                        K.op(K.dve, lambda t1=t1, t2=t2, ec=ec: V_.tensor_tensor(mT[:, ec, 0:W], t1[:, 0:W], t2[:, 0:W], ALU.add), [t1, t2], [mT])
                        if ec % 2 == 1:
                            emit_ln2_one()
                if si + 1 < len(P3SLOTS):
                    load_inputs(*P3SLOTS[si + 1])
                def transposes(xb, m):
                    for half in range(2):
                        p = nb()
                        for kq in range(4):
                            kc = half * 4 + kq
                            K.mm(p, p[:, kq * 128:(kq + 1) * 128], [(xb[:, kc * 128:(kc + 1) * 128], identbf)], reads=[xb, cstbf])
                        K.op(K.act, lambda: A_.copy(x1T[:, half * 4:(half + 1) * 4, m * 128:(m + 1) * 128],
                                                    p[:, :].rearrange("p (k t) -> p k t", k=4)), [p], [x1T])

                while ln2_pending:
                    emit_ln2_one()
                prevC = None
                prevT = None
                for m in range(nt):
                    r0 = tok0 + m * 128
                    xi = xin.next()
                    K.dma(K.sp, xi[:, :], x_own[r0:r0 + 128, :], writes=[xi])
                    r1 = r1r.next()
                    for cg in range(2):
                        p = nb()
                        K.mm(p, p[:, :], [(mT[:, e, m * 128:(m + 1) * 128], Wout[:, e, cg * 512:(cg + 1) * 512]) for e in range(8)], reads=[mT, Wout])
                        K.op(K.dve, lambda: V_.scalar_tensor_tensor(out=r1[:, cg * 512:(cg + 1) * 512], in0=xi[:, cg * 512:(cg + 1) * 512], scalar=ALPHA,
                                                                    in1=p[:, :], op0=ALU.mult, op1=ALU.add), [p, xi], [r1])
                    A, B, C = ln_stages(r1[:, :], r1, x1[:, m, :], x1, lnp[:, 0:D], lnp[:, D:2 * D])
                    A()
                    if prevC is not None:
                        prevC()
                    B()
                    if prevT is not None:
                        transposes(*prevT)
                        prevT = None
                    if prevC is not None:
                        pm = m - 1
                        xb = x1bf.next()
                        K.op(K.act, lambda: A_.copy(xb[:, :], x1[:, pm, :]), [x1], [xb])
                        prevT = (xb, pm)
                    prevC = C
                prevC()
                if prevT is not None:
                    transposes(*prevT)
                xb = x1bf.next()
                K.op(K.act, lambda: A_.copy(xb[:, :], x1[:, nt - 1, :]), [x1], [xb])
                transposes(xb, nt - 1)
                for f4 in range(6):
                    nf = 4 if f4 < 5 else 2
                    if f4 == 0:
                        d_issue(dcnt[0] + 1)
                    wg = w_get()
                    wu = w_get()
                    for f in range(nf):
                        fc = f4 * 4 + f
                        pg = nb()
                        K.mm(pg, pg[:, 0:W], [(wg[:, kc, f * 128:(f + 1) * 128], x1T[:, kc, 0:W]) for kc in range(8)], reads=[wg, x1T])
                        sg = sgr.next()
                        K.op(K.act, lambda pg=pg, sg=sg: A_.activation(sg[:, 0:W], pg[:, 0:W], AF.Silu), [pg], [sg])
                        pu = nb()
                        K.mm(pu, pu[:, 0:W], [(wu[:, kc, f * 128:(f + 1) * 128], x1T[:, kc, 0:W]) for kc in range(8)], reads=[wu, x1T])
                        K.op(K.dve, lambda pu=pu, sg=sg, fc=fc: V_.tensor_tensor(hT[:, fc, 0:W], pu[:, 0:W], sg[:, 0:W], ALU.mult), [pu, sg], [hT])
                for cg in range(4):
                    Wd = d_get()
                    for m in range(nt):
                        p = nb()
                        K.mm(p, p[:, 0:256], [(hT[:, fc, m * 128:(m + 1) * 128], Wd[:, fc, :]) for fc in range(NFC)], reads=[hT, Wd])
                        K.op(K.dve, lambda p=p, cg=cg, m=m: V_.scalar_tensor_tensor(out=x1[:, m, cg * 256:(cg + 1) * 256], in0=x1[:, m, cg * 256:(cg + 1) * 256], scalar=ALPHA,
                                                                                    in1=p[:, 0:256], op0=ALU.mult, op1=ALU.add), [p, x1], [x1])
                for m in range(nt):
                    ln2_pending.append((m, tok0 + m * 128))
            while ln2_pending:
                emit_ln2_one()
            K.finish()
    return nc


def _host_inputs(inp):
    f = np.float32
    w_in = np.ascontiguousarray(inp["w_in"][0], f)
    b_in = np.asarray(inp["b_in"][0], f)
    bias_bc = np.ascontiguousarray(np.broadcast_to(b_in[None, :], (128, DIN)))
    bcols = np.zeros((128, 48), f)
    for pr in range(4):
        bcols[:, pr] = b_in[C_FQ + pr * 128:C_FQ + (pr + 1) * 128]
        bcols[:, 4 + pr] = b_in[C_FK + pr * 128:C_FK + (pr + 1) * 128]
        bcols[0:64, 8 + pr] = b_in[C_GQ + pr * 64:C_GQ + (pr + 1) * 64]
        bcols[0:64, 12 + pr] = b_in[C_GK + pr * 64:C_GK + (pr + 1) * 64]
        bcols[:, 17 + pr] = b_in[C_GR + pr * 128:C_GR + (pr + 1) * 128]
        bcols[:, 37 + pr] = inp["gla_norm_g"][0][pr * 128:(pr + 1) * 128]
    bcols[0:16, 16] = b_in[C_GA:C_GA + 16]
    for e in range(8):
        bcols[:, 21 + e] = b_in[C_ZG + e * 128:C_ZG + (e + 1) * 128]
        bcols[:, 29 + e] = b_in[C_ZF + e * 128:C_ZF + (e + 1) * 128]
    cst = np.zeros((128, NCST), f)
    i = np.arange(128)
    cst[:, 0:128] = np.eye(128)
    cst[:, 128:256] = (i[:, None] <= i[None, :])
    cst[:, 256:384] = 1.0
    same = (i[:, None] // 64) == (i[None, :] // 64)
    cst[:, 384:512] = same & (i[:, None] <= i[None, :])
    cst[:, 512:640] = same & (i[:, None] > i[None, :])
    cst[:, 640:768] = np.where(i[:, None] > i[None, :], -30000.0, 0.0)
    cst[:, 768] = (i < 64)
    cst[:, 769] = (i >= 64)
    cst[:, 770] = 1e-5
    cst[:, 772:900] = (i[:, None] > i[None, :])
    bc = lambda v: np.ascontiguousarray(np.broadcast_to(np.asarray(v, f)[None, :], (128, len(v))))
    ln_bc = np.concatenate([bc(inp["ln1_g"][0]), bc(inp["ln1_b"][0]), bc(inp["ln2_g"][0]), bc(inp["ln2_b"][0])], axis=1)
    shared = dict(w_in=w_in, bias_bc=bias_bc, bcols=bcols, wa2=np.ascontiguousarray(inp["w_alpha2"][0], f),
                  ba2_bc=bc(inp["b_alpha2"][0]), wpg=np.ascontiguousarray(inp["w_proj_gla"][0], f),
                  wpf=np.ascontiguousarray(inp["w_proj_fox"][0], f), wout=np.ascontiguousarray(inp["w_out"][0], f),
                  wg=np.ascontiguousarray(inp["w_ffn_gate"][0], f), wu=np.ascontiguousarray(inp["w_ffn_up"][0], f),
                  wd=np.ascontiguousarray(inp["w_ffn_down"][0], f), ln_bc=np.ascontiguousarray(ln_bc), cst=cst)
    maps = []
    xp = np.asarray(inp["x_prompt"], f)
    xs = np.asarray(inp["x_sample"], f)
    for core in range(8):
        b, c = core // 4, core % 4
        chunks = own_chunks(c)
        x_own = np.zeros((NOWN, D), f)
        for s, ci in enumerate(chunks):
            x_own[s * 512:(s + 1) * 512] = xp[b, ci * 512:(ci + 1) * 512]
        valid = np.ones((NOWN,), f)
        for st in range(2):
            sidx = core * 2 + st
            x_own[4096 + st * 128:4096 + st * 128 + 32] = xs[sidx]
            valid[4096 + st * 128 + 32:4096 + (st + 1) * 128] = 0.0
        kflag = np.zeros((8, 128), f)
        oneh = np.zeros((8, 32), f)
        for s, ci in enumerate(chunks):
            kflag[s, 4 * ci:] = NEG
            oneh[s, ci] = 1.0
        m = dict(shared)
        m["xT_seq"] = np.ascontiguousarray(xp[b].T)
        m["xT_own"] = np.ascontiguousarray(x_own.T)
        m["x_own"] = x_own
        m["valid"] = np.ascontiguousarray(valid.reshape(NTO, 128).T)
        m["kflag"] = np.ascontiguousarray(np.broadcast_to(kflag.reshape(1, 1024), (128, 1024)))
        m["onehot"] = np.ascontiguousarray(np.broadcast_to(oneh.reshape(1, 256), (128, 256)))
        sl = slice(core * 2, core * 2 + 2)
        m["kTc"] = np.ascontiguousarray(np.asarray(inp["cache_fox_k"][0][sl], f).transpose(0, 2, 3, 1).reshape(2, 512, 2048))
        m["vc"] = np.ascontiguousarray(np.asarray(inp["cache_fox_v"][0][sl], f).reshape(2, 2048, 512))
        m["lfc"] = np.ascontiguousarray(np.asarray(inp["cache_fox_logf"][0][sl], f))
        m["s0"] = np.ascontiguousarray(np.asarray(inp["state_gla"][0][sl], f).transpose(0, 2, 1, 3).reshape(2, 64, 512))
        maps.append(m)
    return maps


_NC = None


def kernel(**inp):
    global _NC
    if _NC is None:
        _NC = build()
    maps = _host_inputs(inp)
    res = run_bass_kernel_spmd(_NC, maps, core_ids=list(range(8)))
    R = res.results
    f = np.float32
    yp = np.zeros((2, NSEQ, D), f)
    fkp = np.zeros((2, NSEQ, 512), f)
    fvp = np.zeros((2, NSEQ, 512), f)
    lfp = np.zeros((2, NSEQ, 8), f)
    ys = np.zeros((16, 32, D), f)
    fks = np.zeros((16, 32, 512), f)
    fvs = np.zeros((16, 32, 512), f)
    lfs = np.zeros((16, 32, 8), f)
    gss = np.zeros((16, 4, 64, 128), f)
    gsp = np.zeros((2, 4, 64, 128), f)
    for core in range(8):
        b, c = core // 4, core % 4
        r = R[core]
        for s, ci in enumerate(own_chunks(c)):
            yp[b, ci * 512:(ci + 1) * 512] = r["y_own"][s * 512:(s + 1) * 512]
            fkp[b, ci * 512:(ci + 1) * 512] = r["fk_own"][s * 512:(s + 1) * 512]
            fvp[b, ci * 512:(ci + 1) * 512] = r["fv_own"][s * 512:(s + 1) * 512]
            lfp[b, ci * 512:(ci + 1) * 512] = r["lf_own"][s * 512:(s + 1) * 512]
        if c == 0:
            gsp[b] = r["gstate_p"].reshape(64, 4, 128).transpose(1, 0, 2)
        for st in range(2):
            sidx = core * 2 + st
            o = 4096 + st * 128
            ys[sidx] = r["y_own"][o:o + 32]
            fks[sidx] = r["fk_own"][o:o + 32]
            fvs[sidx] = r["fv_own"][o:o + 32]
            lfs[sidx] = r["lf_own"][o:o + 32]
            gss[sidx] = r["gstate_s"][st].reshape(64, 4, 128).transpose(1, 0, 2)
    return (yp, ys, gsp[None], fkp.reshape(1, 2, NSEQ, 8, 64), fvp.reshape(1, 2, NSEQ, 8, 64), lfp[None],
            gss[None], fks.reshape(1, 16, 32, 8, 64), fvs.reshape(1, 16, 32, 8, 64), lfs[None])
```
